# Optimizing a Trainium2 kernel written in Bass

```python
import math
import jax
import jax.numpy as jnp
from jax import lax
import numpy as np

D_MODEL = 1024
BATCH = 8
SEQ = 4096
DEPTH = 4

CTX_LEN = 256
GRID_W = 64
N_MIXERS = 3
N_MOD = 9
D_FF = 2816
NORM_EPS = 1e-6

S5_GROUP = 16
S5_GROUPS = D_MODEL // S5_GROUP
S5_STATE = 64
S5_BLOCK = 128
S5_DT_MIN = 1e-3
S5_DT_MAX = 1e-1

GLA_HEADS = 4
GLA_DK = D_MODEL // 2 // GLA_HEADS
GLA_DV = D_MODEL // GLA_HEADS
GLA_RANK = 16
GLA_TAU = 16.0

HGRN_DK = 128
HGRN_HEADS = D_MODEL // HGRN_DK
HGRN_DV = D_MODEL // HGRN_HEADS

LIN_CHUNK = 64

N_S5 = len(range(0, DEPTH, N_MIXERS))
N_GLA = len(range(1, DEPTH, N_MIXERS))
N_HGRN = len(range(2, DEPTH, N_MIXERS))

F32 = jnp.float32

kernel_name = 'hybrid_s5_gla_hgrn2_prefix_trunk'


def rmsnorm(x, w):
    xf = x.astype(F32)
    y = xf * lax.rsqrt(jnp.mean(xf * xf, axis=-1, keepdims=True) + NORM_EPS)
    return (y * w.astype(F32)).astype(x.dtype)


def swiglu_ffn(h, w_in, w_out):
    gu = jnp.einsum('bld,df->blf', h, w_in)
    g, u = jnp.split(gu, 2, axis=-1)
    return jnp.einsum('blf,fd->bld', jax.nn.silu(g) * u, w_out)


def ffn_half_step(x, w_norm, shift, scale, gate, w_in, w_out):
    h = rmsnorm(x, w_norm) * (1.0 + scale) + shift
    return x + 0.5 * gate * swiglu_ffn(h, w_in, w_out)


def grid_to_col_major(t, rows):
    b, l, e = t.shape
    return t.reshape(b, rows, GRID_W, e).transpose(0, 2, 1, 3).reshape(b, l, e)


def col_major_to_grid(t, rows):
    b, l, e = t.shape
    return t.reshape(b, GRID_W, rows, e).transpose(0, 2, 1, 3).reshape(b, l, e)


def s5_discretize(a_re, a_im, log_step, b_re, b_im):
    dt = jnp.exp(log_step.astype(F32))[..., None]
    a_re = a_re.astype(F32)
    a_im = a_im.astype(F32)
    mag = jnp.exp(a_re * dt)
    abar_re = mag * jnp.cos(a_im * dt)
    abar_im = mag * jnp.sin(a_im * dt)
    den = a_re * a_re + a_im * a_im
    zr = abar_re - 1.0
    zi = abar_im
    coef_re = ((zr * a_re + zi * a_im) / den)[..., None]
    coef_im = ((zi * a_re - zr * a_im) / den)[..., None]
    b_re = b_re.astype(F32)
    b_im = b_im.astype(F32)
    bbar_re = coef_re * b_re - coef_im * b_im
    bbar_im = coef_re * b_im + coef_im * b_re
    return abar_re, abar_im, bbar_re, bbar_im


def s5_combine(e1, e2):
    ar1, ai1, br1, bi1 = e1
    ar2, ai2, br2, bi2 = e2
    return (ar1 * ar2 - ai1 * ai2, ar1 * ai2 + ai1 * ar2,
            ar2 * br1 - ai2 * bi1 + br2, ar2 * bi1 + ai2 * br1 + bi2)


def s5_direction(u, abar_re, abar_im, bbar_re, bbar_im, c_re, c_im, h0_re, h0_im):
    bsz, length = u.shape[0], u.shape[1]
    n_blk = length // S5_BLOCK
    ub = jnp.swapaxes(u.reshape(bsz, n_blk, S5_BLOCK, S5_GROUPS, S5_GROUP), 0, 1)
    a_re = jnp.broadcast_to(abar_re, (bsz, S5_BLOCK, S5_GROUPS, S5_STATE))
    a_im = jnp.broadcast_to(abar_im, (bsz, S5_BLOCK, S5_GROUPS, S5_STATE))

    def step(carry, ublk):
        h_re, h_im = carry
        bu_re = jnp.einsum('blgc,gpc->blgp', ublk, bbar_re)
        bu_im = jnp.einsum('blgc,gpc->blgp', ublk, bbar_im)
        ca_re, ca_im, cb_re, cb_im = lax.associative_scan(s5_combine, (a_re, a_im, bu_re, bu_im), axis=1)
        hs_re = ca_re * h_re[:, None] - ca_im * h_im[:, None] + cb_re
        hs_im = ca_re * h_im[:, None] + ca_im * h_re[:, None] + cb_im
        y = jnp.einsum('blgp,gcp->blgc', hs_re, c_re) - jnp.einsum('blgp,gcp->blgc', hs_im, c_im)
        return (hs_re[:, -1], hs_im[:, -1]), y

    (h_re, h_im), y = lax.scan(step, (h0_re, h0_im), ub)
    y = jnp.swapaxes(y, 0, 1).reshape(bsz, length, S5_GROUPS, S5_GROUP)
    return y, h_re, h_im


def s5_mixer(hc, hx, a_re, a_im, log_step, b_re, b_im, c_re, c_im, d_skip, w_glu, b_glu, need_ctx):
    abar_re, abar_im, bbar_re, bbar_im = s5_discretize(a_re, a_im, log_step, b_re, b_im)
    c_re = c_re.astype(F32)
    c_im = c_im.astype(F32)

    def groups(h):
        return h.astype(F32).reshape(h.shape[0], h.shape[1], S5_GROUPS, S5_GROUP)

    uc, ux = groups(hc), groups(hx)
    h0 = jnp.zeros((hc.shape[0], S5_GROUPS, S5_STATE), F32)
    fwd = (abar_re[0], abar_im[0], bbar_re[0], bbar_im[0], c_re[0], c_im[0])
    bwd = (abar_re[1], abar_im[1], bbar_re[1], bbar_im[1], c_re[1], c_im[1])
    yc_f, hf_re, hf_im = s5_direction(uc, *fwd, h0, h0)
    yc_b, hb_re, hb_im = s5_direction(jnp.flip(uc, 1), *bwd, h0, h0)
    yx_f, _, _ = s5_direction(ux, *fwd, hf_re, hf_im)
    yx_b, _, _ = s5_direction(jnp.flip(ux, 1), *bwd, hb_re, hb_im)

    def readout(h, y_f, y_b_rev):
        bsz, length, _ = h.shape
        y = (y_f + jnp.flip(y_b_rev, 1)).reshape(bsz, length, D_MODEL) + d_skip.astype(F32) * h.astype(F32)
        z = jnp.einsum('bld,de->ble', jax.nn.gelu(y), w_glu.astype(F32)) + b_glu.astype(F32)
        val, gate = jnp.split(z, 2, axis=-1)
        return (val * jax.nn.sigmoid(gate)).astype(h.dtype)

    yx = readout(hx, yx_f, yx_b)
    yc = readout(hc, yc_f, yc_b) if need_ctx else None
    return yc, yx


def chunk_gated_scan(q, k, v, log_g, s0):
    bsz, length, n_heads, _ = q.shape
    dv = v.shape[-1]
    n_chunks = length // LIN_CHUNK

    def blocks(t):
        return jnp.moveaxis(t.reshape(bsz, n_chunks, LIN_CHUNK, *t.shape[2:]), 1, 0)

    lower = jnp.tril(jnp.ones((LIN_CHUNK, LIN_CHUNK), dtype=bool))[None, :, :, None, None]

    def step(state, blk):
        qb, kb, vb, gb = blk
        cum = jnp.cumsum(gb, axis=1)
        rel = jnp.exp(jnp.where(lower, cum[:, :, None] - cum[:, None, :], -jnp.inf))
        scores = jnp.einsum('bihk,bjhk,bijhk->bhij', qb, kb, rel)
        out = (jnp.einsum('bhij,bjhv->bihv', scores, vb)
               + jnp.einsum('bihk,bhkv->bihv', qb * jnp.exp(cum), state))
        last = cum[:, -1]
        k_dec = kb * jnp.exp(last[:, None] - cum)
        state = jnp.exp(last)[..., None] * state + jnp.einsum('bjhk,bjhv->bhkv', k_dec, vb)
        return state, out

    s_final, out = lax.scan(step, s0, (blocks(q), blocks(k), blocks(v), blocks(log_g)))
    out = jnp.moveaxis(out, 0, 1).reshape(bsz, length, n_heads, dv)
    return out, s_final


def bidir_gated(ctx_in, lat_in, need_ctx):
    qc, kfc, kbc, vc, gfc, gbc = ctx_in
    qx, kfx, kbx, vx, gfx, gbx = lat_in

    def flip(t):
        return jnp.flip(t, axis=1)

    s0 = jnp.zeros((qc.shape[0], qc.shape[2], qc.shape[3], vc.shape[3]), F32)
    oc_f, s_f = chunk_gated_scan(qc, kfc, vc, gfc, s0)
    oc_b, s_b = chunk_gated_scan(flip(qc), flip(kbc), flip(vc), flip(gbc), s0)
    ox_f, _ = chunk_gated_scan(qx, kfx, vx, gfx, s_f)
    ox_b, _ = chunk_gated_scan(flip(qx), flip(kbx), flip(vx), flip(gbx), s_b)
    ox = ox_f + flip(ox_b)
    oc = oc_f + flip(oc_b) if need_ctx else None
    return oc, ox


def gated_head_out(o, gate, norm_w, w_out, dtype):
    bsz, length = o.shape[0], o.shape[1]
    o = rmsnorm(o, norm_w).reshape(bsz, length, -1) * jax.nn.silu(gate)
    return jnp.einsum('ble,ed->bld', o, w_out.astype(F32)).astype(dtype)


def gla_mixer(hc, hx, rows, w_in, w_gate, b_gate, norm_w, w_out, need_ctx):
    dkt, dvt = GLA_HEADS * GLA_DK, GLA_HEADS * GLA_DV

    def project(h):
        bsz, length, _ = h.shape
        p = jnp.einsum('bld,de->ble', h, w_in).astype(F32)
        q, k, v, r, low = jnp.split(p, [dkt, 2 * dkt, 2 * dkt + dvt, 2 * dkt + 2 * dvt], axis=-1)
        low = low.reshape(bsz, length, 2, GLA_RANK)
        log_a = jax.nn.log_sigmoid(jnp.einsum('blzr,zrk->blzk', low, w_gate.astype(F32)) + b_gate.astype(F32)) / GLA_TAU
        q = q.reshape(bsz, length, GLA_HEADS, GLA_DK) * (GLA_DK ** -0.5)
        k = k.reshape(bsz, length, GLA_HEADS, GLA_DK)
        v = v.reshape(bsz, length, GLA_HEADS, GLA_DV)
        g_f = log_a[:, :, 0].reshape(bsz, length, GLA_HEADS, GLA_DK)
        g_b = log_a[:, :, 1].reshape(bsz, length, GLA_HEADS, GLA_DK)
        return (q, k, k, v, g_f, g_b), r

    ctx_in, r_c = project(hc)
    lat_in, r_x = project(grid_to_col_major(hx, rows))
    oc, ox = bidir_gated(ctx_in, lat_in, need_ctx)
    yx = col_major_to_grid(gated_head_out(ox, r_x, norm_w, w_out, hx.dtype), rows)
    yc = gated_head_out(oc, r_c, norm_w, w_out, hc.dtype) if need_ctx else None
    return yc, yx


def hgrn_mixer(hc, hx, w_in, lower_bound, norm_w, w_out, need_ctx):
    lb = lower_bound.astype(F32).reshape(2, HGRN_HEADS, HGRN_DK)

    def project(h):
        bsz, length, _ = h.shape
        p = jnp.einsum('bld,de->ble', h, w_in).astype(F32)
        q, i, z_f, z_b, g = jnp.split(p, 5, axis=-1)

        def heads(t):
            return t.reshape(bsz, length, HGRN_HEADS, HGRN_DK)

        z_f, z_b = heads(z_f), heads(z_b)
        log_f_f = jnp.log(lb[0] + (1.0 - lb[0]) * jax.nn.sigmoid(z_f))
        log_f_b = jnp.log(lb[1] + (1.0 - lb[1]) * jax.nn.sigmoid(z_b))
        k_f = (1.0 - lb[0]) * jax.nn.sigmoid(-z_f)
        k_b = (1.0 - lb[1]) * jax.nn.sigmoid(-z_b)
        v = i.reshape(bsz, length, HGRN_HEADS, HGRN_DV)
        return (heads(jax.nn.silu(q)), k_f, k_b, v, log_f_f, log_f_b), g

    ctx_in, g_c = project(hc)
    lat_in, g_x = project(hx)
    oc, ox = bidir_gated(ctx_in, lat_in, need_ctx)
    yx = gated_head_out(ox, g_x, norm_w, w_out, hx.dtype)
    yc = gated_head_out(oc, g_c, norm_w, w_out, hc.dtype) if need_ctx else None
    return yc, yx


def setup_inputs(seed: int = 0) -> dict:
    key = jax.random.key(seed)
    ks = iter(jax.random.split(key, 32))

    def normal(shape, scale):
        return jax.random.normal(next(ks), shape, jnp.float32) * scale

    d = D_MODEL
    dkt, dvt = GLA_HEADS * GLA_DK, GLA_HEADS * GLA_DV
    n_idx = jnp.arange(S5_STATE, dtype=jnp.float32)
    s5_a = (N_S5, 2, S5_GROUPS, S5_STATE)
    s5_b = (N_S5, 2, S5_GROUPS, S5_STATE, S5_GROUP)
    s5_c = (N_S5, 2, S5_GROUPS, S5_GROUP, S5_STATE)
    return {
        'x': normal((BATCH, SEQ, d), 1.0),
        'c': normal((BATCH, d), 1.0),
        'ctx': normal((BATCH, CTX_LEN, d), 1.0),
        'c_ctx': normal((d,), 1.0),
        'ada_w': normal((DEPTH, d, N_MOD * d), 0.5 * d ** -0.5),
        'ada_b': normal((DEPTH, N_MOD * d), 0.02),
        'norm_w': 1.0 + normal((DEPTH, 3, d), 0.02),
        'ffn_w_in': normal((DEPTH, 2, d, 2 * D_FF), d ** -0.5),
        'ffn_w_out': normal((DEPTH, 2, D_FF, d), D_FF ** -0.5),
        's5_a_re': -0.5 + normal(s5_a, 0.01),
        's5_a_im': math.pi * n_idx + normal(s5_a, 0.01),
        's5_log_step': jax.random.uniform(next(ks), (N_S5, 2, S5_GROUPS), jnp.float32,
                                          math.log(S5_DT_MIN), math.log(S5_DT_MAX)),
        's5_b_re': normal(s5_b, (2 * S5_GROUP) ** -0.5),
        's5_b_im': normal(s5_b, (2 * S5_GROUP) ** -0.5),
        's5_c_re': normal(s5_c, S5_STATE ** -0.5),
        's5_c_im': normal(s5_c, S5_STATE ** -0.5),
        's5_d': normal((N_S5, d), 0.5),
        's5_w_glu': normal((N_S5, d, 2 * d), d ** -0.5),
        's5_b_glu': normal((N_S5, 2 * d), 0.02),
        'gla_w_in': normal((N_GLA, d, 2 * dkt + 2 * dvt + 2 * GLA_RANK), d ** -0.5),
        'gla_w_gate': normal((N_GLA, 2, GLA_RANK, dkt), GLA_RANK ** -0.5),
        'gla_b_gate': normal((N_GLA, 2, dkt), 0.02),
        'gla_norm_w': 1.0 + normal((N_GLA, GLA_DV), 0.02),
        'gla_w_out': normal((N_GLA, dvt, d), dvt ** -0.5),
        'hgrn_w_in': normal((N_HGRN, d, 5 * d), d ** -0.5),
        'hgrn_lb_logits': normal((DEPTH, 2, d), 0.1),
        'hgrn_norm_w': 1.0 + normal((N_HGRN, HGRN_DV), 0.02),
        'hgrn_w_out': normal((N_HGRN, d, d), d ** -0.5),
        'final_norm_w': 1.0 + normal((d,), 0.02),
    }


def reference(x, c, ctx, c_ctx, ada_w, ada_b, norm_w, ffn_w_in, ffn_w_out,
              s5_a_re, s5_a_im, s5_log_step, s5_b_re, s5_b_im, s5_c_re, s5_c_im, s5_d,
              s5_w_glu, s5_b_glu, gla_w_in, gla_w_gate, gla_b_gate, gla_norm_w, gla_w_out,
              hgrn_w_in, hgrn_lb_logits, hgrn_norm_w, hgrn_w_out, final_norm_w):
    rows = x.shape[1] // GRID_W
    lb_soft = jax.nn.softmax(hgrn_lb_logits.astype(F32), axis=0)
    lb_all = jnp.cumsum(lb_soft, axis=0) - lb_soft[0]
    xc = ctx
    for i in range(DEPTH):
        last = i == DEPTH - 1
        mod_x = (jnp.einsum('bd,de->be', jax.nn.silu(c), ada_w[i]) + ada_b[i])[:, None, :]
        mod_c = (jnp.einsum('d,de->e', jax.nn.silu(c_ctx), ada_w[i]) + ada_b[i])[None, None, :]
        sx = jnp.split(mod_x, N_MOD, axis=-1)
        sc = jnp.split(mod_c, N_MOD, axis=-1)

        x = ffn_half_step(x, norm_w[i, 0], sx[0], sx[1], sx[2], ffn_w_in[i, 0], ffn_w_out[i, 0])
        xc = ffn_half_step(xc, norm_w[i, 0], sc[0], sc[1], sc[2], ffn_w_in[i, 0], ffn_w_out[i, 0])

        hx = rmsnorm(x, norm_w[i, 1]) * (1.0 + sx[4]) + sx[3]
        hc = rmsnorm(xc, norm_w[i, 1]) * (1.0 + sc[4]) + sc[3]
        kind, j = i % N_MIXERS, i // N_MIXERS
        if kind == 0:
            yc, yx = s5_mixer(hc, hx, s5_a_re[j], s5_a_im[j], s5_log_step[j], s5_b_re[j], s5_b_im[j],
                              s5_c_re[j], s5_c_im[j], s5_d[j], s5_w_glu[j], s5_b_glu[j], not last)
        elif kind == 1:
            yc, yx = gla_mixer(hc, hx, rows, gla_w_in[j], gla_w_gate[j], gla_b_gate[j],
                               gla_norm_w[j], gla_w_out[j], not last)
        else:
            yc, yx = hgrn_mixer(hc, hx, hgrn_w_in[j], lb_all[i], hgrn_norm_w[j], hgrn_w_out[j], not last)
        x = x + sx[5] * yx

        x = ffn_half_step(x, norm_w[i, 2], sx[6], sx[7], sx[8], ffn_w_in[i, 1], ffn_w_out[i, 1])
        if not last:
            xc = xc + sc[5] * yc
            xc = ffn_half_step(xc, norm_w[i, 2], sc[6], sc[7], sc[8], ffn_w_in[i, 1], ffn_w_out[i, 1])
    return rmsnorm(x, final_norm_w)
```

```python
import math
from contextlib import ExitStack

import numpy as np
import concourse.bass as bass
import concourse.mybir as mybir
from concourse.bass_utils import run_bass_kernel_spmd

F32 = mybir.dt.float32
BF16 = mybir.dt.bfloat16
ALU = mybir.AluOpType
AF = mybir.ActivationFunctionType

ENG = ['tensor', 'vector', 'scalar', 'gpsimd', 'sync']
NDMA = 24
SAME_ENG_WINDOW = 10 ** 9

D = 1024
L = 4096
LC = 256
LT = L + LC
DFF = 2816
NKC = 8
NFC = 22
TT = 256
NTT = LT // TT
EPS = 1e-6
PI = math.pi
TWO_PI = 2.0 * math.pi


class Buf:
    def __init__(self, ap, name=''):
        self.ap = ap
        self.name = name
        self.w = None
        self.r = {}


class Prog:
    def __init__(self, nc, stack, arena_cols_f32=47 * 1024):
        self.nc = nc
        self.stack = stack
        self.q = {e: [] for e in ENG}
        self.cnt = {e: 0 for e in ENG}
        self.epoch = 0
        self.sems = {}
        self.seen = {e: {} for e in ENG}
        self.dma_sems = [stack.enter_context(nc.semaphore(f"dma{j}")) for j in range(NDMA)]
        self.dma_cnt = [0] * NDMA
        self.dma_pool = {'sync': list(range(0, NDMA // 2)), 'gpsimd': list(range(NDMA // 2, NDMA))}
        self.dma_rr = {'sync': 0, 'gpsimd': 0}
        self._new_epoch_sems()
        self.arena = stack.enter_context(nc.sbuf_tensor("arena", [128, arena_cols_f32], F32))
        self.arena_cols = arena_cols_f32
        self.bump = 0
        self.psum = [Buf(stack.enter_context(nc.psum_tensor(f"ps{i}", [128, 512], F32))[:, :], f"ps{i}")
                     for i in range(8)]
        self.ps_rr = 0
        self.n_instr = 0

    def alloc(self, cols, dtype=F32, name='', parts=128):
        nbytes = cols * (4 if dtype == F32 else 2)
        n32 = (nbytes + 3) // 4
        n32 = (n32 + 7) // 8 * 8
        assert self.bump + n32 <= self.arena_cols, f"SBUF arena overflow at {name}: {self.bump}+{n32}"
        v = self.arena[:, self.bump:self.bump + n32]
        self.bump += n32
        if dtype != F32:
            v = v.bitcast(dtype)
        v = v[0:parts, 0:cols]
        return Buf(v, name)

    def mark(self):
        return self.bump

    def release(self, mark):
        self.bump = mark

    def next_psum(self):
        b = self.psum[self.ps_rr]
        self.ps_rr = (self.ps_rr + 1) % 8
        return b

    def _new_epoch_sems(self):
        if not hasattr(self, 'semset'):
            self.semset = [{e: self.stack.enter_context(self.nc.semaphore(f"s_{e}_{j}")) for e in ENG}
                           for j in range(3)]
        for e in ENG:
            self.sems[(e, self.epoch)] = self.semset[self.epoch % 3][e]
        if self.epoch >= 2:
            for e in ENG:
                self.q[e].append(('clear', self.semset[(self.epoch + 1) % 3][e]))

    def _need(self, eng, ev, waits, raw):
        if ev is None:
            return
        if ev[0] == 'e':
            _, f, ep, k = ev
            if ep != self.epoch:
                return
            if f == eng:
                if eng == 'tensor' or (not raw and eng != 'gpsimd'):
                    return
                if k <= self.cnt[eng] - SAME_ENG_WINDOW:
                    return
            key = ('e', f, ep)
        else:
            _, j, k = ev
            key = ('d', j)
        if self.seen[eng].get(key, 0) >= k:
            return
        waits[key] = max(waits.get(key, 0), k)

    def _emit_waits(self, eng, waits):
        for key, k in waits.items():
            self.seen[eng][key] = k
            sem = self.sems[(key[1], key[2])] if key[0] == 'e' else self.dma_sems[key[1]]
            self.q[eng].append(('wait', sem, k))

    def _deps(self, eng, reads, writes):
        waits = {}
        for b in reads:
            self._need(eng, b.w, waits, True)
        for b in writes:
            self._need(eng, b.w, waits, False)
            for ev in b.r.values():
                self._need(eng, ev, waits, False)
        self._emit_waits(eng, waits)

    def _commit(self, ev, rkey, reads, writes):
        for b in writes:
            b.w = ev
            b.r = {}
        for b in reads:
            if b in writes:
                continue
            b.r[rkey] = ev

    def op(self, eng, fn, reads=(), writes=()):
        self._deps(eng, reads, writes)
        self.cnt[eng] += 1
        self.q[eng].append(('op', fn, self.sems[(eng, self.epoch)]))
        ev = ('e', eng, self.epoch, self.cnt[eng])
        self._commit(ev, ('e', eng), reads, writes)
        self.n_instr += 1
        return ev

    def dma(self, eng, out_ap, in_ap, reads=(), writes=(), **kw):
        self._deps(eng, reads, writes)
        pool = self.dma_pool[eng]
        j = pool[self.dma_rr[eng]]
        self.dma_rr[eng] = (self.dma_rr[eng] + 1) % len(pool)
        w = {}
        if self.dma_cnt[j] > 0:
            self._need(eng, ('d', j, self.dma_cnt[j]), w, True)
            self._emit_waits(eng, w)
        self.dma_cnt[j] += 16
        self.q[eng].append(('dma', out_ap, in_ap, kw, self.dma_sems[j]))
        ev = ('d', j, self.dma_cnt[j])
        self._commit(ev, ('d', j), reads, writes)
        self.n_instr += 1
        return ev

    def barrier(self):
        for e in ENG:
            waits = {}
            for f in ENG:
                if f != e and self.cnt[f] > 0:
                    ev = ('e', f, self.epoch, self.cnt[f])
                    self._need(e, ev, waits, True)
            for j in range(NDMA):
                if self.dma_cnt[j] > 0:
                    self._need(e, ('d', j, self.dma_cnt[j]), waits, True)
            self._emit_waits(e, waits)
        self.epoch += 1
        self._new_epoch_sems()
        for e in ENG:
            self.cnt[e] = 0

    def final_wait(self, eng='sync'):
        waits = {}
        for f in ENG:
            if f != eng and self.cnt[f] > 0:
                self._need(eng, ('e', f, self.epoch, self.cnt[f]), waits, True)
        for j in range(NDMA):
            if self.dma_cnt[j] > 0:
                self._need(eng, ('d', j, self.dma_cnt[j]), waits, True)
        self._emit_waits(eng, waits)

    def emit(self):
        nc = self.nc
        with nc.Block() as block:
            def run(engname):
                def body(e):
                    for item in self.q[engname]:
                        if item[0] == 'wait':
                            e.wait_ge(item[1], item[2])
                        elif item[0] == 'clear':
                            e.sem_clear(item[1])
                        elif item[0] == 'op':
                            item[1](e).then_inc(item[2], 1)
                        else:
                            _, o, i, kw, sem = item
                            e.dma_start(out=o, in_=i, **kw).then_inc(sem, 16)
                return body
            block.tensor(run('tensor'))
            block.vector(run('vector'))
            block.scalar(run('scalar'))
            block.gpsimd(run('gpsimd'))
            block.sync(run('sync'))


def sub(buf, ap, name=''):
    return Buf(ap, name or buf.name)


INPUT_SHAPES = {
    'x': [L, D], 'c': [1, D], 'ctx': [LC, D], 'c_ctx': [1, D],
    'ada_w': [4, D, 9 * D], 'ada_b': [4, 9 * D], 'norm_w': [4, 3, D],
    'ffn_w_in': [4, 2, D, 2 * DFF], 'ffn_w_out': [4, 2, DFF, D],
    's5_a_re': [2, 2, 64, 64], 's5_a_im': [2, 2, 64, 64], 's5_log_step': [2, 2, 64],
    's5_b_re': [2, 2, 64, 64, 16], 's5_b_im': [2, 2, 64, 64, 16],
    's5_c_re': [2, 2, 64, 16, 64], 's5_c_im': [2, 2, 64, 16, 64],
    's5_d': [2, D], 's5_w_glu': [2, D, 2 * D], 's5_b_glu': [2, 2 * D],
    'gla_w_in': [1, D, 3104], 'gla_w_gate': [1, 2, 16, 512], 'gla_b_gate': [1, 2, 512],
    'gla_norm_w': [1, 256], 'gla_w_out': [1, D, D],
    'hgrn_w_in': [1, D, 5 * D], 'hgrn_lb_logits': [4, 2, D], 'hgrn_norm_w': [1, 128],
    'hgrn_w_out': [1, D, D], 'final_norm_w': [1, D],
    'k_ident': [128, 128], 'k_iota': [128, LT], 'k_mask': [2, 64, 64],
}


class K:
    pass


def mod_col(i, m, c, s):
    return ((i * 9 + m) * 8 + c) * 2 + s


def build_nc(n_layers=4, mixers=True, dump_xT=False, layer_list=None):
    nc = bass.Bass("TRN2", target_bir_lowering=False)
    I = {n: nc.dram_tensor(n, list(s), F32, kind="ExternalInput").ap() for n, s in INPUT_SHAPES.items()}
    out = nc.dram_tensor("out", [L, D], F32, kind="ExternalOutput").ap()
    xT = nc.dram_tensor("xT_scr", [8, 128, LT], F32, kind="Internal").ap()
    hT_d = nc.dram_tensor("hT_scr", [8, 128, LT], BF16, kind="Internal").ap()
    yg_d = nc.dram_tensor("yg_scr", [8, 128, LT], BF16, kind="Internal").ap()
    pq_d = nc.dram_tensor("pq_scr", [8, 128, LT], BF16, kind="Internal").ap()
    pk_d = nc.dram_tensor("pk_scr", [2, 8, 128, LT], BF16, kind="Internal").ap()
    pg_d = nc.dram_tensor("pg_scr", [2, 8, 128, LT], F32, kind="Internal").ap()
    pgate_d = nc.dram_tensor("pgate_scr", [8, 128, LT], BF16, kind="Internal").ap()
    pv_d = nc.dram_tensor("pv_scr", [LT // 64, 64, 1024], BF16, kind="Internal").ap()
    if dump_xT:
        xdump = nc.dram_tensor("xdump", [8, 128, LT], F32, kind="ExternalOutput").ap()
        dbg = nc.dram_tensor("dbg", [128, 1024], F32, kind="ExternalOutput").ap()
        dbg2 = nc.dram_tensor("dbg2", [32, 128, 512], F32, kind="ExternalOutput").ap()

    with ExitStack() as st:
        P = Prog(nc, st)
        k = K()
        k.P, k.I, k.xT, k.hT_d, k.yg_d = P, I, xT, hT_d, yg_d
        k.dbg2 = dbg2 if dump_xT else None
        k.pq_d, k.pk_d, k.pg_d, k.pgate_d, k.pv_d = pq_d, pk_d, pg_d, pgate_d, pv_d
        k.pqb = [Buf(None) for _ in range(8)]
        k.pkb = [[Buf(None) for _ in range(8)] for _ in range(2)]
        k.pgb = [[Buf(None) for _ in range(8)] for _ in range(2)]
        k.pgateb = [Buf(None) for _ in range(8)]
        k.pvb = Buf(None)
        k.ogb = [Buf(None) for _ in range(8)]
        k.mask = P.alloc(128, F32, 'mask', parts=64)
        P.dma('sync', k.mask.ap.rearrange("p (d i) -> p d i", d=2), I['k_mask'].rearrange("d j i -> j d i"),
              writes=[k.mask])
        k.onec = P.alloc(1, F32, 'onec')
        P.op('gpsimd', lambda e: e.memset(k.onec.ap, 1.0), writes=[k.onec])
        k.dbg_n = 0
        k.xT_p = xT.rearrange("c p t -> p c t")
        k.hT_p = hT_d.rearrange("c p t -> p c t")
        k.yg_p = yg_d.rearrange("c p t -> p c t")
        k.xTb = [Buf(None, f"xT{t}") for t in range(NTT)]
        k.hTb = [Buf(None, f"hT{t}") for t in range(NTT)]
        k.ygb = [Buf(None, f"yg{t}") for t in range(NTT)]

        k.ident_f = P.alloc(128, F32, 'ident_f')
        k.ident_b = P.alloc(128, BF16, 'ident_b')
        k.ones_b = P.alloc(128, BF16, 'ones_b')
        k.modT = P.alloc(4 * 9 * 8 * 2, F32, 'modT')
        k.WS = P.alloc(4 * 3 * 2 * 8, F32, 'WS')
        k.HG = P.alloc(4 * 3 * 2 * 8, F32, 'HG')
        k.normT = P.alloc(4 * 3 * 8, F32, 'normT')
        k.fnT = P.alloc(8, F32, 'fnT')
        P.dma('sync', k.ident_f.ap, I['k_ident'], writes=[k.ident_f])
        P.op('vector', lambda e: e.tensor_copy(k.ident_b.ap, k.ident_f.ap), reads=[k.ident_f], writes=[k.ident_b])
        P.op('gpsimd', lambda e: e.memset(k.ones_b.ap, 1.0), writes=[k.ones_b])

        prologue(k)
        P.barrier()
        for i in (layer_list if layer_list is not None else range(n_layers)):
            ffn_phase(k, i, 0)
            P.barrier()
            if mixers:
                kind = i % 3
                mixer_prep(k, i, permute=(kind == 1))
                P.barrier()
                if kind == 0:
                    s5_phase(k, i)
                elif kind == 1:
                    gated_phase(k, i, 'gla')
                else:
                    gated_phase(k, i, 'hgrn')
                P.barrier()
            ffn_phase(k, i, 1)
            P.barrier()
        epilogue(k, out)
        if dump_xT:
            P.barrier()
            mk = P.mark()
            t = P.alloc(8 * 512, F32, 'dump')
            for n in range(0, LT, 512):
                w = min(512, LT - n)
                P.dma('sync', t.ap.rearrange("p (c t) -> p c t", c=8)[:, :, 0:w], k.xT_p[:, :, n:n + w], writes=[t])
                P.dma('sync', xdump.rearrange("c p t -> p c t")[:, :, n:n + w],
                      t.ap.rearrange("p (c t) -> p c t", c=8)[:, :, 0:w], reads=[t])
            P.release(mk)
            P.dma('sync', dbg[:, 0:576], k.modT.ap, reads=[k.modT])
            P.dma('sync', dbg[:, 576:768], k.WS.ap, reads=[k.WS])
            P.dma('sync', dbg[:, 768:960], k.HG.ap, reads=[k.HG])
        P.final_wait('sync')
        P.emit()
        k.n_instr = P.n_instr
    return nc


def dbg_dump(k, buf, ap=None, label=''):
    if k.dbg2 is None or k.dbg_n >= 32:
        return
    ap = buf.ap if ap is None else ap
    pp, cc = ap.shape[0], ap.shape[1]
    print('DBG slot', k.dbg_n, label, pp, cc)
    k.P.dma('gpsimd', k.dbg2[k.dbg_n, 0:pp, 0:cc], ap, reads=[buf])
    k.dbg_n += 1


def slow_dma(P, out_ap, in_ap, **kw):
    return P.dma('sync', out_ap, in_ap, allow_slow_non_contiguous=True, **kw)


def prologue(k):
    P, I = k.P, k.I
    mk = P.mark()
    adabT = P.alloc(4 * 72, F32, 'adabT')
    for i in range(4):
        slow_dma(P, adabT.ap[:, i * 72:(i + 1) * 72], I['ada_b'][i].rearrange("(m p) -> p m", p=128), writes=[adabT])
    slow_dma(P, k.normT.ap, I['norm_w'].rearrange("i j (c p) -> p (i j c)", p=128), writes=[k.normT])
    slow_dma(P, k.fnT.ap, I['final_norm_w'].rearrange("o (c p) -> p (o c)", p=128), writes=[k.fnT])
    cs32 = P.alloc(16, F32, 'cs32')
    csv = cs32.ap.rearrange("p (k s) -> p k s", s=2)
    slow_dma(P, csv[:, :, 0], I['c'].rearrange("o (k p) -> p (o k)", p=128), writes=[cs32])
    slow_dma(P, csv[:, :, 1], I['c_ctx'].rearrange("o (k p) -> p (o k)", p=128), writes=[cs32])
    csb = P.alloc(16, BF16, 'csb')
    P.op('scalar', lambda e: e.activation(csb.ap, cs32.ap, AF.Silu), reads=[cs32], writes=[csb])

    Wa = [P.alloc(8 * 1024, BF16, f'Wa{j}') for j in range(2)]
    n = 0
    for i in range(4):
        for m in range(9):
            W = Wa[n % 2]
            n += 1
            P.dma('gpsimd', W.ap.rearrange("p (k n) -> p k n", k=8),
                  I['ada_w'][i].rearrange("(k p) n -> p k n", p=128)[:, :, m * 1024:(m + 1) * 1024], writes=[W])
            ps = P.next_psum()
            for oc in range(8):
                for kc in range(8):
                    P.op('tensor', lambda e, W=W, ps=ps, oc=oc, kc=kc: e.matmul(
                        ps.ap[:, oc * 2:oc * 2 + 2], W.ap[:, kc * 1024 + oc * 128: kc * 1024 + (oc + 1) * 128],
                        csb.ap[:, kc * 2:kc * 2 + 2], start=(kc == 0), stop=(kc == 7)),
                        reads=[W, csb], writes=[ps])
            base = mod_col(i, m, 0, 0)
            for s in range(2):
                P.op('vector', lambda e, ps=ps, s=s, base=base, i=i, m=m: e.tensor_tensor(
                    k.modT.ap[:, base + s: base + 16: 2], ps.ap[:, s:16:2],
                    adabT.ap[:, (i * 9 + m) * 8:(i * 9 + m) * 8 + 8], ALU.add),
                    reads=[ps, adabT], writes=[k.modT])
    for i in range(4):
        for j in range(3):
            for s in range(2):
                col = ((i * 3 + j) * 2 + s) * 8
                b_scale = mod_col(i, 3 * j + 1, 0, s)
                b_gate = mod_col(i, 3 * j + 2, 0, s)
                P.op('vector', lambda e, col=col, b=b_scale, i=i, j=j: e.scalar_tensor_tensor(
                    k.WS.ap[:, col:col + 8], k.modT.ap[:, b:b + 15:2], 1.0,
                    k.normT.ap[:, (i * 3 + j) * 8:(i * 3 + j) * 8 + 8], ALU.add, ALU.mult),
                    reads=[k.modT, k.normT], writes=[k.WS])
                P.op('vector', lambda e, col=col, b=b_gate, j=j: e.tensor_scalar(
                    k.HG.ap[:, col:col + 8], k.modT.ap[:, b:b + 15:2], (1.0 if j == 1 else 0.5), None, ALU.mult),
                    reads=[k.modT], writes=[k.HG])

    xin = [P.alloc(1024, F32, f'xin{j}') for j in range(2)]
    stg = [P.alloc(1024, F32, f'stg{j}') for j in range(2)]
    for blk in range(LT // 128):
        src = I['ctx'][blk * 128:(blk + 1) * 128, :] if blk < 2 else I['x'][(blk - 2) * 128:(blk - 1) * 128, :]
        xi = xin[blk % 2]
        sg = stg[blk % 2]
        P.dma('sync', xi.ap, src, writes=[xi])
        for h in range(2):
            ps = P.next_psum()
            for jj in range(4):
                c = h * 4 + jj
                P.op('tensor', lambda e, ps=ps, xi=xi, c=c, jj=jj: e.matmul(
                    ps.ap[:, jj * 128:(jj + 1) * 128], xi.ap[:, c * 128:(c + 1) * 128], k.ident_f.ap,
                    start=True, stop=True), reads=[xi, k.ident_f], writes=[ps])
            eng = 'vector' if h == 0 else 'scalar'
            if h == 0:
                P.op('vector', lambda e, ps=ps, sg=sg: e.tensor_copy(sg.ap[:, 0:512], ps.ap), reads=[ps], writes=[sg])
            else:
                P.op('scalar', lambda e, ps=ps, sg=sg: e.copy(sg.ap[:, 512:1024], ps.ap), reads=[ps], writes=[sg])
        P.dma('sync', k.xT_p[:, :, blk * 128:(blk + 1) * 128], sg.ap.rearrange("p (c t) -> p c t", c=8),
              reads=[sg], writes=[k.xTb[blk // 2]])
    P.release(mk)


def WS_ap(k, i, j, s, c):
    col = ((i * 3 + j) * 2 + s) * 8 + c
    return k.WS.ap[:, col:col + 1]


def HG_ap(k, i, j, s, c):
    col = ((i * 3 + j) * 2 + s) * 8 + c
    return k.HG.ap[:, col:col + 1]


def SH_ap(k, i, j, s, c):
    col = mod_col(i, 3 * j, c, s)
    return k.modT.ap[:, col:col + 1]


def alloc_norm_scratch(k, T):
    P = k.P
    k.nT = T
    sq = P.alloc(8 * T, BF16, 'sq')
    k.sq_c = [sub(sq, sq.ap[:, c * T:(c + 1) * T]) for c in range(8)]
    k.rstd = P.alloc(T, F32, 'rstd')
    k.ntmp = [P.alloc(T, F32, f'ntmp{j}') for j in range(2)]


def norm_tile(k, xt_c, hT_c, ws, sh, T, extra_reads=()):
    P = k.P
    sq_c, rstd, ntmp = k.sq_c, k.rstd, k.ntmp
    wsa = [ws(c) for c in range(8)]
    sha = [sh(c) for c in range(8)] if sh is not None else None
    for c in range(8):
        P.op('scalar', lambda e, c=c: e.activation(sq_c[c].ap[:, :T], xt_c[c].ap[:, :T], AF.Square),
             reads=[xt_c[c]], writes=[sq_c[c]])
    ps = P.next_psum()
    for c in range(8):
        P.op('tensor', lambda e, c=c, ps=ps: e.matmul(ps.ap[:, :T], k.ones_b.ap, sq_c[c].ap[:, :T],
                                                      start=(c == 0), stop=(c == 7)),
             reads=[sq_c[c], k.ones_b], writes=[ps])
    P.op('scalar', lambda e, ps=ps: e.activation(rstd.ap[:, :T], ps.ap[:, :T], AF.Sqrt, bias=EPS, scale=1.0 / D),
         reads=[ps], writes=[rstd])
    P.op('vector', lambda e: e.reciprocal(rstd.ap[:, :T], rstd.ap[:, :T]), reads=[rstd], writes=[rstd])
    for c in range(8):
        tmp = ntmp[c % 2]
        P.op('gpsimd', lambda e, c=c, tmp=tmp: e.tensor_tensor(
            tmp.ap[:, :T], xt_c[c].ap[:, :T], rstd.ap[:, :T], ALU.mult),
            reads=[xt_c[c], rstd], writes=[tmp])
        if sh is None:
            P.op('scalar', lambda e, c=c, tmp=tmp: e.activation(
                hT_c[c].ap[:, :T], tmp.ap[:, :T], AF.Identity, scale=wsa[c]),
                reads=[tmp, k.WS, k.fnT], writes=[hT_c[c]])
        else:
            P.op('scalar', lambda e, c=c, tmp=tmp: e.activation(
                hT_c[c].ap[:, :T], tmp.ap[:, :T], AF.Identity, bias=sha[c], scale=wsa[c]),
                reads=[tmp, k.modT, k.WS, k.fnT], writes=[hT_c[c]])


def ffn_phase(k, i, jf):
    P, I = k.P, k.I
    mk = P.mark()
    jn = 0 if jf == 0 else 2
    T = TT
    Win = P.alloc(8 * 5632, BF16, 'Win')
    Wout = P.alloc(22 * 1024, BF16, 'Wout')
    Win_k = [sub(Win, Win.ap[:, kc * 5632:(kc + 1) * 5632]) for kc in range(8)]
    Wout_f = [sub(Wout, Wout.ap[:, f * 1024:(f + 1) * 1024]) for f in range(22)]
    for kc in range(8):
        for q4 in range(4):
            P.dma('gpsimd', Win_k[kc].ap[:, q4 * 1408:(q4 + 1) * 1408],
                  I['ffn_w_in'][i, jf, kc * 128:(kc + 1) * 128, q4 * 1408:(q4 + 1) * 1408], writes=[Win_k[kc]])
    for f in range(22):
        P.dma('gpsimd', Wout_f[f].ap, I['ffn_w_out'][i, jf, f * 128:(f + 1) * 128, :], writes=[Wout_f[f]])
    alloc_norm_scratch(k, T)
    xt = [P.alloc(8 * T, F32, f'xt{j}') for j in range(2)]
    xt_c = [[sub(b, b.ap[:, c * T:(c + 1) * T]) for c in range(8)] for b in xt]
    hT = [P.alloc(8 * T, BF16, f'hT{j}') for j in range(2)]
    hT_c = [[sub(b, b.ap[:, c * T:(c + 1) * T]) for c in range(8)] for b in hT]
    act = P.alloc(22 * T, BF16, 'act')
    act_f = [sub(act, act.ap[:, f * T:(f + 1) * T]) for f in range(22)]
    sg = [P.alloc(T, F32, f'sg{j}') for j in range(2)]

    for tix in range(NTT):
        s = 1 if tix == 0 else 0
        t0 = tix * T
        X, Xc, Hc = xt[tix % 2], xt_c[tix % 2], hT_c[tix % 2]
        P.dma('sync', X.ap.rearrange("p (c t) -> p c t", c=8), k.xT_p[:, :, t0:t0 + T],
              reads=[k.xTb[tix]], writes=Xc)
        norm_tile(k, Xc, Hc, lambda c, s=s: WS_ap(k, i, jn, s, c), lambda c, s=s: SH_ap(k, i, jn, s, c), T)
        if tix == 0 and i == 0 and jf == 0:
            dbg_dump(k, Xc[0], label='x0')
            dbg_dump(k, k.rstd, label='rstd')
            dbg_dump(k, Hc[0], label='h0')
            dbg_dump(k, Win_k[0], Win_k[0].ap[:, 0:512], label='win0')
            dbg_dump(k, Wout_f[0], Wout_f[0].ap[:, 0:512], label='wout0')
        for f in range(22):
            pg, pu = P.next_psum(), P.next_psum()
            for (pp, off) in ((pg, 0), (pu, DFF)):
                for kc in range(8):
                    P.op('tensor', lambda e, pp=pp, off=off, kc=kc, f=f, Hc=Hc: e.matmul(
                        pp.ap[:, :T], Win_k[kc].ap[:, off + f * 128: off + (f + 1) * 128], Hc[kc].ap,
                        start=(kc == 0), stop=(kc == 7)), reads=[Win_k[kc], Hc[kc]], writes=[pp])
            sgb = sg[f % 2]
            P.op('scalar', lambda e, pg=pg, sgb=sgb: e.activation(sgb.ap, pg.ap[:, :T], AF.Silu),
                 reads=[pg], writes=[sgb])
            P.op('vector', lambda e, pu=pu, sgb=sgb, f=f: e.tensor_tensor(act_f[f].ap, sgb.ap, pu.ap[:, :T], ALU.mult),
                 reads=[pu, sgb], writes=[act_f[f]])
            if tix == 0 and i == 0 and jf == 0 and f == 0:
                dbg_dump(k, sgb, label='silu_g0')
                dbg_dump(k, act_f[0], label='act0')
        for c in range(8):
            po = P.next_psum()
            for f in range(22):
                P.op('tensor', lambda e, po=po, f=f, c=c: e.matmul(
                    po.ap[:, :T], Wout_f[f].ap[:, c * 128:(c + 1) * 128], act_f[f].ap,
                    start=(f == 0), stop=(f == 21)), reads=[Wout_f[f], act_f[f]], writes=[po])
            P.op('vector', lambda e, po=po, c=c, Xc=Xc, s=s: e.scalar_tensor_tensor(
                Xc[c].ap, po.ap[:, :T], HG_ap(k, i, jn, s, c), Xc[c].ap, ALU.mult, ALU.add),
                reads=[po, Xc[c], k.HG], writes=[Xc[c]])
        P.dma('sync', k.xT_p[:, :, t0:t0 + T], X.ap.rearrange("p (c t) -> p c t", c=8),
              reads=Xc, writes=[k.xTb[tix]])
    P.release(mk)


def epilogue(k, out):
    P = k.P
    mk = P.mark()
    T = TT
    alloc_norm_scratch(k, T)
    xt = [P.alloc(8 * T, F32, f'ext{j}') for j in range(2)]
    xt_c = [[sub(b, b.ap[:, c * T:(c + 1) * T]) for c in range(8)] for b in xt]
    yT = [P.alloc(8 * T, F32, f'eyT{j}') for j in range(2)]
    yT_c = [[sub(b, b.ap[:, c * T:(c + 1) * T]) for c in range(8)] for b in yT]
    stg = [P.alloc(1024, F32, f'estg{j}') for j in range(2)]
    n = 0
    for tix in range(1, NTT):
        t0 = tix * T
        X, Xc, Yc = xt[tix % 2], xt_c[tix % 2], yT_c[tix % 2]
        P.dma('sync', X.ap.rearrange("p (c t) -> p c t", c=8), k.xT_p[:, :, t0:t0 + T],
              reads=[k.xTb[tix]], writes=Xc)
        norm_tile(k, Xc, Yc, lambda c: k.fnT.ap[:, c:c + 1], None, T)
        for tb in range(T // 128):
            sgb = stg[n % 2]
            n += 1
            for h in range(2):
                ps = P.next_psum()
                for jj in range(4):
                    c = h * 4 + jj
                    P.op('tensor', lambda e, ps=ps, c=c, jj=jj, tb=tb, Yc=Yc: e.matmul(
                        ps.ap[:, jj * 128:(jj + 1) * 128], Yc[c].ap[:, tb * 128:(tb + 1) * 128], k.ident_f.ap,
                        start=True, stop=True), reads=[Yc[c], k.ident_f], writes=[ps])
                if h == 0:
                    P.op('vector', lambda e, ps=ps, sgb=sgb: e.tensor_copy(sgb.ap[:, 0:512], ps.ap),
                         reads=[ps], writes=[sgb])
                else:
                    P.op('scalar', lambda e, ps=ps, sgb=sgb: e.copy(sgb.ap[:, 512:1024], ps.ap),
                         reads=[ps], writes=[sgb])
            r0 = t0 - LC + tb * 128
            P.dma('sync', out[r0:r0 + 128, :], sgb.ap, reads=[sgb])
    P.release(mk)


def mixer_prep(k, i, permute):
    P = k.P
    mk = P.mark()
    T = TT
    alloc_norm_scratch(k, T)
    xt = [P.alloc(8 * T, F32, f'pxt{j}') for j in range(2)]
    xt_c = [[sub(b, b.ap[:, c * T:(c + 1) * T]) for c in range(8)] for b in xt]
    if permute:
        hall = P.alloc(8 * L, BF16, 'hall')
        hc0 = P.alloc(8 * T, BF16, 'hc0')
        hc0_c = [sub(hc0, hc0.ap[:, c * T:(c + 1) * T]) for c in range(8)]
        hall_c = [sub(hall, hall.ap[:, c * L:(c + 1) * L]) for c in range(8)]
        perm = [P.alloc(L, BF16, f'perm{j}') for j in range(2)]
    else:
        hT = [P.alloc(8 * T, BF16, f'phT{j}') for j in range(2)]
        hT_c = [[sub(b, b.ap[:, c * T:(c + 1) * T]) for c in range(8)] for b in hT]
    for tix in range(NTT):
        s = 1 if tix == 0 else 0
        t0 = tix * T
        X, Xc = xt[tix % 2], xt_c[tix % 2]
        P.dma('sync', X.ap.rearrange("p (c t) -> p c t", c=8), k.xT_p[:, :, t0:t0 + T],
              reads=[k.xTb[tix]], writes=Xc)
        if permute and tix > 0:
            Hc = [Buf(hall_c[c].ap[:, t0 - LC:t0 - LC + T]) for c in range(8)]
        elif permute:
            Hc = hc0_c
        else:
            Hc = hT_c[tix % 2]
        norm_tile(k, Xc, Hc, lambda c, s=s: WS_ap(k, i, 1, s, c), lambda c, s=s: SH_ap(k, i, 1, s, c), T)
        if permute and tix > 0:
            for c in range(8):
                hall_c[c].w = Hc[c].w
        elif permute:
            P.dma('sync', k.hT_p[:, :, 0:T], hc0.ap.rearrange("p (c t) -> p c t", c=8), reads=hc0_c,
                  writes=[k.hTb[0]])
        else:
            P.dma('sync', k.hT_p[:, :, t0:t0 + T], hT[tix % 2].ap.rearrange("p (c t) -> p c t", c=8),
                  reads=Hc, writes=[k.hTb[tix]])
    if permute:
        for c in range(8):
            pb = perm[c % 2]
            eng = 'vector' if c % 2 == 0 else 'gpsimd'
            P.op(eng, lambda e, c=c, pb=pb: e.tensor_copy(
                pb.ap.rearrange("p (cc r) -> p cc r", r=64),
                hall_c[c].ap.rearrange("p (r cc) -> p cc r", cc=64)), reads=[hall_c[c]], writes=[pb])
            P.dma('sync', k.hT_d[c, :, LC:LT], pb.ap, reads=[pb], writes=k.hTb[1:])
    P.release(mk)


def s5_phase(k, i):
    P, I = k.P, k.I
    js = i // 3
    mk = P.mark()
    V, G, A = 'vector', 'gpsimd', 'scalar'

    def tt(out, a, b, op, eng=V):
        P.op(eng, lambda e: e.tensor_tensor(out.ap, a.ap, b.ap, op), reads=[a, b], writes=[out])

    def ts(out, a, s1, op0, s2=None, op1=None, eng=V):
        if op1 is None:
            P.op(eng, lambda e: e.tensor_scalar(out.ap, a.ap, s1, None, op0), reads=[a], writes=[out])
        else:
            P.op(eng, lambda e: e.tensor_scalar(out.ap, a.ap, s1, s2, op0, op1), reads=[a], writes=[out])

    def act(out, a, func, scale=1.0):
        P.op(A, lambda e: e.activation(out.ap, a.ap, func, scale=scale), reads=[a], writes=[out])

    def sm(name=''):
        return P.alloc(32, F32, name)

    tmp = [sm(f't{j}') for j in range(4)]

    def csq(outr, outi, r, im):
        tt(tmp[0], r, r, ALU.mult)
        tt(tmp[1], im, im, ALU.mult)
        tt(outr, tmp[0], tmp[1], ALU.subtract)
        tt(tmp[2], r, im, ALU.mult)
        ts(outi, tmp[2], 2.0, ALU.mult)

    NLEV = 13
    par = []
    for d in range(2):
        are, aim, ls = sm(), sm(), sm()
        slow_dma(P, are.ap, I['s5_a_re'][js, d].rearrange("(q g2) p -> (g2 p) q", g2=2), writes=[are])
        slow_dma(P, aim.ap, I['s5_a_im'][js, d].rearrange("(q g2) p -> (g2 p) q", g2=2), writes=[aim])
        for g2 in range(2):
            slow_dma(P, ls.ap[g2 * 64:(g2 + 1) * 64, :],
                     I['s5_log_step'][js, d].rearrange("(q g2) -> g2 q", g2=2)[g2].partition_broadcast(64),
                     writes=[ls])
        dt, xr, th, rho = sm(), sm(), sm(), sm()
        act(dt, ls, AF.Exp)
        tt(xr, are, dt, ALU.mult)
        tt(th, aim, dt, ALU.mult)
        act(rho, xr, AF.Exp)
        zr, zi, th2 = sm(), sm(), sm()
        act(zi, th, AF.Sin, scale=1.0 / 16)
        ts(th2, th, 1.0 / 16, ALU.mult, PI / 2, ALU.add)
        act(zr, th2, AF.Sin)
        for _ in range(4):
            nr, ni = sm(), sm()
            csq(nr, ni, zr, zi)
            zr, zi = nr, ni
        cr, ci = zr, zi
        abr, abi, den, cfr, cfi = sm(), sm(), sm(), sm(), sm()
        tt(abr, rho, cr, ALU.mult)
        tt(abi, rho, ci, ALU.mult)
        tt(tmp[0], are, are, ALU.mult)
        tt(tmp[1], aim, aim, ALU.mult)
        tt(den, tmp[0], tmp[1], ALU.add)
        P.op(V, lambda e, den=den: e.reciprocal(den.ap, den.ap), reads=[den], writes=[den])
        zr_ = sm()
        ts(zr_, abr, -1.0, ALU.add)
        tt(tmp[0], zr_, are, ALU.mult)
        tt(tmp[1], abi, aim, ALU.mult)
        tt(tmp[2], tmp[0], tmp[1], ALU.add)
        tt(cfr, tmp[2], den, ALU.mult)
        tt(tmp[0], abi, are, ALU.mult)
        tt(tmp[1], zr_, aim, ALU.mult)
        tt(tmp[2], tmp[0], tmp[1], ALU.subtract)
        tt(cfi, tmp[2], den, ALU.mult)
        U = []
        ur, ui = cr, sm()
        ts(ui, ci, -1.0, ALU.mult)
        for m in range(NLEV):
            nui = sm()
            ts(nui, ui, -1.0, ALU.mult)
            U.append((ur, ui, nui))
            if m < NLEV - 1:
                nr, ni = sm(), sm()
                csq(nr, ni, ur, ui)
                ur, ui = nr, ni
        par.append(dict(rho=rho, cfr=cfr, cfi=cfi, U=U))

    Bn = [[P.alloc(32 * 32, F32, f'Bn{d}{r}') for r in range(2)] for d in range(2)]
    Cn = [[P.alloc(32 * 32, F32, f'Cn{d}{r}') for r in range(2)] for d in range(2)]
    mk2 = P.mark()
    Xz = P.alloc(8 * 128, F32, 'Xz')
    for d in range(2):
        for r, nm in enumerate(('s5_b_re', 's5_b_im')):
            T_ = Bn[d][r]
            P.op(G, lambda e, T_=T_: e.memset(T_.ap, 0.0), writes=[T_])
            v = T_.ap.rearrange("p (q c) -> p q c", c=32)
            for g2 in range(2):
                slow_dma(P, v[g2 * 64:(g2 + 1) * 64, :, g2 * 16:(g2 + 1) * 16],
                         I[nm][js, d].rearrange("(q g2) p c -> g2 p q c", g2=2)[g2], writes=[T_])
        for r, nm in enumerate(('s5_c_re', 's5_c_im')):
            T_ = Cn[d][r]
            for q8 in range(4):
                P.op(G, lambda e: e.memset(Xz.ap, 0.0), writes=[Xz])
                xv = Xz.ap.rearrange("p (q m) -> p q m", m=128)
                for g2 in range(2):
                    slow_dma(P, xv[g2 * 16:(g2 + 1) * 16, :, g2 * 64:(g2 + 1) * 64],
                             I[nm][js, d].rearrange("(q g2) c p -> g2 c q p", g2=2)[g2][:, q8 * 8:(q8 + 1) * 8, :],
                             writes=[Xz])
                ps = P.next_psum()
                for q in range(8):
                    P.op('tensor', lambda e, ps=ps, q=q: e.matmul(
                        ps.ap[:, q * 32:(q + 1) * 32], Xz.ap[0:32, q * 128:(q + 1) * 128], k.ident_f.ap[0:32, 0:32],
                        start=True, stop=True), reads=[Xz, k.ident_f], writes=[ps])
                P.op(V, lambda e, ps=ps, T_=T_, q8=q8: e.tensor_copy(
                    T_.ap[:, q8 * 256:(q8 + 1) * 256], ps.ap[:, 0:256]), reads=[ps], writes=[T_])
    P.release(mk2)
    P.barrier()

    Bw = [[P.alloc(128, BF16, f'Bw{q}{r}') for r in range(2)] for q in range(4)]
    Cw = [[P.alloc(128, BF16, f'Cw{q}{r}') for r in range(2)] for q in range(4)]
    for q in range(4):
        for r in range(2):
            P.op(G, lambda e, b=Bw[q][r]: e.memset(b.ap, 0.0), writes=[Bw[q][r]])
            P.op(G, lambda e, b=Cw[q][r]: e.memset(b.ap, 0.0), writes=[Cw[q][r]])
    BT = [[P.alloc(128, BF16, f'BT{j}{r}') for r in range(2)] for j in range(2)]
    ctmp = [P.alloc(32, F32, f'ctmp{j}') for j in range(3)]

    TS = 512
    uT = [P.alloc(LT, BF16, f'uT{j}') for j in range(2)]
    Tr = P.alloc(LT, F32, 'Tr')
    Ti = P.alloc(LT, F32, 'Ti')
    ttmp = [P.alloc(2048, F32, f'ttmp{j}') for j in range(2)]
    yT = P.alloc(LT, F32, 'yT')
    NB = 2
    prs = [P.alloc(TS, F32, f'prs{j}') for j in range(NB)]
    pis = [P.alloc(TS, F32, f'pis{j}') for j in range(NB)]
    t1 = [P.alloc(TS, F32, f't1_{j}') for j in range(NB)]
    t2 = [P.alloc(TS, F32, f't2_{j}') for j in range(NB)]
    t3 = [P.alloc(TS, F32, f't3_{j}') for j in range(NB)]
    t4 = [P.alloc(TS, F32, f't4_{j}') for j in range(NB)]
    wr = [P.alloc(TS, F32, f'wr{j}') for j in range(NB)]
    wi = [P.alloc(TS, F32, f'wi{j}') for j in range(NB)]
    sr = [P.alloc(TS, F32, f'sr{j}') for j in range(NB)]
    si = [P.alloc(TS, F32, f'si{j}') for j in range(NB)]
    hr = [P.alloc(TS, BF16, f'hr{j}') for j in range(NB)]
    hi = [P.alloc(TS, BF16, f'hi{j}') for j in range(NB)]
    dT = P.alloc(8, F32, 'dT')
    slow_dma(P, dT.ap, I['s5_d'][js].rearrange("(c p) -> p c", p=128), writes=[dT])

    tiles = [(0, 256)] + [(256 + j * TS, TS) for j in range(L // TS)]

    def nat_range(d, tau0, w):
        if d == 0:
            return tau0, tau0 + w, False
        if tau0 < LC:
            hi_ = LC - 1 - tau0
        else:
            hi_ = LT + LC - 1 - tau0
        return hi_ - w + 1, hi_ + 1, True

    def rv(ap, rev):
        return ap[:, ::-1] if rev else ap

    first_y = True
    for fc in range(8):
        u = uT[fc % 2]
        P.dma('sync', u.ap, k.hT_d[fc], reads=k.hTb, writes=[u])
        first_y = True
        for d in range(2):
            pr_ = par[d]
            for ql in range(4):
                pq = fc * 4 + ql
                blk = slice(pq * 32, pq * 32 + 32)
                for r in range(2):
                    P.op(G, lambda e, r=r, ql=ql, blk=blk, d=d: e.tensor_copy(
                        Bw[ql][r].ap[:, ql * 32:ql * 32 + 32], Bn[d][r].ap[:, blk]),
                        reads=[Bn[d][r]], writes=[Bw[ql][r]])
                bt = BT[pq % 2]
                for r in range(2):
                    ps = P.next_psum()
                    P.op('tensor', lambda e, ps=ps, r=r, ql=ql: e.matmul(
                        ps.ap[:, 0:128], Bw[ql][r].ap, k.ident_b.ap, start=True, stop=True),
                        reads=[Bw[ql][r], k.ident_b], writes=[ps])
                    P.op(A, lambda e, ps=ps, r=r, bt=bt: e.copy(bt[r].ap, ps.ap[:, 0:128]),
                         reads=[ps], writes=[bt[r]])
                cfr_ap = pr_['cfr'].ap[:, pq:pq + 1]
                cfi_ap = pr_['cfi'].ap[:, pq:pq + 1]
                cre, cim = Cn[d][0], Cn[d][1]
                P.op(V, lambda e, cim=cim, blk=blk, cfi_ap=cfi_ap: e.tensor_scalar(
                    ctmp[0].ap, cim.ap[:, blk], cfi_ap, None, ALU.mult), reads=[cim, pr_['cfi']], writes=[ctmp[0]])
                P.op(V, lambda e, cre=cre, blk=blk, cfr_ap=cfr_ap, ql=ql: e.scalar_tensor_tensor(
                    Cw[ql][0].ap[:, ql * 32:ql * 32 + 32], cre.ap[:, blk], cfr_ap, ctmp[0].ap, ALU.mult, ALU.subtract),
                    reads=[cre, ctmp[0], pr_['cfr']], writes=[Cw[ql][0]])
                P.op(V, lambda e, cim=cim, blk=blk, cfr_ap=cfr_ap: e.tensor_scalar(
                    ctmp[1].ap, cim.ap[:, blk], cfr_ap, None, ALU.mult), reads=[cim, pr_['cfr']], writes=[ctmp[1]])
                P.op(V, lambda e, cre=cre, blk=blk, cfi_ap=cfi_ap: e.scalar_tensor_tensor(
                    ctmp[2].ap, cre.ap[:, blk], cfi_ap, ctmp[1].ap, ALU.mult, ALU.add),
                    reads=[cre, ctmp[1], pr_['cfi']], writes=[ctmp[2]])
                P.op(V, lambda e, ql=ql: e.tensor_scalar(
                    Cw[ql][1].ap[:, ql * 32:ql * 32 + 32], ctmp[2].ap, -1.0, None, ALU.mult),
                    reads=[ctmp[2]], writes=[Cw[ql][1]])
                P.op(G, lambda e: e.memset(Tr.ap[:, 0:1], 1.0), writes=[Tr])
                P.op(G, lambda e: e.memset(Ti.ap[:, 0:1], 0.0), writes=[Ti])
                for m in range(NLEV):
                    n = 1 << m
                    cnt = min(n, LT - n)
                    ur, ui, nui = pr_['U'][m]
                    ur_ap, ui_ap, nui_ap = ur.ap[:, pq:pq + 1], ui.ap[:, pq:pq + 1], nui.ap[:, pq:pq + 1]
                    for c0 in range(0, cnt, 2048):
                        cw = min(2048, cnt - c0)
                        a0, a1 = c0, c0 + cw
                        b0, b1 = n + c0, n + c0 + cw
                        e1 = G if cw >= 256 else V
                        P.op(e1, lambda e, a0=a0, a1=a1, cw=cw, nui_ap=nui_ap: e.tensor_scalar(
                            ttmp[0].ap[:, 0:cw], Ti.ap[:, a0:a1], nui_ap, None, ALU.mult),
                            reads=[Ti, nui], writes=[ttmp[0]])
                        P.op(V, lambda e, a0=a0, a1=a1, b0=b0, b1=b1, cw=cw, ur_ap=ur_ap: e.scalar_tensor_tensor(
                            Tr.ap[:, b0:b1], Tr.ap[:, a0:a1], ur_ap, ttmp[0].ap[:, 0:cw], ALU.mult, ALU.add),
                            reads=[Tr, ttmp[0], ur], writes=[Tr])
                        P.op(e1, lambda e, a0=a0, a1=a1, cw=cw, ur_ap=ur_ap: e.tensor_scalar(
                            ttmp[1].ap[:, 0:cw], Ti.ap[:, a0:a1], ur_ap, None, ALU.mult),
                            reads=[Ti, ur], writes=[ttmp[1]])
                        P.op(V, lambda e, a0=a0, a1=a1, b0=b0, b1=b1, cw=cw, ui_ap=ui_ap: e.scalar_tensor_tensor(
                            Ti.ap[:, b0:b1], Tr.ap[:, a0:a1], ui_ap, ttmp[1].ap[:, 0:cw], ALU.mult, ALU.add),
                            reads=[Tr, ttmp[1], ui], writes=[Ti])
                rho_ap = pr_['rho'].ap[:, pq:pq + 1]
                prev = None
                for tix, (tau0, w) in enumerate(tiles):
                    n0, n1, rev = nat_range(d, tau0, w)
                    j = tix % NB
                    pA, pB = P.next_psum(), P.next_psum()
                    for (pp, r) in ((pA, 0), (pB, 1)):
                        P.op('tensor', lambda e, pp=pp, r=r, bt=bt, n0=n0, n1=n1, w=w, u=u: e.matmul(
                            pp.ap[:, 0:w], bt[r].ap, u.ap[:, n0:n1], start=True, stop=True),
                            reads=[bt[r], u], writes=[pp])
                    P.op(A, lambda e, pA=pA, j=j, w=w: e.copy(prs[j].ap[:, 0:w], pA.ap[:, 0:w]),
                         reads=[pA], writes=[prs[j]])
                    P.op(A, lambda e, pB=pB, j=j, w=w: e.copy(pis[j].ap[:, 0:w], pB.ap[:, 0:w]),
                         reads=[pB], writes=[pis[j]])
                    trs, tis = Tr.ap[:, tau0:tau0 + w], Ti.ap[:, tau0:tau0 + w]
                    P.op(V, lambda e, j=j, w=w, rev=rev, trs=trs: e.tensor_tensor(
                        t1[j].ap[:, 0:w], rv(prs[j].ap[:, 0:w], rev), trs, ALU.mult), reads=[prs[j], Tr], writes=[t1[j]])
                    P.op(V, lambda e, j=j, w=w, rev=rev, tis=tis: e.tensor_tensor(
                        t2[j].ap[:, 0:w], rv(pis[j].ap[:, 0:w], rev), tis, ALU.mult), reads=[pis[j], Ti], writes=[t2[j]])
                    P.op(V, lambda e, j=j, w=w, rev=rev, trs=trs: e.tensor_tensor(
                        t3[j].ap[:, 0:w], rv(pis[j].ap[:, 0:w], rev), trs, ALU.mult), reads=[pis[j], Tr], writes=[t3[j]])
                    P.op(V, lambda e, j=j, w=w, rev=rev, tis=tis: e.tensor_tensor(
                        t4[j].ap[:, 0:w], rv(prs[j].ap[:, 0:w], rev), tis, ALU.mult), reads=[prs[j], Ti], writes=[t4[j]])
                    P.op(G, lambda e, j=j, w=w: e.tensor_tensor(
                        wr[j].ap[:, 0:w], t1[j].ap[:, 0:w], t2[j].ap[:, 0:w], ALU.subtract),
                        reads=[t1[j], t2[j]], writes=[wr[j]])
                    P.op(G, lambda e, j=j, w=w: e.tensor_tensor(
                        wi[j].ap[:, 0:w], t3[j].ap[:, 0:w], t4[j].ap[:, 0:w], ALU.add),
                        reads=[t3[j], t4[j]], writes=[wi[j]])
                    for (sb, wb) in ((sr, wr), (si, wi)):
                        if prev is None:
                            init, rd = 0.0, []
                        else:
                            pj, pw = prev
                            init, rd = sb[pj].ap[:, pw - 1:pw], [sb[pj]]
                        P.op(V, lambda e, sb=sb, wb=wb, j=j, w=w, init=init, rho_ap=rho_ap: e.tensor_tensor_scan(
                            sb[j].ap[:, 0:w], rho_ap.broadcast_to([128, w]), wb[j].ap[:, 0:w], init,
                            ALU.mult, ALU.add), reads=[wb[j], pr_['rho']] + rd, writes=[sb[j]])
                    prev = (j, w)
                    P.op(G, lambda e, j=j, w=w, trs=trs: e.tensor_tensor(
                        t1[j].ap[:, 0:w], sr[j].ap[:, 0:w], trs, ALU.mult), reads=[sr[j], Tr], writes=[t1[j]])
                    P.op(G, lambda e, j=j, w=w, tis=tis: e.tensor_tensor(
                        t2[j].ap[:, 0:w], si[j].ap[:, 0:w], tis, ALU.mult), reads=[si[j], Ti], writes=[t2[j]])
                    P.op(G, lambda e, j=j, w=w, trs=trs: e.tensor_tensor(
                        t3[j].ap[:, 0:w], si[j].ap[:, 0:w], trs, ALU.mult), reads=[si[j], Tr], writes=[t3[j]])
                    P.op(G, lambda e, j=j, w=w, tis=tis: e.tensor_tensor(
                        t4[j].ap[:, 0:w], sr[j].ap[:, 0:w], tis, ALU.mult), reads=[sr[j], Ti], writes=[t4[j]])
                    P.op(V, lambda e, j=j, w=w, rev=rev: e.tensor_tensor(
                        rv(hr[j].ap[:, 0:w], rev), t1[j].ap[:, 0:w], t2[j].ap[:, 0:w], ALU.add),
                        reads=[t1[j], t2[j]], writes=[hr[j]])
                    P.op(V, lambda e, j=j, w=w, rev=rev: e.tensor_tensor(
                        rv(hi[j].ap[:, 0:w], rev), t3[j].ap[:, 0:w], t4[j].ap[:, 0:w], ALU.subtract),
                        reads=[t3[j], t4[j]], writes=[hi[j]])
                    py = P.next_psum()
                    P.op('tensor', lambda e, py=py, ql=ql, j=j, w=w: e.matmul(
                        py.ap[:, 0:w], Cw[ql][0].ap, hr[j].ap[:, 0:w], start=True, stop=False),
                        reads=[Cw[ql][0], hr[j]], writes=[py])
                    P.op('tensor', lambda e, py=py, ql=ql, j=j, w=w: e.matmul(
                        py.ap[:, 0:w], Cw[ql][1].ap, hi[j].ap[:, 0:w], start=False, stop=True),
                        reads=[Cw[ql][1], hi[j]], writes=[py])
                    if d == 0 and ql == 0:
                        P.op(A, lambda e, py=py, n0=n0, n1=n1, w=w: e.copy(yT.ap[:, n0:n1], py.ap[:, 0:w]),
                             reads=[py], writes=[yT])
                    else:
                        P.op(V, lambda e, py=py, n0=n0, n1=n1, w=w: e.tensor_tensor(
                            yT.ap[:, n0:n1], py.ap[:, 0:w], yT.ap[:, n0:n1], ALU.add), reads=[py, yT], writes=[yT])
        for tix, (tau0, w) in enumerate(tiles):
            j = tix % NB
            sl = slice(tau0, tau0 + w)
            P.op(V, lambda e, j=j, w=w, sl=sl, u=u, fc=fc: e.scalar_tensor_tensor(
                t1[j].ap[:, 0:w], u.ap[:, sl], dT.ap[:, fc:fc + 1], yT.ap[:, sl], ALU.mult, ALU.add),
                reads=[u, yT, dT], writes=[t1[j]])
            P.op(G, lambda e, j=j, w=w: e.tensor_tensor(
                t2[j].ap[:, 0:w], t1[j].ap[:, 0:w], t1[j].ap[:, 0:w], ALU.mult), reads=[t1[j]], writes=[t2[j]])
            P.op(G, lambda e, j=j, w=w: e.tensor_scalar(
                t2[j].ap[:, 0:w], t2[j].ap[:, 0:w], 0.044715, 1.0, ALU.mult, ALU.add), reads=[t2[j]], writes=[t2[j]])
            P.op(G, lambda e, j=j, w=w: e.tensor_tensor(
                t3[j].ap[:, 0:w], t2[j].ap[:, 0:w], t1[j].ap[:, 0:w], ALU.mult), reads=[t1[j], t2[j]], writes=[t3[j]])
            P.op(A, lambda e, j=j, w=w: e.activation(
                t4[j].ap[:, 0:w], t3[j].ap[:, 0:w], AF.Sigmoid, scale=1.5957691216057308), reads=[t3[j]], writes=[t4[j]])
            P.op(V, lambda e, j=j, w=w: e.tensor_tensor(
                hr[j].ap[:, 0:w], t1[j].ap[:, 0:w], t4[j].ap[:, 0:w], ALU.mult), reads=[t1[j], t4[j]], writes=[hr[j]])
            P.dma('sync', k.yg_d[fc, :, sl], hr[j].ap[:, 0:w], reads=[hr[j]], writes=[k.ygb[tau0 // TT]])
    P.release(mk)
    P.barrier()
    s5_glu(k, i)


def s5_glu(k, i):
    P, I = k.P, k.I
    js = i // 3
    mk = P.mark()
    T = TT
    Wg = P.alloc(8 * 2048, BF16, 'Wg')
    Wg_k = [sub(Wg, Wg.ap[:, kc * 2048:(kc + 1) * 2048]) for kc in range(8)]
    for kc in range(8):
        for h in range(2):
            P.dma('gpsimd', Wg_k[kc].ap[:, h * 1024:(h + 1) * 1024],
                  I['s5_w_glu'][js, kc * 128:(kc + 1) * 128, h * 1024:(h + 1) * 1024], writes=[Wg_k[kc]])
    bT = P.alloc(16, F32, 'bgluT')
    slow_dma(P, bT.ap, I['s5_b_glu'][js].rearrange("(c p) -> p c", p=128), writes=[bT])
    xt = [P.alloc(8 * T, F32, f'gxt{j}') for j in range(2)]
    xt_c = [[sub(b, b.ap[:, c * T:(c + 1) * T]) for c in range(8)] for b in xt]
    yg = [P.alloc(8 * T, BF16, f'gyg{j}') for j in range(2)]
    sg = [P.alloc(T, F32, f'gsg{j}') for j in range(2)]
    ob = [P.alloc(T, F32, f'gob{j}') for j in range(2)]
    for tix in range(NTT):
        s = 1 if tix == 0 else 0
        t0 = tix * T
        X, Xc, Y = xt[tix % 2], xt_c[tix % 2], yg[tix % 2]
        P.dma('sync', X.ap.rearrange("p (c t) -> p c t", c=8), k.xT_p[:, :, t0:t0 + T],
              reads=[k.xTb[tix]], writes=Xc)
        P.dma('sync', Y.ap.rearrange("p (c t) -> p c t", c=8), k.yg_p[:, :, t0:t0 + T],
              reads=[k.ygb[tix]], writes=[Y])
        for c in range(8):
            pv, pg = P.next_psum(), P.next_psum()
            for (pp, off) in ((pv, 0), (pg, 1024)):
                for kc in range(8):
                    P.op('tensor', lambda e, pp=pp, off=off, kc=kc, c=c, Y=Y: e.matmul(
                        pp.ap[:, :T], Wg_k[kc].ap[:, off + c * 128: off + (c + 1) * 128],
                        Y.ap[:, kc * T:(kc + 1) * T], start=(kc == 0), stop=(kc == 7)),
                        reads=[Wg_k[kc], Y], writes=[pp])
            sgb, obb = sg[c % 2], ob[c % 2]
            P.op('scalar', lambda e, pg=pg, sgb=sgb, c=c: e.activation(
                sgb.ap, pg.ap[:, :T], AF.Sigmoid, bias=bT.ap[:, 8 + c:9 + c], scale=1.0),
                reads=[pg, bT], writes=[sgb])
            P.op('vector', lambda e, pv=pv, sgb=sgb, obb=obb, c=c: e.scalar_tensor_tensor(
                obb.ap, pv.ap[:, :T], bT.ap[:, c:c + 1], sgb.ap, ALU.add, ALU.mult),
                reads=[pv, sgb, bT], writes=[obb])
            P.op('vector', lambda e, obb=obb, c=c, Xc=Xc, s=s: e.scalar_tensor_tensor(
                Xc[c].ap, obb.ap, HG_ap(k, i, 1, s, c), Xc[c].ap, ALU.mult, ALU.add),
                reads=[obb, Xc[c], k.HG], writes=[Xc[c]])
        P.dma('sync', k.xT_p[:, :, t0:t0 + T], X.ap.rearrange("p (c t) -> p c t", c=8),
              reads=Xc, writes=[k.xTb[tix]])
    P.release(mk)


TILES512 = [(0, 256)] + [(256 + j * 512, 512) for j in range(L // 512)]
NCH = LT // 64


def nat_range(d, tau0, w):
    if d == 0:
        return tau0, tau0 + w, False
    hi_ = (LC - 1 - tau0) if tau0 < LC else (LT + LC - 1 - tau0)
    return hi_ - w + 1, hi_ + 1, True


def tv(ap, d, tau0, w):
    n0, n1, rev = nat_range(d, tau0, w)
    v = ap[:, n0:n1]
    return v[:, ::-1] if rev else v


def gated_phase(k, i, kind):
    P, I = k.P, k.I
    hg = kind == 'hgrn'
    NH = 8 if hg else 4
    DV = 128 if hg else 256
    VT = DV // 128
    W_in = I['hgrn_w_in'][0] if hg else I['gla_w_in'][0]
    NCOL = 5120 if hg else 3104
    V, G, A = 'vector', 'gpsimd', 'scalar'
    mk = P.mark()
    Wb = P.alloc(8 * NCOL, BF16, 'Wb')
    Wb_k = [sub(Wb, Wb.ap[:, kc * NCOL:(kc + 1) * NCOL]) for kc in range(8)]
    for kc in range(8):
        for c0 in range(0, NCOL, 1024):
            cw = min(1024, NCOL - c0)
            P.dma('gpsimd', Wb_k[kc].ap[:, c0:c0 + cw], W_in[kc * 128:(kc + 1) * 128, c0:c0 + cw], writes=[Wb_k[kc]])
    if hg:
        lg = P.alloc(64, F32, 'lg')
        slow_dma(P, lg.ap, I['hgrn_lb_logits'].rearrange("l d (c p) -> p (l d c)", p=128), writes=[lg])
        E = P.alloc(64, F32, 'lgE')
        P.op(A, lambda e: e.activation(E.ap, lg.ap, AF.Exp), reads=[lg], writes=[E])
        den, num, oml = P.alloc(16, F32, 'den'), P.alloc(16, F32, 'num'), P.alloc(16, F32, 'oml')
        P.op(V, lambda e: e.tensor_tensor(den.ap, E.ap[:, 0:16], E.ap[:, 16:32], ALU.add), reads=[E], writes=[den])
        P.op(V, lambda e: e.tensor_tensor(den.ap, den.ap, E.ap[:, 32:48], ALU.add), reads=[E, den], writes=[den])
        P.op(V, lambda e: e.tensor_tensor(den.ap, den.ap, E.ap[:, 48:64], ALU.add), reads=[E, den], writes=[den])
        P.op(G, lambda e: e.memset(num.ap, 0.0), writes=[num])
        for l in range(1, i + 1):
            P.op(V, lambda e, l=l: e.tensor_tensor(num.ap, num.ap, E.ap[:, l * 16:(l + 1) * 16], ALU.add),
                 reads=[E, num], writes=[num])
        P.op(V, lambda e: e.reciprocal(den.ap, den.ap), reads=[den], writes=[den])
        P.op(V, lambda e: e.tensor_tensor(num.ap, num.ap, den.ap, ALU.mult), reads=[num, den], writes=[num])
        P.op(V, lambda e: e.tensor_scalar(oml.ap, num.ap, -1.0, 1.0, ALU.mult, ALU.add), reads=[num], writes=[oml])
    else:
        wg = P.alloc(1024, BF16, 'wg', parts=16)
        P.dma('gpsimd', wg.ap.rearrange("p (z n) -> p z n", z=2), I['gla_w_gate'][0].rearrange("z r n -> r z n"),
              writes=[wg])
        nbg = P.alloc(8, F32, 'nbg')
        slow_dma(P, nbg.ap, I['gla_b_gate'][0].rearrange("z (h p) -> p (z h)", p=128), writes=[nbg])
        P.op(V, lambda e: e.tensor_scalar(nbg.ap, nbg.ap, -1.0, None, ALU.mult), reads=[nbg], writes=[nbg])
        lowsb = [[P.alloc(512, BF16, f'low{z}{j}', parts=16) for j in range(2)] for z in range(2)]
    ht = [P.alloc(8 * 512, BF16, f'ght{j}') for j in range(2)]
    stb = [P.alloc(512, BF16, f'stb{j}') for j in range(4)]
    stf = [P.alloc(512, F32, f'stf{j}') for j in range(4)]
    tf = [P.alloc(512, F32, f'tf{j}') for j in range(4)]
    vst = [P.alloc(1024, BF16, f'vst{j}', parts=64) for j in range(2)]
    cnt = {'b': 0, 'f': 0, 't': 0, 'v': 0}

    def nxt(lst, key):
        b_ = lst[cnt[key] % len(lst)]
        cnt[key] += 1
        return b_

    for tix, (t0, w) in enumerate(TILES512):
        H = ht[tix % 2]
        hv = H.ap.rearrange("p (c t) -> p c t", c=8)
        P.dma('sync', hv[:, :, 0:w], k.hT_p[:, :, t0:t0 + w], reads=k.hTb, writes=[H])
        sl = slice(t0, t0 + w)

        def proj(col0, M=128, H=H, w=w):
            ps = P.next_psum()
            for kc in range(8):
                P.op('tensor', lambda e, ps=ps, kc=kc, col0=col0, M=M, H=H, w=w: e.matmul(
                    ps.ap[0:M, 0:w], Wb_k[kc].ap[:, col0:col0 + M], H.ap[:, kc * 512:kc * 512 + w],
                    start=(kc == 0), stop=(kc == 7)), reads=[Wb_k[kc], H], writes=[ps])
            return ps

        def act_store(ps, func, scale, dst_ap, dst_buf, w=w):
            sb = nxt(stb, 'b')
            P.op(A, lambda e, ps=ps, sb=sb, func=func, scale=scale, w=w: e.activation(
                sb.ap[:, 0:w], ps.ap[:, 0:w], func, scale=scale), reads=[ps], writes=[sb])
            P.dma('sync', dst_ap, sb.ap[:, 0:w], reads=[sb], writes=[dst_buf])

        if hg:
            for h in range(8):
                act_store(proj(h * 128), AF.Silu, 1.0, k.pq_d[h, :, sl], k.pqb[h])
                act_store(proj(4096 + h * 128), AF.Silu, 1.0, k.pgate_d[h, :, sl], k.pgateb[h])
                for d, zoff in ((0, 2048), (1, 3072)):
                    ps = proj(zoff + h * 128)
                    a1, a2, sb, sf = nxt(tf, 't'), nxt(tf, 't'), nxt(stb, 'b'), nxt(stf, 'f')
                    P.op(A, lambda e, ps=ps, a1=a1, w=w: e.activation(a1.ap[:, 0:w], ps.ap[:, 0:w], AF.Sigmoid, scale=-1.0),
                         reads=[ps], writes=[a1])
                    P.op(V, lambda e, a1=a1, a2=a2, w=w, d=d, h=h: e.tensor_scalar(
                        a2.ap[:, 0:w], a1.ap[:, 0:w], oml.ap[:, d * 8 + h:d * 8 + h + 1], None, ALU.mult),
                        reads=[a1, oml], writes=[a2])
                    P.op(G, lambda e, a2=a2, sb=sb, w=w: e.tensor_copy(sb.ap[:, 0:w], a2.ap[:, 0:w]),
                         reads=[a2], writes=[sb])
                    P.dma('sync', k.pk_d[d, h, :, sl], sb.ap[:, 0:w], reads=[sb], writes=[k.pkb[d][h]])
                    P.op(A, lambda e, a2=a2, sf=sf, w=w: e.activation(
                        sf.ap[:, 0:w], a2.ap[:, 0:w], AF.Ln, bias=k.onec.ap[:, 0:1], scale=-1.0),
                        reads=[a2, k.onec], writes=[sf])
                    P.dma('sync', k.pg_d[d, h, :, sl], sf.ap[:, 0:w], reads=[sf], writes=[k.pgb[d][h]])
        else:
            for h in range(4):
                act_store(proj(h * 128), AF.Identity, 128.0 ** -0.5, k.pq_d[h, :, sl], k.pqb[h])
                act_store(proj(512 + h * 128), AF.Identity, 1.0, k.pk_d[0, h, :, sl], k.pkb[0][h])
            for b_ in range(8):
                act_store(proj(2048 + b_ * 128), AF.Silu, 1.0, k.pgate_d[b_, :, sl], k.pgateb[b_])
            for z in range(2):
                ps = proj(3072 + z * 16, M=16)
                lw = lowsb[z][tix % 2]
                P.op(A, lambda e, ps=ps, lw=lw, w=w: e.copy(lw.ap[:, 0:w], ps.ap[0:16, 0:w]), reads=[ps], writes=[lw])
                for h in range(4):
                    pg = P.next_psum()
                    P.op('tensor', lambda e, pg=pg, lw=lw, z=z, h=h, w=w: e.matmul(
                        pg.ap[:, 0:w], wg.ap[:, z * 512 + h * 128: z * 512 + (h + 1) * 128], lw.ap[:, 0:w],
                        start=True, stop=True), reads=[wg, lw], writes=[pg])
                    a1, a2, sf = nxt(tf, 't'), nxt(tf, 't'), nxt(stf, 'f')
                    P.op(A, lambda e, pg=pg, a1=a1, z=z, h=h, w=w: e.activation(
                        a1.ap[:, 0:w], pg.ap[:, 0:w], AF.Exp, bias=nbg.ap[:, z * 4 + h:z * 4 + h + 1], scale=-1.0),
                        reads=[pg, nbg], writes=[a1])
                    P.op(A, lambda e, a1=a1, a2=a2, w=w: e.activation(
                        a2.ap[:, 0:w], a1.ap[:, 0:w], AF.Ln, bias=k.onec.ap[:, 0:1], scale=1.0),
                        reads=[a1, k.onec], writes=[a2])
                    P.op(G, lambda e, a2=a2, sf=sf, w=w: e.tensor_scalar(
                        sf.ap[:, 0:w], a2.ap[:, 0:w], -1.0 / 16.0, None, ALU.mult), reads=[a2], writes=[sf])
                    P.dma('sync', k.pg_d[z, h, :, sl], sf.ap[:, 0:w], reads=[sf], writes=[k.pgb[z][h]])
        for ci in range(w // 64):
            vs_ = nxt(vst, 'v')
            for half in range(2):
                ps = P.next_psum()
                for kc in range(8):
                    P.op('tensor', lambda e, ps=ps, kc=kc, ci=ci, half=half, H=H: e.matmul(
                        ps.ap[0:64, 0:512], H.ap[:, kc * 512 + ci * 64: kc * 512 + ci * 64 + 64],
                        Wb_k[kc].ap[:, 1024 + half * 512: 1024 + (half + 1) * 512],
                        start=(kc == 0), stop=(kc == 7)), reads=[Wb_k[kc], H], writes=[ps])
                if half == 0:
                    P.op(V, lambda e, ps=ps, vs_=vs_: e.tensor_copy(vs_.ap[:, 0:512], ps.ap[0:64, 0:512]),
                         reads=[ps], writes=[vs_])
                else:
                    P.op(G if False else A, lambda e, ps=ps, vs_=vs_: e.copy(vs_.ap[:, 512:1024], ps.ap[0:64, 0:512]),
                         reads=[ps], writes=[vs_])
            P.dma('sync', k.pv_d[t0 // 64 + ci], vs_.ap, reads=[vs_], writes=[k.pvb])
    P.release(mk)
    P.barrier()

    mk = P.mark()
    nw = P.alloc(VT, F32, 'nw')
    slow_dma(P, nw.ap, (I['hgrn_norm_w'] if hg else I['gla_norm_w'])[0].rearrange("(v p) -> p v", p=128), writes=[nw])
    rmask = P.alloc(L, F32, 'rmask')
    P.op(G, lambda e: e.memset(rmask.ap, 1.0), writes=[rmask])
    P.op(G, lambda e: e.memset(rmask.ap.rearrange("p (m j) -> p m j", j=64)[:, :, 0:1], 0.0), writes=[rmask])
    qn = P.alloc(LT, BF16, 'qn')
    kn = P.alloc(LT, BF16, 'kn')
    bufA = P.alloc(LT, F32, 'bufA')
    bufB = P.alloc(LT, F32, 'bufB')
    qt = P.alloc(LT, BF16, 'qt')
    ktn = P.alloc(LT, BF16, 'ktn')
    kdn = P.alloc(LT, BF16, 'kdn')
    vsb = P.alloc(NCH * DV, BF16, 'vsb', parts=64)
    oT = [P.alloc(LT, F32, f'oT{v}') for v in range(VT)]
    S32 = [P.alloc(DV, F32, f'S32_{j}') for j in range(2)]
    Sbf = [P.alloc(DV, BF16, f'Sbf{j}') for j in range(2)]
    scs = [P.alloc(64, BF16, f'scs{j}', parts=64) for j in range(2)]
    kdT = [P.alloc(128, BF16, f'kdT{j}', parts=64) for j in range(2)]
    gsb = [P.alloc(512, BF16, f'gsb{j}') for j in range(2)]
    sqb = [P.alloc(512, BF16, f'sqb{j}') for j in range(2)]
    rs = [P.alloc(512, F32, f'rs{j}') for j in range(2)]
    o1 = [P.alloc(512, F32, f'o1{j}') for j in range(2)]
    ogs = [P.alloc(512, BF16, f'ogs{j}') for j in range(2)]
    segs = [(0, LC), (LC, L)]

    for h in range(NH):
        P.dma('sync', qn.ap, k.pq_d[h], reads=[k.pqb[h]], writes=[qn])
        P.dma('sync', vsb.ap.rearrange("j (n c) -> j n c", c=DV),
              k.pv_d.rearrange("n j c -> j n c")[:, :, h * DV:(h + 1) * DV], reads=[k.pvb], writes=[vsb])
        for d in range(2):
            kd = d if hg else 0
            if hg or d == 0:
                P.dma('sync', kn.ap, k.pk_d[kd, h], reads=[k.pkb[kd][h]], writes=[kn])
            if d == 0:
                P.dma('sync', bufB.ap, k.pg_d[d, h], reads=[k.pgb[d][h]], writes=[bufB])
            else:
                P.dma('sync', bufA.ap, k.pg_d[d, h], reads=[k.pgb[d][h]], writes=[bufA])
                for (s0, sw) in segs:
                    P.op(V, lambda e, s0=s0, sw=sw: e.tensor_copy(bufB.ap[:, s0:s0 + sw], tv(bufA.ap, 1, s0, sw)),
                         reads=[bufA], writes=[bufB])
            for (s0, sw) in segs:
                P.op(V, lambda e, s0=s0, sw=sw: e.tensor_tensor_scan(
                    bufA.ap[:, s0:s0 + sw], rmask.ap[:, 0:sw], bufB.ap[:, s0:s0 + sw], 0.0, ALU.mult, ALU.add),
                    reads=[bufB, rmask], writes=[bufA])
            P.op(A, lambda e: e.activation(bufB.ap, bufA.ap, AF.Exp, scale=-1.0), reads=[bufA], writes=[bufB])
            P.op(A, lambda e: e.activation(bufA.ap, bufA.ap, AF.Exp), reads=[bufA], writes=[bufA])
            for (s0, sw) in segs:
                P.op(V, lambda e, s0=s0, sw=sw, d=d: e.tensor_tensor(
                    qt.ap[:, s0:s0 + sw], tv(qn.ap, d, s0, sw), bufA.ap[:, s0:s0 + sw], ALU.mult),
                    reads=[qn, bufA], writes=[qt])
                P.op(V, lambda e, s0=s0, sw=sw, d=d: e.tensor_tensor(
                    tv(ktn.ap, d, s0, sw), tv(kn.ap, d, s0, sw), bufB.ap[:, s0:s0 + sw], ALU.mult),
                    reads=[kn, bufB], writes=[ktn])
                nm = sw // 64
                P.op(V, lambda e, s0=s0, sw=sw, d=d, nm=nm: e.tensor_tensor(
                    tv(kdn.ap, d, s0, sw).rearrange("p (m j) -> p m j", j=64),
                    tv(ktn.ap, d, s0, sw).rearrange("p (m j) -> p m j", j=64),
                    bufA.ap[:, s0:s0 + sw].rearrange("p (m j) -> p m j", j=64)[:, :, 63:64].broadcast_to([128, nm, 64]),
                    ALU.mult), reads=[ktn, bufA], writes=[kdn])
            P.op(G, lambda e: e.memset(S32[0].ap, 0.0), writes=[S32[0]])
            P.op(G, lambda e: e.memset(Sbf[0].ap, 0.0), writes=[Sbf[0]])
            for m in range(NCH):
                tau0 = m * 64
                n0, n1, rev = nat_range(d, tau0, 64)
                nn = n0 // 64
                jb = m % 2
                Sp32, Sn32, Spb, Snb = S32[m % 2], S32[(m + 1) % 2], Sbf[m % 2], Sbf[(m + 1) % 2]
                psS = P.next_psum()
                P.op('tensor', lambda e, psS=psS, n0=n0, n1=n1, tau0=tau0: e.matmul(
                    psS.ap[0:64, 0:64], ktn.ap[:, n0:n1], qt.ap[:, tau0:tau0 + 64], start=True, stop=True),
                    reads=[ktn, qt], writes=[psS])
                sc = scs[jb]
                P.op(V, lambda e, psS=psS, sc=sc, d=d: e.tensor_tensor(
                    sc.ap, psS.ap[0:64, 0:64], k.mask.ap[:, d * 64:(d + 1) * 64], ALU.mult),
                    reads=[psS, k.mask], writes=[sc])
                psK = P.next_psum()
                P.op('tensor', lambda e, psK=psK, n0=n0, n1=n1: e.matmul(
                    psK.ap[0:64, 0:128], kdn.ap[:, n0:n1], k.ident_b.ap, start=True, stop=True),
                    reads=[kdn, k.ident_b], writes=[psK])
                kt_ = kdT[jb]
                P.op(A, lambda e, psK=psK, kt_=kt_: e.copy(kt_.ap, psK.ap[0:64, 0:128]), reads=[psK], writes=[kt_])
                for v in range(VT):
                    psO = P.next_psum()
                    P.op('tensor', lambda e, psO=psO, nn=nn, v=v, sc=sc: e.matmul(
                        psO.ap[:, 0:64], vsb.ap[:, nn * DV + v * 128: nn * DV + (v + 1) * 128], sc.ap,
                        start=True, stop=False), reads=[vsb, sc], writes=[psO])
                    P.op('tensor', lambda e, psO=psO, v=v, Spb=Spb, tau0=tau0: e.matmul(
                        psO.ap[:, 0:64], Spb.ap[:, v * 128:(v + 1) * 128], qt.ap[:, tau0:tau0 + 64],
                        start=False, stop=True), reads=[Spb, qt], writes=[psO])
                    ov = oT[v].ap[:, n0:n1]
                    ov = ov[:, ::-1] if rev else ov
                    if d == 0:
                        P.op(G if False else A, lambda e, psO=psO, ov=ov: e.copy(ov, psO.ap[:, 0:64]),
                             reads=[psO], writes=[oT[v]])
                    else:
                        P.op(V, lambda e, psO=psO, ov=ov: e.tensor_tensor(ov, psO.ap[:, 0:64], ov, ALU.add),
                             reads=[psO, oT[v]], writes=[oT[v]])
                psV = P.next_psum()
                P.op('tensor', lambda e, psV=psV, kt_=kt_, nn=nn: e.matmul(
                    psV.ap[:, 0:DV], kt_.ap, vsb.ap[:, nn * DV:(nn + 1) * DV], start=True, stop=True),
                    reads=[kt_, vsb], writes=[psV])
                P.op(V, lambda e, psV=psV, Sp32=Sp32, Sn32=Sn32, tau0=tau0: e.scalar_tensor_tensor(
                    Sn32.ap, Sp32.ap, bufA.ap[:, tau0 + 63:tau0 + 64], psV.ap[:, 0:DV], ALU.mult, ALU.add),
                    reads=[psV, Sp32, bufA], writes=[Sn32])
                P.op(G, lambda e, Sn32=Sn32, Snb=Snb: e.tensor_copy(Snb.ap, Sn32.ap), reads=[Sn32], writes=[Snb])
        for tix, (t0, w) in enumerate(TILES512):
            jb = tix % 2
            pss = P.next_psum()
            for v in range(VT):
                sq_ = sqb[(tix * VT + v) % 2]
                P.op(A, lambda e, sq_=sq_, v=v, t0=t0, w=w: e.activation(sq_.ap[:, 0:w], oT[v].ap[:, t0:t0 + w], AF.Square),
                     reads=[oT[v]], writes=[sq_])
                P.op('tensor', lambda e, pss=pss, sq_=sq_, v=v, w=w: e.matmul(
                    pss.ap[:, 0:w], k.ones_b.ap, sq_.ap[:, 0:w], start=(v == 0), stop=(v == VT - 1)),
                    reads=[sq_, k.ones_b], writes=[pss])
            r_ = rs[jb]
            P.op(A, lambda e, pss=pss, r_=r_, w=w: e.activation(r_.ap[:, 0:w], pss.ap[:, 0:w], AF.Sqrt, bias=EPS, scale=1.0 / DV),
                 reads=[pss], writes=[r_])
            P.op(V, lambda e, r_=r_, w=w: e.reciprocal(r_.ap[:, 0:w], r_.ap[:, 0:w]), reads=[r_], writes=[r_])
            for v in range(VT):
                blk = h * VT + v
                g_, o_, og_ = gsb[(tix * VT + v) % 2], o1[(tix * VT + v) % 2], ogs[(tix * VT + v) % 2]
                P.dma('sync', g_.ap[:, 0:w], k.pgate_d[blk, :, t0:t0 + w], reads=[k.pgateb[blk]], writes=[g_])
                P.op(V, lambda e, o_=o_, v=v, t0=t0, w=w, r_=r_: e.scalar_tensor_tensor(
                    o_.ap[:, 0:w], oT[v].ap[:, t0:t0 + w], nw.ap[:, v:v + 1], r_.ap[:, 0:w], ALU.mult, ALU.mult),
                    reads=[oT[v], nw, r_], writes=[o_])
                P.op(G, lambda e, o_=o_, g_=g_, og_=og_, w=w: e.tensor_tensor(
                    og_.ap[:, 0:w], o_.ap[:, 0:w], g_.ap[:, 0:w], ALU.mult), reads=[o_, g_], writes=[og_])
                P.dma('sync', k.yg_d[blk, :, t0:t0 + w], og_.ap[:, 0:w], reads=[og_], writes=[k.ogb[blk]])
    P.release(mk)
    P.barrier()

    mk = P.mark()
    Wo = P.alloc(8 * 1024, BF16, 'Wo')
    Wo_k = [sub(Wo, Wo.ap[:, kc * 1024:(kc + 1) * 1024]) for kc in range(8)]
    W_out = I['hgrn_w_out'][0] if hg else I['gla_w_out'][0]
    for kc in range(8):
        P.dma('gpsimd', Wo_k[kc].ap, W_out[kc * 128:(kc + 1) * 128, :], writes=[Wo_k[kc]])
    og = P.alloc(8 * LT, BF16, 'og_all')
    og_k = [sub(og, og.ap[:, kc * LT:(kc + 1) * LT]) for kc in range(8)]
    for kc in range(8):
        P.dma('sync', og_k[kc].ap, k.yg_d[kc], reads=[k.ogb[kc]], writes=[og_k[kc]])
    xc = [P.alloc(LT, F32, f'xc{j}') for j in range(2)]
    for c in range(8):
        X = xc[c % 2]
        P.dma('sync', X.ap, k.xT[c], reads=k.xTb, writes=[X])
        for tix, (t0, w) in enumerate(TILES512):
            s = 1 if tix == 0 else 0
            ps = P.next_psum()
            for kc in range(8):
                P.op('tensor', lambda e, ps=ps, kc=kc, c=c, t0=t0, w=w: e.matmul(
                    ps.ap[:, 0:w], Wo_k[kc].ap[:, c * 128:(c + 1) * 128], og_k[kc].ap[:, t0:t0 + w],
                    start=(kc == 0), stop=(kc == 7)), reads=[Wo_k[kc], og_k[kc]], writes=[ps])
            if hg or tix == 0:
                xv = X.ap[:, t0:t0 + w]
                pv_ = ps.ap[:, 0:w]
            else:
                cc0 = (t0 - LC) // 64
                xv = X.ap[:, LC:LT].rearrange("p (r cc) -> p cc r", cc=64)[:, cc0:cc0 + 8, :]
                pv_ = ps.ap[:, 0:w].rearrange("p (cc r) -> p cc r", r=64)
            P.op(V, lambda e, pv_=pv_, xv=xv, c=c, s=s: e.scalar_tensor_tensor(
                xv, pv_, HG_ap(k, i, 1, s, c), xv, ALU.mult, ALU.add), reads=[ps, X, k.HG], writes=[X])
        P.dma('sync', k.xT[c], X.ap, reads=[X], writes=k.xTb)
    P.release(mk)


def make_consts():
    ident = np.eye(128, dtype=np.float32)
    iota = np.broadcast_to(np.arange(LT, dtype=np.float32)[None, :], (128, LT)).copy()
    jj = np.arange(64)[:, None]
    ii = np.arange(64)[None, :]
    mask = np.stack([(jj <= ii), (jj >= 63 - ii)]).astype(np.float32)
    return {'k_ident': ident, 'k_iota': iota, 'k_mask': mask}


def make_in_maps(inputs, cores):
    consts = make_consts()
    shared = {}
    for n, s in INPUT_SHAPES.items():
        if n in ('x', 'c', 'ctx') or n.startswith('k_'):
            continue
        shared[n] = np.ascontiguousarray(np.asarray(inputs[n], dtype=np.float32).reshape(s))
    maps = []
    for b in cores:
        m = dict(shared)
        m.update(consts)
        m['x'] = np.ascontiguousarray(np.asarray(inputs['x'][b], dtype=np.float32))
        m['ctx'] = np.ascontiguousarray(np.asarray(inputs['ctx'][b], dtype=np.float32))
        m['c'] = np.ascontiguousarray(np.asarray(inputs['c'][b], dtype=np.float32).reshape(1, D))
        maps.append(m)
    return maps


_NC_CACHE = {}


def kernel(**inputs):
    if 'full' not in _NC_CACHE:
        _NC_CACHE['full'] = build_nc()
    nc = _NC_CACHE['full']
    maps = make_in_maps(inputs, list(range(8)))
    res = run_bass_kernel_spmd(nc, maps, core_ids=list(range(8)))
    return np.stack([np.asarray(r["out"], dtype=np.float32) for r in res.results], axis=0)
```

```python
import math
from contextlib import ExitStack

import numpy as np
import concourse.bass as bass
import concourse.mybir as mybir
from concourse.bass_utils import run_bass_kernel_spmd

F32 = mybir.dt.float32
BF16 = mybir.dt.bfloat16
ALU = mybir.AluOpType
AF = mybir.ActivationFunctionType

ENG = ['tensor', 'vector', 'scalar', 'gpsimd', 'sync']
NDMA = 24
SAME_ENG_WINDOW = 10 ** 9

D = 1024
L = 4096
LC = 256
LT = L + LC
DFF = 2816
NKC = 8
NFC = 22
TT = 256
NTT = LT // TT
EPS = 1e-6
PI = math.pi
TWO_PI = 2.0 * math.pi


class Buf:
    def __init__(self, ap, name=''):
        self.ap = ap
        self.name = name
        self.w = None
        self.r = {}


class Prog:
    def __init__(self, nc, stack, arena_cols_f32=47 * 1024):
        self.nc = nc
        self.stack = stack
        self.q = {e: [] for e in ENG}
        self.cnt = {e: 0 for e in ENG}
        self.epoch = 0
        self.sems = {}
        self.seen = {e: {} for e in ENG}
        self.dma_sems = [stack.enter_context(nc.semaphore(f"dma{j}")) for j in range(NDMA)]
        self.dma_cnt = [0] * NDMA
        self.dma_pool = {'sync': list(range(0, NDMA // 2)), 'gpsimd': list(range(NDMA // 2, NDMA))}
        self.dma_rr = {'sync': 0, 'gpsimd': 0}
        self._new_epoch_sems()
        self.arena = stack.enter_context(nc.sbuf_tensor("arena", [128, arena_cols_f32], F32))
        self.arena_cols = arena_cols_f32
        self.bump = 0
        self.psum = [Buf(stack.enter_context(nc.psum_tensor(f"ps{i}", [128, 512], F32))[:, :], f"ps{i}")
                     for i in range(8)]
        self.ps_rr = 0
        self.n_instr = 0

    def alloc(self, cols, dtype=F32, name='', parts=128):
        nbytes = cols * (4 if dtype == F32 else 2)
        n32 = (nbytes + 3) // 4
        n32 = (n32 + 7) // 8 * 8
        assert self.bump + n32 <= self.arena_cols, f"SBUF arena overflow at {name}: {self.bump}+{n32}"
        v = self.arena[:, self.bump:self.bump + n32]
        self.bump += n32
        if dtype != F32:
            v = v.bitcast(dtype)
        v = v[0:parts, 0:cols]
        return Buf(v, name)

    def mark(self):
        return self.bump

    def release(self, mark):
        self.bump = mark

    def next_psum(self):
        b = self.psum[self.ps_rr]
        self.ps_rr = (self.ps_rr + 1) % 8
        return b

    def _new_epoch_sems(self):
        if not hasattr(self, 'semset'):
            self.semset = [{e: self.stack.enter_context(self.nc.semaphore(f"s_{e}_{j}")) for e in ENG}
                           for j in range(3)]
        for e in ENG:
            self.sems[(e, self.epoch)] = self.semset[self.epoch % 3][e]
        if self.epoch >= 2:
            for e in ENG:
                self.q[e].append(('clear', self.semset[(self.epoch + 1) % 3][e]))

    def _need(self, eng, ev, waits, raw):
        if ev is None:
            return
        if ev[0] == 'e':
            _, f, ep, k = ev
            if ep != self.epoch:
                return
            if f == eng:
                if eng == 'tensor':
                    return
                if k <= self.cnt[eng] - SAME_ENG_WINDOW:
                    return
            key = ('e', f, ep)
        else:
            _, j, k = ev
            key = ('d', j)
        if self.seen[eng].get(key, 0) >= k:
            return
        waits[key] = max(waits.get(key, 0), k)

    def _emit_waits(self, eng, waits):
        for key, k in waits.items():
            self.seen[eng][key] = k
            sem = self.sems[(key[1], key[2])] if key[0] == 'e' else self.dma_sems[key[1]]
            self.q[eng].append(('wait', sem, k))

    def _deps(self, eng, reads, writes):
        waits = {}
        for b in reads:
            self._need(eng, b.w, waits, True)
        for b in writes:
            self._need(eng, b.w, waits, False)
            for ev in b.r.values():
                self._need(eng, ev, waits, False)
        self._emit_waits(eng, waits)

    def _commit(self, ev, rkey, reads, writes):
        for b in writes:
            b.w = ev
            b.r = {}
        for b in reads:
            if b in writes:
                continue
            b.r[rkey] = ev

    def op(self, eng, fn, reads=(), writes=()):
        self._deps(eng, reads, writes)
        self.cnt[eng] += 1
        self.q[eng].append(('op', fn, self.sems[(eng, self.epoch)]))
        ev = ('e', eng, self.epoch, self.cnt[eng])
        self._commit(ev, ('e', eng), reads, writes)
        self.n_instr += 1
        return ev

    def dma(self, eng, out_ap, in_ap, reads=(), writes=(), **kw):
        self._deps(eng, reads, writes)
        pool = self.dma_pool[eng]
        j = pool[self.dma_rr[eng]]
        self.dma_rr[eng] = (self.dma_rr[eng] + 1) % len(pool)
        w = {}
        if self.dma_cnt[j] > 0:
            self._need(eng, ('d', j, self.dma_cnt[j]), w, True)
            self._emit_waits(eng, w)
        self.dma_cnt[j] += 16
        self.q[eng].append(('dma', out_ap, in_ap, kw, self.dma_sems[j]))
        ev = ('d', j, self.dma_cnt[j])
        self._commit(ev, ('d', j), reads, writes)
        self.n_instr += 1
        return ev

    def barrier(self):
        for e in ENG:
            waits = {}
            for f in ENG:
                if f != e and self.cnt[f] > 0:
                    ev = ('e', f, self.epoch, self.cnt[f])
                    self._need(e, ev, waits, True)
            for j in range(NDMA):
                if self.dma_cnt[j] > 0:
                    self._need(e, ('d', j, self.dma_cnt[j]), waits, True)
            self._emit_waits(e, waits)
        self.epoch += 1
        self._new_epoch_sems()
        for e in ENG:
            self.cnt[e] = 0

    def final_wait(self, eng='sync'):
        waits = {}
        for f in ENG:
            if f != eng and self.cnt[f] > 0:
                self._need(eng, ('e', f, self.epoch, self.cnt[f]), waits, True)
        for j in range(NDMA):
            if self.dma_cnt[j] > 0:
                self._need(eng, ('d', j, self.dma_cnt[j]), waits, True)
        self._emit_waits(eng, waits)

    def emit(self):
        nc = self.nc
        with nc.Block() as block:
            def run(engname):
                def body(e):
                    for item in self.q[engname]:
                        if item[0] == 'wait':
                            e.wait_ge(item[1], item[2])
                        elif item[0] == 'clear':
                            e.sem_clear(item[1])
                        elif item[0] == 'op':
                            item[1](e).then_inc(item[2], 1)
                        else:
                            _, o, i, kw, sem = item
                            e.dma_start(out=o, in_=i, **kw).then_inc(sem, 16)
                return body
            block.tensor(run('tensor'))
            block.vector(run('vector'))
            block.scalar(run('scalar'))
            block.gpsimd(run('gpsimd'))
            block.sync(run('sync'))


def sub(buf, ap, name=''):
    return Buf(ap, name or buf.name)


INPUT_SHAPES = {
    'x': [L, D], 'c': [1, D], 'ctx': [LC, D], 'c_ctx': [1, D],
    'ada_w': [4, D, 9 * D], 'ada_b': [4, 9 * D], 'norm_w': [4, 3, D],
    'ffn_w_in': [4, 2, D, 2 * DFF], 'ffn_w_out': [4, 2, DFF, D],
    's5_a_re': [2, 2, 64, 64], 's5_a_im': [2, 2, 64, 64], 's5_log_step': [2, 2, 64],
    's5_b_re': [2, 2, 64, 64, 16], 's5_b_im': [2, 2, 64, 64, 16],
    's5_c_re': [2, 2, 64, 16, 64], 's5_c_im': [2, 2, 64, 16, 64],
    's5_d': [2, D], 's5_w_glu': [2, D, 2 * D], 's5_b_glu': [2, 2 * D],
    'gla_w_in': [1, D, 3104], 'gla_w_gate': [1, 2, 16, 512], 'gla_b_gate': [1, 2, 512],
    'gla_norm_w': [1, 256], 'gla_w_out': [1, D, D],
    'hgrn_w_in': [1, D, 5 * D], 'hgrn_lb_logits': [4, 2, D], 'hgrn_norm_w': [1, 128],
    'hgrn_w_out': [1, D, D], 'final_norm_w': [1, D],
    'k_ident': [128, 128], 'k_iota': [128, LT], 'k_mask': [2, 64, 64],
}


class K:
    pass


def mod_col(i, m, c, s):
    return ((i * 9 + m) * 8 + c) * 2 + s


def build_nc(n_layers=4, mixers=True, dump_xT=False, layer_list=None):
    nc = bass.Bass("TRN2", target_bir_lowering=False)
    I = {n: nc.dram_tensor(n, list(s), F32, kind="ExternalInput").ap() for n, s in INPUT_SHAPES.items()}
    out = nc.dram_tensor("out", [L, D], F32, kind="ExternalOutput").ap()
    xT = nc.dram_tensor("xT_scr", [8, 128, LT], F32, kind="Internal").ap()
    hT_d = nc.dram_tensor("hT_scr", [8, 128, LT], BF16, kind="Internal").ap()
    yg_d = nc.dram_tensor("yg_scr", [8, 128, LT], BF16, kind="Internal").ap()
    pq_d = nc.dram_tensor("pq_scr", [8, 128, LT], BF16, kind="Internal").ap()
    pk_d = nc.dram_tensor("pk_scr", [2, 8, 128, LT], BF16, kind="Internal").ap()
    pg_d = nc.dram_tensor("pg_scr", [2, 8, 128, LT], F32, kind="Internal").ap()
    pgate_d = nc.dram_tensor("pgate_scr", [8, 128, LT], BF16, kind="Internal").ap()
    pv_d = nc.dram_tensor("pv_scr", [LT // 64, 64, 1024], BF16, kind="Internal").ap()
    if dump_xT:
        xdump = nc.dram_tensor("xdump", [8, 128, LT], F32, kind="ExternalOutput").ap()
        dbg = nc.dram_tensor("dbg", [128, 1024], F32, kind="ExternalOutput").ap()
        dbg2 = nc.dram_tensor("dbg2", [32, 128, 512], F32, kind="ExternalOutput").ap()

    with ExitStack() as st:
        P = Prog(nc, st)
        k = K()
        k.P, k.I, k.xT, k.hT_d, k.yg_d = P, I, xT, hT_d, yg_d
        k.dbg2 = dbg2 if dump_xT else None
        k.pq_d, k.pk_d, k.pg_d, k.pgate_d, k.pv_d = pq_d, pk_d, pg_d, pgate_d, pv_d
        k.pqb = [Buf(None) for _ in range(8)]
        k.pkb = [[Buf(None) for _ in range(8)] for _ in range(2)]
        k.pgb = [[Buf(None) for _ in range(8)] for _ in range(2)]
        k.pgateb = [Buf(None) for _ in range(8)]
        k.pvb = Buf(None)
        k.ogb = [Buf(None) for _ in range(8)]
        k.mask = P.alloc(128, F32, 'mask', parts=64)
        P.dma('sync', k.mask.ap.rearrange("p (d i) -> p d i", d=2), I['k_mask'].rearrange("d j i -> j d i"),
              writes=[k.mask])
        k.onec = P.alloc(1, F32, 'onec')
        P.op('gpsimd', lambda e: e.memset(k.onec.ap, 1.0), writes=[k.onec])
        k.dbg_n = 0
        k.xT_p = xT.rearrange("c p t -> p c t")
        k.hT_p = hT_d.rearrange("c p t -> p c t")
        k.yg_p = yg_d.rearrange("c p t -> p c t")
        k.xTb = [Buf(None, f"xT{t}") for t in range(NTT)]
        k.hTb = [Buf(None, f"hT{t}") for t in range(NTT)]
        k.ygb = [Buf(None, f"yg{t}") for t in range(NTT)]

        k.ident_f = P.alloc(128, F32, 'ident_f')
        k.ident_b = P.alloc(128, BF16, 'ident_b')
        k.ones_b = P.alloc(128, BF16, 'ones_b')
        k.modT = P.alloc(4 * 9 * 8 * 2, F32, 'modT')
        k.WS = P.alloc(4 * 3 * 2 * 8, F32, 'WS')
        k.HG = P.alloc(4 * 3 * 2 * 8, F32, 'HG')
        k.normT = P.alloc(4 * 3 * 8, F32, 'normT')
        k.fnT = P.alloc(8, F32, 'fnT')
        P.dma('sync', k.ident_f.ap, I['k_ident'], writes=[k.ident_f])
        P.op('vector', lambda e: e.tensor_copy(k.ident_b.ap, k.ident_f.ap), reads=[k.ident_f], writes=[k.ident_b])
        P.op('gpsimd', lambda e: e.memset(k.ones_b.ap, 1.0), writes=[k.ones_b])

        prologue(k)
        P.barrier()
        for i in (layer_list if layer_list is not None else range(n_layers)):
            ffn_phase(k, i, 0)
            P.barrier()
            if mixers:
                kind = i % 3
                mixer_prep(k, i, permute=(kind == 1))
                P.barrier()
                if kind == 0:
                    s5_phase(k, i)
                elif kind == 1:
                    gated_phase(k, i, 'gla')
                else:
                    gated_phase(k, i, 'hgrn')
                P.barrier()
            ffn_phase(k, i, 1)
            P.barrier()
        epilogue(k, out)
        if dump_xT:
            P.barrier()
            mk = P.mark()
            t = P.alloc(8 * 512, F32, 'dump')
            for n in range(0, LT, 512):
                w = min(512, LT - n)
                P.dma('sync', t.ap.rearrange("p (c t) -> p c t", c=8)[:, :, 0:w], k.xT_p[:, :, n:n + w], writes=[t])
                P.dma('sync', xdump.rearrange("c p t -> p c t")[:, :, n:n + w],
                      t.ap.rearrange("p (c t) -> p c t", c=8)[:, :, 0:w], reads=[t])
            P.release(mk)
            P.dma('sync', dbg[:, 0:576], k.modT.ap, reads=[k.modT])
            P.dma('sync', dbg[:, 576:768], k.WS.ap, reads=[k.WS])
            P.dma('sync', dbg[:, 768:960], k.HG.ap, reads=[k.HG])
        P.final_wait('sync')
        P.emit()
        k.n_instr = P.n_instr
    return nc


def dbg_dump(k, buf, ap=None, label=''):
    if k.dbg2 is None or k.dbg_n >= 32:
        return
    ap = buf.ap if ap is None else ap
    pp, cc = ap.shape[0], ap.shape[1]
    print('DBG slot', k.dbg_n, label, pp, cc)
    k.P.dma('gpsimd', k.dbg2[k.dbg_n, 0:pp, 0:cc], ap, reads=[buf])
    k.dbg_n += 1


def slow_dma(P, out_ap, in_ap, **kw):
    return P.dma('sync', out_ap, in_ap, allow_slow_non_contiguous=True, **kw)


def prologue(k):
    P, I = k.P, k.I
    mk = P.mark()
    adabT = P.alloc(4 * 72, F32, 'adabT')
    for i in range(4):
        slow_dma(P, adabT.ap[:, i * 72:(i + 1) * 72], I['ada_b'][i].rearrange("(m p) -> p m", p=128), writes=[adabT])
    slow_dma(P, k.normT.ap, I['norm_w'].rearrange("i j (c p) -> p (i j c)", p=128), writes=[k.normT])
    slow_dma(P, k.fnT.ap, I['final_norm_w'].rearrange("o (c p) -> p (o c)", p=128), writes=[k.fnT])
    cs32 = P.alloc(16, F32, 'cs32')
    csv = cs32.ap.rearrange("p (k s) -> p k s", s=2)
    slow_dma(P, csv[:, :, 0], I['c'].rearrange("o (k p) -> p (o k)", p=128), writes=[cs32])
    slow_dma(P, csv[:, :, 1], I['c_ctx'].rearrange("o (k p) -> p (o k)", p=128), writes=[cs32])
    csb = P.alloc(16, BF16, 'csb')
    P.op('scalar', lambda e: e.activation(csb.ap, cs32.ap, AF.Silu), reads=[cs32], writes=[csb])

    Wa = [P.alloc(8 * 1024, BF16, f'Wa{j}') for j in range(2)]
    n = 0
    for i in range(4):
        for m in range(9):
            W = Wa[n % 2]
            n += 1
            P.dma('gpsimd', W.ap.rearrange("p (k n) -> p k n", k=8),
                  I['ada_w'][i].rearrange("(k p) n -> p k n", p=128)[:, :, m * 1024:(m + 1) * 1024], writes=[W])
            ps = P.next_psum()
            for oc in range(8):
                for kc in range(8):
                    P.op('tensor', lambda e, W=W, ps=ps, oc=oc, kc=kc: e.matmul(
                        ps.ap[:, oc * 2:oc * 2 + 2], W.ap[:, kc * 1024 + oc * 128: kc * 1024 + (oc + 1) * 128],
                        csb.ap[:, kc * 2:kc * 2 + 2], start=(kc == 0), stop=(kc == 7)),
                        reads=[W, csb], writes=[ps])
            base = mod_col(i, m, 0, 0)
            for s in range(2):
                P.op('vector', lambda e, ps=ps, s=s, base=base, i=i, m=m: e.tensor_tensor(
                    k.modT.ap[:, base + s: base + 16: 2], ps.ap[:, s:16:2],
                    adabT.ap[:, (i * 9 + m) * 8:(i * 9 + m) * 8 + 8], ALU.add),
                    reads=[ps, adabT], writes=[k.modT])
    for i in range(4):
        for j in range(3):
            for s in range(2):
                col = ((i * 3 + j) * 2 + s) * 8
                b_scale = mod_col(i, 3 * j + 1, 0, s)
                b_gate = mod_col(i, 3 * j + 2, 0, s)
                P.op('vector', lambda e, col=col, b=b_scale, i=i, j=j: e.scalar_tensor_tensor(
                    k.WS.ap[:, col:col + 8], k.modT.ap[:, b:b + 15:2], 1.0,
                    k.normT.ap[:, (i * 3 + j) * 8:(i * 3 + j) * 8 + 8], ALU.add, ALU.mult),
                    reads=[k.modT, k.normT], writes=[k.WS])
                P.op('vector', lambda e, col=col, b=b_gate, j=j: e.tensor_scalar(
                    k.HG.ap[:, col:col + 8], k.modT.ap[:, b:b + 15:2], (1.0 if j == 1 else 0.5), None, ALU.mult),
                    reads=[k.modT], writes=[k.HG])

    xin = [P.alloc(1024, F32, f'xin{j}') for j in range(2)]
    stg = [P.alloc(1024, F32, f'stg{j}') for j in range(2)]
    for blk in range(LT // 128):
        src = I['ctx'][blk * 128:(blk + 1) * 128, :] if blk < 2 else I['x'][(blk - 2) * 128:(blk - 1) * 128, :]
        xi = xin[blk % 2]
        sg = stg[blk % 2]
        P.dma('sync', xi.ap, src, writes=[xi])
        for h in range(2):
            ps = P.next_psum()
            for jj in range(4):
                c = h * 4 + jj
                P.op('tensor', lambda e, ps=ps, xi=xi, c=c, jj=jj: e.matmul(
                    ps.ap[:, jj * 128:(jj + 1) * 128], xi.ap[:, c * 128:(c + 1) * 128], k.ident_f.ap,
                    start=True, stop=True), reads=[xi, k.ident_f], writes=[ps])
            eng = 'vector' if h == 0 else 'scalar'
            if h == 0:
                P.op('vector', lambda e, ps=ps, sg=sg: e.tensor_copy(sg.ap[:, 0:512], ps.ap), reads=[ps], writes=[sg])
            else:
                P.op('scalar', lambda e, ps=ps, sg=sg: e.copy(sg.ap[:, 512:1024], ps.ap), reads=[ps], writes=[sg])
        P.dma('sync', k.xT_p[:, :, blk * 128:(blk + 1) * 128], sg.ap.rearrange("p (c t) -> p c t", c=8),
              reads=[sg], writes=[k.xTb[blk // 2]])
    P.release(mk)


def WS_ap(k, i, j, s, c):
    col = ((i * 3 + j) * 2 + s) * 8 + c
    return k.WS.ap[:, col:col + 1]


def HG_ap(k, i, j, s, c):
    col = ((i * 3 + j) * 2 + s) * 8 + c
    return k.HG.ap[:, col:col + 1]


def SH_ap(k, i, j, s, c):
    col = mod_col(i, 3 * j, c, s)
    return k.modT.ap[:, col:col + 1]


def alloc_norm_scratch(k, T):
    P = k.P
    k.nT = T
    sq = P.alloc(8 * T, BF16, 'sq')
    k.sq_c = [sub(sq, sq.ap[:, c * T:(c + 1) * T]) for c in range(8)]
    k.rstd = P.alloc(T, F32, 'rstd')
    k.ntmp = [P.alloc(T, F32, f'ntmp{j}') for j in range(2)]


def norm_tile(k, xt_c, hT_c, ws, sh, T, extra_reads=()):
    P = k.P
    sq_c, rstd, ntmp = k.sq_c, k.rstd, k.ntmp
    wsa = [ws(c) for c in range(8)]
    sha = [sh(c) for c in range(8)] if sh is not None else None
    for c in range(8):
        P.op('scalar', lambda e, c=c: e.activation(sq_c[c].ap[:, :T], xt_c[c].ap[:, :T], AF.Square),
             reads=[xt_c[c]], writes=[sq_c[c]])
    ps = P.next_psum()
    for c in range(8):
        P.op('tensor', lambda e, c=c, ps=ps: e.matmul(ps.ap[:, :T], k.ones_b.ap, sq_c[c].ap[:, :T],
                                                      start=(c == 0), stop=(c == 7)),
             reads=[sq_c[c], k.ones_b], writes=[ps])
    P.op('scalar', lambda e, ps=ps: e.activation(rstd.ap[:, :T], ps.ap[:, :T], AF.Sqrt, bias=EPS, scale=1.0 / D),
         reads=[ps], writes=[rstd])
    P.op('vector', lambda e: e.reciprocal(rstd.ap[:, :T], rstd.ap[:, :T]), reads=[rstd], writes=[rstd])
    for c in range(8):
        tmp = ntmp[c % 2]
        P.op('gpsimd', lambda e, c=c, tmp=tmp: e.tensor_tensor(
            tmp.ap[:, :T], xt_c[c].ap[:, :T], rstd.ap[:, :T], ALU.mult),
            reads=[xt_c[c], rstd], writes=[tmp])
        if sh is None:
            P.op('scalar', lambda e, c=c, tmp=tmp: e.activation(
                hT_c[c].ap[:, :T], tmp.ap[:, :T], AF.Identity, scale=wsa[c]),
                reads=[tmp, k.WS, k.fnT], writes=[hT_c[c]])
        else:
            P.op('scalar', lambda e, c=c, tmp=tmp: e.activation(
                hT_c[c].ap[:, :T], tmp.ap[:, :T], AF.Identity, bias=sha[c], scale=wsa[c]),
                reads=[tmp, k.modT, k.WS, k.fnT], writes=[hT_c[c]])


def ffn_phase(k, i, jf):
    P, I = k.P, k.I
    mk = P.mark()
    jn = 0 if jf == 0 else 2
    T = TT
    Win = P.alloc(8 * 5632, BF16, 'Win')
    Wout = P.alloc(22 * 1024, BF16, 'Wout')
    Win_k = [sub(Win, Win.ap[:, kc * 5632:(kc + 1) * 5632]) for kc in range(8)]
    Wout_f = [sub(Wout, Wout.ap[:, f * 1024:(f + 1) * 1024]) for f in range(22)]
    for kc in range(8):
        for q4 in range(4):
            P.dma('gpsimd', Win_k[kc].ap[:, q4 * 1408:(q4 + 1) * 1408],
                  I['ffn_w_in'][i, jf, kc * 128:(kc + 1) * 128, q4 * 1408:(q4 + 1) * 1408], writes=[Win_k[kc]])
    for f in range(22):
        P.dma('gpsimd', Wout_f[f].ap, I['ffn_w_out'][i, jf, f * 128:(f + 1) * 128, :], writes=[Wout_f[f]])
    alloc_norm_scratch(k, T)
    xt = [P.alloc(8 * T, F32, f'xt{j}') for j in range(2)]
    xt_c = [[sub(b, b.ap[:, c * T:(c + 1) * T]) for c in range(8)] for b in xt]
    hT = [P.alloc(8 * T, BF16, f'hT{j}') for j in range(2)]
    hT_c = [[sub(b, b.ap[:, c * T:(c + 1) * T]) for c in range(8)] for b in hT]
    act = P.alloc(22 * T, BF16, 'act')
    act_f = [sub(act, act.ap[:, f * T:(f + 1) * T]) for f in range(22)]
    sg = [P.alloc(T, F32, f'sg{j}') for j in range(2)]

    for tix in range(NTT):
        s = 1 if tix == 0 else 0
        t0 = tix * T
        X, Xc, Hc = xt[tix % 2], xt_c[tix % 2], hT_c[tix % 2]
        P.dma('sync', X.ap.rearrange("p (c t) -> p c t", c=8), k.xT_p[:, :, t0:t0 + T],
              reads=[k.xTb[tix]], writes=Xc)
        norm_tile(k, Xc, Hc, lambda c, s=s: WS_ap(k, i, jn, s, c), lambda c, s=s: SH_ap(k, i, jn, s, c), T)
        if tix == 0 and i == 0 and jf == 0:
            dbg_dump(k, Xc[0], label='x0')
            dbg_dump(k, k.rstd, label='rstd')
            dbg_dump(k, Hc[0], label='h0')
            dbg_dump(k, Win_k[0], Win_k[0].ap[:, 0:512], label='win0')
            dbg_dump(k, Wout_f[0], Wout_f[0].ap[:, 0:512], label='wout0')
        for f in range(22):
            pg, pu = P.next_psum(), P.next_psum()
            for (pp, off) in ((pg, 0), (pu, DFF)):
                for kc in range(8):
                    P.op('tensor', lambda e, pp=pp, off=off, kc=kc, f=f, Hc=Hc: e.matmul(
                        pp.ap[:, :T], Win_k[kc].ap[:, off + f * 128: off + (f + 1) * 128], Hc[kc].ap,
                        start=(kc == 0), stop=(kc == 7)), reads=[Win_k[kc], Hc[kc]], writes=[pp])
            sgb = sg[f % 2]
            P.op('scalar', lambda e, pg=pg, sgb=sgb: e.activation(sgb.ap, pg.ap[:, :T], AF.Silu),
                 reads=[pg], writes=[sgb])
            P.op('vector', lambda e, pu=pu, sgb=sgb, f=f: e.tensor_tensor(act_f[f].ap, sgb.ap, pu.ap[:, :T], ALU.mult),
                 reads=[pu, sgb], writes=[act_f[f]])
            if tix == 0 and i == 0 and jf == 0 and f == 0:
                dbg_dump(k, sgb, label='silu_g0')
                dbg_dump(k, act_f[0], label='act0')
        for c in range(8):
            po = P.next_psum()
            for f in range(22):
                P.op('tensor', lambda e, po=po, f=f, c=c: e.matmul(
                    po.ap[:, :T], Wout_f[f].ap[:, c * 128:(c + 1) * 128], act_f[f].ap,
                    start=(f == 0), stop=(f == 21)), reads=[Wout_f[f], act_f[f]], writes=[po])
            P.op('vector', lambda e, po=po, c=c, Xc=Xc, s=s: e.scalar_tensor_tensor(
                Xc[c].ap, po.ap[:, :T], HG_ap(k, i, jn, s, c), Xc[c].ap, ALU.mult, ALU.add),
                reads=[po, Xc[c], k.HG], writes=[Xc[c]])
        P.dma('sync', k.xT_p[:, :, t0:t0 + T], X.ap.rearrange("p (c t) -> p c t", c=8),
              reads=Xc, writes=[k.xTb[tix]])
    P.release(mk)


def epilogue(k, out):
    P = k.P
    mk = P.mark()
    T = TT
    alloc_norm_scratch(k, T)
    xt = [P.alloc(8 * T, F32, f'ext{j}') for j in range(2)]
    xt_c = [[sub(b, b.ap[:, c * T:(c + 1) * T]) for c in range(8)] for b in xt]
    yT = [P.alloc(8 * T, F32, f'eyT{j}') for j in range(2)]
    yT_c = [[sub(b, b.ap[:, c * T:(c + 1) * T]) for c in range(8)] for b in yT]
    stg = [P.alloc(1024, F32, f'estg{j}') for j in range(2)]
    n = 0
    for tix in range(1, NTT):
        t0 = tix * T
        X, Xc, Yc = xt[tix % 2], xt_c[tix % 2], yT_c[tix % 2]
        P.dma('sync', X.ap.rearrange("p (c t) -> p c t", c=8), k.xT_p[:, :, t0:t0 + T],
              reads=[k.xTb[tix]], writes=Xc)
        norm_tile(k, Xc, Yc, lambda c: k.fnT.ap[:, c:c + 1], None, T)
        for tb in range(T // 128):
            sgb = stg[n % 2]
            n += 1
            for h in range(2):
                ps = P.next_psum()
                for jj in range(4):
                    c = h * 4 + jj
                    P.op('tensor', lambda e, ps=ps, c=c, jj=jj, tb=tb, Yc=Yc: e.matmul(
                        ps.ap[:, jj * 128:(jj + 1) * 128], Yc[c].ap[:, tb * 128:(tb + 1) * 128], k.ident_f.ap,
                        start=True, stop=True), reads=[Yc[c], k.ident_f], writes=[ps])
                if h == 0:
                    P.op('vector', lambda e, ps=ps, sgb=sgb: e.tensor_copy(sgb.ap[:, 0:512], ps.ap),
                         reads=[ps], writes=[sgb])
                else:
                    P.op('scalar', lambda e, ps=ps, sgb=sgb: e.copy(sgb.ap[:, 512:1024], ps.ap),
                         reads=[ps], writes=[sgb])
            r0 = t0 - LC + tb * 128
            P.dma('sync', out[r0:r0 + 128, :], sgb.ap, reads=[sgb])
    P.release(mk)


def mixer_prep(k, i, permute):
    P = k.P
    mk = P.mark()
    T = TT
    alloc_norm_scratch(k, T)
    xt = [P.alloc(8 * T, F32, f'pxt{j}') for j in range(2)]
    xt_c = [[sub(b, b.ap[:, c * T:(c + 1) * T]) for c in range(8)] for b in xt]
    if permute:
        hall = P.alloc(8 * L, BF16, 'hall')
        hc0 = P.alloc(8 * T, BF16, 'hc0')
        hc0_c = [sub(hc0, hc0.ap[:, c * T:(c + 1) * T]) for c in range(8)]
        hall_c = [sub(hall, hall.ap[:, c * L:(c + 1) * L]) for c in range(8)]
        perm = [P.alloc(L, BF16, f'perm{j}') for j in range(2)]
    else:
        hT = [P.alloc(8 * T, BF16, f'phT{j}') for j in range(2)]
        hT_c = [[sub(b, b.ap[:, c * T:(c + 1) * T]) for c in range(8)] for b in hT]
    for tix in range(NTT):
        s = 1 if tix == 0 else 0
        t0 = tix * T
        X, Xc = xt[tix % 2], xt_c[tix % 2]
        P.dma('sync', X.ap.rearrange("p (c t) -> p c t", c=8), k.xT_p[:, :, t0:t0 + T],
              reads=[k.xTb[tix]], writes=Xc)
        if permute and tix > 0:
            Hc = [Buf(hall_c[c].ap[:, t0 - LC:t0 - LC + T]) for c in range(8)]
        elif permute:
            Hc = hc0_c
        else:
            Hc = hT_c[tix % 2]
        norm_tile(k, Xc, Hc, lambda c, s=s: WS_ap(k, i, 1, s, c), lambda c, s=s: SH_ap(k, i, 1, s, c), T)
        if permute and tix > 0:
            for c in range(8):
                hall_c[c].w = Hc[c].w
        elif permute:
            P.dma('sync', k.hT_p[:, :, 0:T], hc0.ap.rearrange("p (c t) -> p c t", c=8), reads=hc0_c,
                  writes=[k.hTb[0]])
        else:
            P.dma('sync', k.hT_p[:, :, t0:t0 + T], hT[tix % 2].ap.rearrange("p (c t) -> p c t", c=8),
                  reads=Hc, writes=[k.hTb[tix]])
    if permute:
        for c in range(8):
            pb = perm[c % 2]
            eng = 'vector' if c % 2 == 0 else 'gpsimd'
            P.op(eng, lambda e, c=c, pb=pb: e.tensor_copy(
                pb.ap.rearrange("p (cc r) -> p cc r", r=64),
                hall_c[c].ap.rearrange("p (r cc) -> p cc r", cc=64)), reads=[hall_c[c]], writes=[pb])
            P.dma('sync', k.hT_d[c, :, LC:LT], pb.ap, reads=[pb], writes=k.hTb[1:])
    P.release(mk)


def s5_phase(k, i):
    P, I = k.P, k.I
    js = i // 3
    mk = P.mark()
    V, G, A = 'vector', 'gpsimd', 'scalar'

    def tt(out, a, b, op, eng=V):
        P.op(eng, lambda e: e.tensor_tensor(out.ap, a.ap, b.ap, op), reads=[a, b], writes=[out])

    def ts(out, a, s1, op0, s2=None, op1=None, eng=V):
        if op1 is None:
            P.op(eng, lambda e: e.tensor_scalar(out.ap, a.ap, s1, None, op0), reads=[a], writes=[out])
        else:
            P.op(eng, lambda e: e.tensor_scalar(out.ap, a.ap, s1, s2, op0, op1), reads=[a], writes=[out])

    def act(out, a, func, scale=1.0):
        P.op(A, lambda e: e.activation(out.ap, a.ap, func, scale=scale), reads=[a], writes=[out])

    def sm(name=''):
        return P.alloc(32, F32, name)

    tmp = [sm(f't{j}') for j in range(4)]

    def csq(outr, outi, r, im):
        tt(tmp[0], r, r, ALU.mult)
        tt(tmp[1], im, im, ALU.mult)
        tt(outr, tmp[0], tmp[1], ALU.subtract)
        tt(tmp[2], r, im, ALU.mult)
        ts(outi, tmp[2], 2.0, ALU.mult)

    NLEV = 13
    par = []
    for d in range(2):
        are, aim, ls = sm(), sm(), sm()
        slow_dma(P, are.ap, I['s5_a_re'][js, d].rearrange("(q g2) p -> (g2 p) q", g2=2), writes=[are])
        slow_dma(P, aim.ap, I['s5_a_im'][js, d].rearrange("(q g2) p -> (g2 p) q", g2=2), writes=[aim])
        for g2 in range(2):
            slow_dma(P, ls.ap[g2 * 64:(g2 + 1) * 64, :],
                     I['s5_log_step'][js, d].rearrange("(q g2) -> g2 q", g2=2)[g2].partition_broadcast(64),
                     writes=[ls])
        dt, xr, th, rho = sm(), sm(), sm(), sm()
        act(dt, ls, AF.Exp)
        tt(xr, are, dt, ALU.mult)
        tt(th, aim, dt, ALU.mult)
        act(rho, xr, AF.Exp)
        zr, zi, th2 = sm(), sm(), sm()
        act(zi, th, AF.Sin, scale=1.0 / 16)
        ts(th2, th, 1.0 / 16, ALU.mult, PI / 2, ALU.add)
        act(zr, th2, AF.Sin)
        for _ in range(4):
            nr, ni = sm(), sm()
            csq(nr, ni, zr, zi)
            zr, zi = nr, ni
        cr, ci = zr, zi
        abr, abi, den, cfr, cfi = sm(), sm(), sm(), sm(), sm()
        tt(abr, rho, cr, ALU.mult)
        tt(abi, rho, ci, ALU.mult)
        tt(tmp[0], are, are, ALU.mult)
        tt(tmp[1], aim, aim, ALU.mult)
        tt(den, tmp[0], tmp[1], ALU.add)
        P.op(V, lambda e, den=den: e.reciprocal(den.ap, den.ap), reads=[den], writes=[den])
        zr_ = sm()
        ts(zr_, abr, -1.0, ALU.add)
        tt(tmp[0], zr_, are, ALU.mult)
        tt(tmp[1], abi, aim, ALU.mult)
        tt(tmp[2], tmp[0], tmp[1], ALU.add)
        tt(cfr, tmp[2], den, ALU.mult)
        tt(tmp[0], abi, are, ALU.mult)
        tt(tmp[1], zr_, aim, ALU.mult)
        tt(tmp[2], tmp[0], tmp[1], ALU.subtract)
        tt(cfi, tmp[2], den, ALU.mult)
        U = []
        ur, ui = cr, sm()
        ts(ui, ci, -1.0, ALU.mult)
        for m in range(NLEV):
            nui = sm()
            ts(nui, ui, -1.0, ALU.mult)
            U.append((ur, ui, nui))
            if m < NLEV - 1:
                nr, ni = sm(), sm()
                csq(nr, ni, ur, ui)
                ur, ui = nr, ni
        par.append(dict(rho=rho, cfr=cfr, cfi=cfi, U=U))

    Bn = [[P.alloc(32 * 32, F32, f'Bn{d}{r}') for r in range(2)] for d in range(2)]
    Cn = [[P.alloc(32 * 32, F32, f'Cn{d}{r}') for r in range(2)] for d in range(2)]
    mk2 = P.mark()
    Xz = P.alloc(8 * 128, F32, 'Xz')
    for d in range(2):
        for r, nm in enumerate(('s5_b_re', 's5_b_im')):
            T_ = Bn[d][r]
            P.op(G, lambda e, T_=T_: e.memset(T_.ap, 0.0), writes=[T_])
            v = T_.ap.rearrange("p (q c) -> p q c", c=32)
            for g2 in range(2):
                slow_dma(P, v[g2 * 64:(g2 + 1) * 64, :, g2 * 16:(g2 + 1) * 16],
                         I[nm][js, d].rearrange("(q g2) p c -> g2 p q c", g2=2)[g2], writes=[T_])
        for r, nm in enumerate(('s5_c_re', 's5_c_im')):
            T_ = Cn[d][r]
            for q8 in range(4):
                P.op(G, lambda e: e.memset(Xz.ap, 0.0), writes=[Xz])
                xv = Xz.ap.rearrange("p (q m) -> p q m", m=128)
                for g2 in range(2):
                    slow_dma(P, xv[g2 * 16:(g2 + 1) * 16, :, g2 * 64:(g2 + 1) * 64],
                             I[nm][js, d].rearrange("(q g2) c p -> g2 c q p", g2=2)[g2][:, q8 * 8:(q8 + 1) * 8, :],
                             writes=[Xz])
                ps = P.next_psum()
                for q in range(8):
                    P.op('tensor', lambda e, ps=ps, q=q: e.matmul(
                        ps.ap[:, q * 32:(q + 1) * 32], Xz.ap[0:32, q * 128:(q + 1) * 128], k.ident_f.ap[0:32, 0:32],
                        start=True, stop=True), reads=[Xz, k.ident_f], writes=[ps])
                P.op(V, lambda e, ps=ps, T_=T_, q8=q8: e.tensor_copy(
                    T_.ap[:, q8 * 256:(q8 + 1) * 256], ps.ap[:, 0:256]), reads=[ps], writes=[T_])
    P.release(mk2)
    P.barrier()

    Bw = [[P.alloc(128, BF16, f'Bw{q}{r}') for r in range(2)] for q in range(4)]
    Cw = [[P.alloc(128, BF16, f'Cw{q}{r}') for r in range(2)] for q in range(4)]
    for q in range(4):
        for r in range(2):
            P.op(G, lambda e, b=Bw[q][r]: e.memset(b.ap, 0.0), writes=[Bw[q][r]])
            P.op(G, lambda e, b=Cw[q][r]: e.memset(b.ap, 0.0), writes=[Cw[q][r]])
    BT = [[P.alloc(128, BF16, f'BT{j}{r}') for r in range(2)] for j in range(2)]
    ctmp = [P.alloc(32, F32, f'ctmp{j}') for j in range(3)]

    uT = [P.alloc(LT, BF16, f'uT{j}') for j in range(2)]
    Tr = P.alloc(LT, F32, 'Tr')
    Ti = P.alloc(LT, F32, 'Ti')
    yT = P.alloc(LT, F32, 'yT')
    WT = 2048
    prs, pis = P.alloc(WT, F32, 'prs'), P.alloc(WT, F32, 'pis')
    t1, t2, t3, t4 = (P.alloc(WT, F32, f't{j}w') for j in range(4))
    hr, hi = P.alloc(WT, BF16, 'hr'), P.alloc(WT, BF16, 'hi')
    car = [P.alloc(1, F32, f'car{j}') for j in range(2)]
    ttmp = [t1, t2]
    dT = P.alloc(8, F32, 'dT')
    slow_dma(P, dT.ap, I['s5_d'][js].rearrange("(c p) -> p c", p=128), writes=[dT])
    tiles3 = [(0, 256), (256, 2048), (2304, 2048)]

    def rv(ap, rev):
        return ap[:, ::-1] if rev else ap

    for fc in range(8):
        u = uT[fc % 2]
        P.dma('sync', u.ap, k.hT_d[fc], reads=k.hTb, writes=[u])
        for d in range(2):
            pr_ = par[d]
            for ql in range(4):
                pq = fc * 4 + ql
                blk = slice(pq * 32, pq * 32 + 32)
                for r in range(2):
                    P.op(G, lambda e, r=r, ql=ql, blk=blk, d=d: e.tensor_copy(
                        Bw[ql][r].ap[:, ql * 32:ql * 32 + 32], Bn[d][r].ap[:, blk]),
                        reads=[Bn[d][r]], writes=[Bw[ql][r]])
                bt = BT[pq % 2]
                for r in range(2):
                    ps = P.next_psum()
                    P.op('tensor', lambda e, ps=ps, r=r, ql=ql: e.matmul(
                        ps.ap[:, 0:128], Bw[ql][r].ap, k.ident_b.ap, start=True, stop=True),
                        reads=[Bw[ql][r], k.ident_b], writes=[ps])
                    P.op(A, lambda e, ps=ps, r=r, bt=bt: e.copy(bt[r].ap, ps.ap[:, 0:128]),
                         reads=[ps], writes=[bt[r]])
                cfr_ap = pr_['cfr'].ap[:, pq:pq + 1]
                cfi_ap = pr_['cfi'].ap[:, pq:pq + 1]
                cre, cim = Cn[d][0], Cn[d][1]
                P.op(V, lambda e, cim=cim, blk=blk, cfi_ap=cfi_ap: e.tensor_scalar(
                    ctmp[0].ap, cim.ap[:, blk], cfi_ap, None, ALU.mult), reads=[cim, pr_['cfi']], writes=[ctmp[0]])
                P.op(V, lambda e, cre=cre, blk=blk, cfr_ap=cfr_ap, ql=ql: e.scalar_tensor_tensor(
                    Cw[ql][0].ap[:, ql * 32:ql * 32 + 32], cre.ap[:, blk], cfr_ap, ctmp[0].ap, ALU.mult, ALU.subtract),
                    reads=[cre, ctmp[0], pr_['cfr']], writes=[Cw[ql][0]])
                P.op(V, lambda e, cim=cim, blk=blk, cfr_ap=cfr_ap: e.tensor_scalar(
                    ctmp[1].ap, cim.ap[:, blk], cfr_ap, None, ALU.mult), reads=[cim, pr_['cfr']], writes=[ctmp[1]])
                P.op(V, lambda e, cre=cre, blk=blk, cfi_ap=cfi_ap: e.scalar_tensor_tensor(
                    ctmp[2].ap, cre.ap[:, blk], cfi_ap, ctmp[1].ap, ALU.mult, ALU.add),
                    reads=[cre, ctmp[1], pr_['cfi']], writes=[ctmp[2]])
                P.op(V, lambda e, ql=ql: e.tensor_scalar(
                    Cw[ql][1].ap[:, ql * 32:ql * 32 + 32], ctmp[2].ap, -1.0, None, ALU.mult),
                    reads=[ctmp[2]], writes=[Cw[ql][1]])
                P.op(G, lambda e: e.memset(Tr.ap[:, 0:1], 1.0), writes=[Tr])
                P.op(G, lambda e: e.memset(Ti.ap[:, 0:1], 0.0), writes=[Ti])
                for m in range(NLEV):
                    n = 1 << m
                    cnt = min(n, LT - n)
                    ur, ui, nui = pr_['U'][m]
                    ur_ap, ui_ap, nui_ap = ur.ap[:, pq:pq + 1], ui.ap[:, pq:pq + 1], nui.ap[:, pq:pq + 1]
                    for c0 in range(0, cnt, 2048):
                        cw = min(2048, cnt - c0)
                        a0, a1 = c0, c0 + cw
                        b0, b1 = n + c0, n + c0 + cw
                        if cw >= 256:
                            P.op(A, lambda e, a0=a0, a1=a1, cw=cw, nui_ap=nui_ap: e.activation(
                                ttmp[0].ap[:, 0:cw], Ti.ap[:, a0:a1], AF.Copy, scale=nui_ap),
                                reads=[Ti, nui], writes=[ttmp[0]])
                        else:
                            P.op(V, lambda e, a0=a0, a1=a1, cw=cw, nui_ap=nui_ap: e.tensor_scalar(
                                ttmp[0].ap[:, 0:cw], Ti.ap[:, a0:a1], nui_ap, None, ALU.mult),
                                reads=[Ti, nui], writes=[ttmp[0]])
                        P.op(V, lambda e, a0=a0, a1=a1, b0=b0, b1=b1, cw=cw, ur_ap=ur_ap: e.scalar_tensor_tensor(
                            Tr.ap[:, b0:b1], Tr.ap[:, a0:a1], ur_ap, ttmp[0].ap[:, 0:cw], ALU.mult, ALU.add),
                            reads=[Tr, ttmp[0], ur], writes=[Tr])
                        if cw >= 256:
                            P.op(A, lambda e, a0=a0, a1=a1, cw=cw, ur_ap=ur_ap: e.activation(
                                ttmp[1].ap[:, 0:cw], Ti.ap[:, a0:a1], AF.Copy, scale=ur_ap),
                                reads=[Ti, ur], writes=[ttmp[1]])
                        else:
                            P.op(V, lambda e, a0=a0, a1=a1, cw=cw, ur_ap=ur_ap: e.tensor_scalar(
                                ttmp[1].ap[:, 0:cw], Ti.ap[:, a0:a1], ur_ap, None, ALU.mult),
                                reads=[Ti, ur], writes=[ttmp[1]])
                        P.op(V, lambda e, a0=a0, a1=a1, b0=b0, b1=b1, cw=cw, ui_ap=ui_ap: e.scalar_tensor_tensor(
                            Ti.ap[:, b0:b1], Tr.ap[:, a0:a1], ui_ap, ttmp[1].ap[:, 0:cw], ALU.mult, ALU.add),
                            reads=[Tr, ttmp[1], ui], writes=[Ti])
                rho_ap = pr_['rho'].ap[:, pq:pq + 1]
                for tix, (tau0, W) in enumerate(tiles3):
                    n0, n1, rev = nat_range(d, tau0, W)
                    subs = [(x0, min(512, W - x0)) for x0 in range(0, W, 512)]
                    for (x0, w) in subs:
                        pA, pB = P.next_psum(), P.next_psum()
                        for (pp, r, dst) in ((pA, 0, prs), (pB, 1, pis)):
                            P.op('tensor', lambda e, pp=pp, r=r, bt=bt, a=n0 + x0, w=w, u=u: e.matmul(
                                pp.ap[:, 0:w], bt[r].ap, u.ap[:, a:a + w], start=True, stop=True),
                                reads=[bt[r], u], writes=[pp])
                            P.op(A, lambda e, pp=pp, dst=dst, x0=x0, w=w: e.copy(dst.ap[:, x0:x0 + w], pp.ap[:, 0:w]),
                                 reads=[pp], writes=[dst])
                    trs, tis = Tr.ap[:, tau0:tau0 + W], Ti.ap[:, tau0:tau0 + W]
                    prv, piv = rv(prs.ap[:, 0:W], rev), rv(pis.ap[:, 0:W], rev)
                    P.op(V, lambda e, W=W, prv=prv, trs=trs: e.tensor_tensor(t1.ap[:, 0:W], prv, trs, ALU.mult),
                         reads=[prs, Tr], writes=[t1])
                    P.op(V, lambda e, W=W, piv=piv, tis=tis: e.tensor_tensor(t2.ap[:, 0:W], piv, tis, ALU.mult),
                         reads=[pis, Ti], writes=[t2])
                    P.op(V, lambda e, W=W, piv=piv, trs=trs: e.tensor_tensor(t3.ap[:, 0:W], piv, trs, ALU.mult),
                         reads=[pis, Tr], writes=[t3])
                    P.op(V, lambda e, W=W, prv=prv, tis=tis: e.tensor_tensor(t4.ap[:, 0:W], prv, tis, ALU.mult),
                         reads=[prs, Ti], writes=[t4])
                    P.op(G, lambda e, W=W: e.tensor_tensor(t1.ap[:, 0:W], t1.ap[:, 0:W], t2.ap[:, 0:W], ALU.subtract),
                         reads=[t1, t2], writes=[t1])
                    P.op(G, lambda e, W=W: e.tensor_tensor(t3.ap[:, 0:W], t3.ap[:, 0:W], t4.ap[:, 0:W], ALU.add),
                         reads=[t3, t4], writes=[t3])
                    for (src, dst, cj) in ((t1, t2, 0), (t3, t4, 1)):
                        init = 0.0 if tix == 0 else car[cj].ap[:, 0:1]
                        rd = [] if tix == 0 else [car[cj]]
                        P.op(V, lambda e, src=src, dst=dst, W=W, init=init, rho_ap=rho_ap: e.tensor_tensor_scan(
                            dst.ap[:, 0:W], rho_ap.broadcast_to([128, W]), src.ap[:, 0:W], init, ALU.mult, ALU.add),
                            reads=[src, pr_['rho']] + rd, writes=[dst])
                        if tix < len(tiles3) - 1:
                            P.op(A, lambda e, dst=dst, cj=cj, W=W: e.copy(car[cj].ap, dst.ap[:, W - 1:W]),
                                 reads=[dst], writes=[car[cj]])
                    P.op(G, lambda e, W=W, trs=trs: e.tensor_tensor(prs.ap[:, 0:W], t2.ap[:, 0:W], trs, ALU.mult),
                         reads=[t2, Tr], writes=[prs])
                    P.op(G, lambda e, W=W, tis=tis: e.tensor_tensor(pis.ap[:, 0:W], t4.ap[:, 0:W], tis, ALU.mult),
                         reads=[t4, Ti], writes=[pis])
                    P.op(G, lambda e, W=W, trs=trs: e.tensor_tensor(t1.ap[:, 0:W], t4.ap[:, 0:W], trs, ALU.mult),
                         reads=[t4, Tr], writes=[t1])
                    P.op(G, lambda e, W=W, tis=tis: e.tensor_tensor(t3.ap[:, 0:W], t2.ap[:, 0:W], tis, ALU.mult),
                         reads=[t2, Ti], writes=[t3])
                    P.op(V, lambda e, W=W, rev=rev: e.tensor_tensor(
                        rv(hr.ap[:, 0:W], rev), prs.ap[:, 0:W], pis.ap[:, 0:W], ALU.add),
                        reads=[prs, pis], writes=[hr])
                    P.op(V, lambda e, W=W, rev=rev: e.tensor_tensor(
                        rv(hi.ap[:, 0:W], rev), t1.ap[:, 0:W], t3.ap[:, 0:W], ALU.subtract),
                        reads=[t1, t3], writes=[hi])
                    for (x0, w) in subs:
                        a = n0 + x0
                        py = P.next_psum()
                        P.op('tensor', lambda e, py=py, ql=ql, x0=x0, w=w: e.matmul(
                            py.ap[:, 0:w], Cw[ql][0].ap, hr.ap[:, x0:x0 + w], start=True, stop=False),
                            reads=[Cw[ql][0], hr], writes=[py])
                        P.op('tensor', lambda e, py=py, ql=ql, x0=x0, w=w: e.matmul(
                            py.ap[:, 0:w], Cw[ql][1].ap, hi.ap[:, x0:x0 + w], start=False, stop=True),
                            reads=[Cw[ql][1], hi], writes=[py])
                        if d == 0 and ql == 0:
                            P.op(A, lambda e, py=py, a=a, w=w: e.copy(yT.ap[:, a:a + w], py.ap[:, 0:w]),
                                 reads=[py], writes=[yT])
                        else:
                            P.op(V, lambda e, py=py, a=a, w=w: e.tensor_tensor(
                                yT.ap[:, a:a + w], py.ap[:, 0:w], yT.ap[:, a:a + w], ALU.add),
                                reads=[py, yT], writes=[yT])
        for (tau0, W) in tiles3:
            sl = slice(tau0, tau0 + W)
            P.op(V, lambda e, W=W, sl=sl, u=u, fc=fc: e.scalar_tensor_tensor(
                t1.ap[:, 0:W], u.ap[:, sl], dT.ap[:, fc:fc + 1], yT.ap[:, sl], ALU.mult, ALU.add),
                reads=[u, yT, dT], writes=[t1])
            P.op(G, lambda e, W=W: e.tensor_tensor(t2.ap[:, 0:W], t1.ap[:, 0:W], t1.ap[:, 0:W], ALU.mult),
                 reads=[t1], writes=[t2])
            P.op(A, lambda e, W=W: e.activation(t2.ap[:, 0:W], t2.ap[:, 0:W], AF.Identity,
                                                 bias=k.onec.ap[:, 0:1], scale=0.044715),
                 reads=[t2, k.onec], writes=[t2])
            P.op(G, lambda e, W=W: e.tensor_tensor(t3.ap[:, 0:W], t2.ap[:, 0:W], t1.ap[:, 0:W], ALU.mult),
                 reads=[t1, t2], writes=[t3])
            P.op(A, lambda e, W=W: e.activation(t4.ap[:, 0:W], t3.ap[:, 0:W], AF.Sigmoid, scale=1.5957691216057308),
                 reads=[t3], writes=[t4])
            P.op(V, lambda e, W=W: e.tensor_tensor(hr.ap[:, 0:W], t1.ap[:, 0:W], t4.ap[:, 0:W], ALU.mult),
                 reads=[t1, t4], writes=[hr])
            P.dma('sync', k.yg_d[fc, :, sl], hr.ap[:, 0:W], reads=[hr],
                  writes=[k.ygb[g] for g in range(tau0 // TT, (tau0 + W) // TT)])
    P.release(mk)
    P.barrier()
    s5_glu(k, i)


def s5_glu(k, i):
    P, I = k.P, k.I
    js = i // 3
    mk = P.mark()
    T = TT
    Wg = P.alloc(8 * 2048, BF16, 'Wg')
    Wg_k = [sub(Wg, Wg.ap[:, kc * 2048:(kc + 1) * 2048]) for kc in range(8)]
    for kc in range(8):
        for h in range(2):
            P.dma('gpsimd', Wg_k[kc].ap[:, h * 1024:(h + 1) * 1024],
                  I['s5_w_glu'][js, kc * 128:(kc + 1) * 128, h * 1024:(h + 1) * 1024], writes=[Wg_k[kc]])
    bT = P.alloc(16, F32, 'bgluT')
    slow_dma(P, bT.ap, I['s5_b_glu'][js].rearrange("(c p) -> p c", p=128), writes=[bT])
    xt = [P.alloc(8 * T, F32, f'gxt{j}') for j in range(2)]
    xt_c = [[sub(b, b.ap[:, c * T:(c + 1) * T]) for c in range(8)] for b in xt]
    yg = [P.alloc(8 * T, BF16, f'gyg{j}') for j in range(2)]
    sg = [P.alloc(T, F32, f'gsg{j}') for j in range(2)]
    ob = [P.alloc(T, F32, f'gob{j}') for j in range(2)]
    for tix in range(NTT):
        s = 1 if tix == 0 else 0
        t0 = tix * T
        X, Xc, Y = xt[tix % 2], xt_c[tix % 2], yg[tix % 2]
        P.dma('sync', X.ap.rearrange("p (c t) -> p c t", c=8), k.xT_p[:, :, t0:t0 + T],
              reads=[k.xTb[tix]], writes=Xc)
        P.dma('sync', Y.ap.rearrange("p (c t) -> p c t", c=8), k.yg_p[:, :, t0:t0 + T],
              reads=[k.ygb[tix]], writes=[Y])
        for c in range(8):
            pv, pg = P.next_psum(), P.next_psum()
            for (pp, off) in ((pv, 0), (pg, 1024)):
                for kc in range(8):
                    P.op('tensor', lambda e, pp=pp, off=off, kc=kc, c=c, Y=Y: e.matmul(
                        pp.ap[:, :T], Wg_k[kc].ap[:, off + c * 128: off + (c + 1) * 128],
                        Y.ap[:, kc * T:(kc + 1) * T], start=(kc == 0), stop=(kc == 7)),
                        reads=[Wg_k[kc], Y], writes=[pp])
            sgb, obb = sg[c % 2], ob[c % 2]
            P.op('scalar', lambda e, pg=pg, sgb=sgb, c=c: e.activation(
                sgb.ap, pg.ap[:, :T], AF.Sigmoid, bias=bT.ap[:, 8 + c:9 + c], scale=1.0),
                reads=[pg, bT], writes=[sgb])
            P.op('vector', lambda e, pv=pv, sgb=sgb, obb=obb, c=c: e.scalar_tensor_tensor(
                obb.ap, pv.ap[:, :T], bT.ap[:, c:c + 1], sgb.ap, ALU.add, ALU.mult),
                reads=[pv, sgb, bT], writes=[obb])
            P.op('vector', lambda e, obb=obb, c=c, Xc=Xc, s=s: e.scalar_tensor_tensor(
                Xc[c].ap, obb.ap, HG_ap(k, i, 1, s, c), Xc[c].ap, ALU.mult, ALU.add),
                reads=[obb, Xc[c], k.HG], writes=[Xc[c]])
        P.dma('sync', k.xT_p[:, :, t0:t0 + T], X.ap.rearrange("p (c t) -> p c t", c=8),
              reads=Xc, writes=[k.xTb[tix]])
    P.release(mk)


TILES512 = [(0, 256)] + [(256 + j * 512, 512) for j in range(L // 512)]
NCH = LT // 64


def nat_range(d, tau0, w):
    if d == 0:
        return tau0, tau0 + w, False
    hi_ = (LC - 1 - tau0) if tau0 < LC else (LT + LC - 1 - tau0)
    return hi_ - w + 1, hi_ + 1, True


def tv(ap, d, tau0, w):
    n0, n1, rev = nat_range(d, tau0, w)
    v = ap[:, n0:n1]
    return v[:, ::-1] if rev else v


def gated_phase(k, i, kind):
    P, I = k.P, k.I
    hg = kind == 'hgrn'
    NH = 8 if hg else 4
    DV = 128 if hg else 256
    VT = DV // 128
    W_in = I['hgrn_w_in'][0] if hg else I['gla_w_in'][0]
    NCOL = 5120 if hg else 3104
    V, G, A = 'vector', 'gpsimd', 'scalar'
    mk = P.mark()
    Wb = P.alloc(8 * NCOL, BF16, 'Wb')
    Wb_k = [sub(Wb, Wb.ap[:, kc * NCOL:(kc + 1) * NCOL]) for kc in range(8)]
    for kc in range(8):
        for c0 in range(0, NCOL, 1024):
            cw = min(1024, NCOL - c0)
            P.dma('gpsimd', Wb_k[kc].ap[:, c0:c0 + cw], W_in[kc * 128:(kc + 1) * 128, c0:c0 + cw], writes=[Wb_k[kc]])
    if hg:
        lg = P.alloc(64, F32, 'lg')
        slow_dma(P, lg.ap, I['hgrn_lb_logits'].rearrange("l d (c p) -> p (l d c)", p=128), writes=[lg])
        E = P.alloc(64, F32, 'lgE')
        P.op(A, lambda e: e.activation(E.ap, lg.ap, AF.Exp), reads=[lg], writes=[E])
        den, num, oml = P.alloc(16, F32, 'den'), P.alloc(16, F32, 'num'), P.alloc(16, F32, 'oml')
        P.op(V, lambda e: e.tensor_tensor(den.ap, E.ap[:, 0:16], E.ap[:, 16:32], ALU.add), reads=[E], writes=[den])
        P.op(V, lambda e: e.tensor_tensor(den.ap, den.ap, E.ap[:, 32:48], ALU.add), reads=[E, den], writes=[den])
        P.op(V, lambda e: e.tensor_tensor(den.ap, den.ap, E.ap[:, 48:64], ALU.add), reads=[E, den], writes=[den])
        P.op(G, lambda e: e.memset(num.ap, 0.0), writes=[num])
        for l in range(1, i + 1):
            P.op(V, lambda e, l=l: e.tensor_tensor(num.ap, num.ap, E.ap[:, l * 16:(l + 1) * 16], ALU.add),
                 reads=[E, num], writes=[num])
        P.op(V, lambda e: e.reciprocal(den.ap, den.ap), reads=[den], writes=[den])
        P.op(V, lambda e: e.tensor_tensor(num.ap, num.ap, den.ap, ALU.mult), reads=[num, den], writes=[num])
        P.op(V, lambda e: e.tensor_scalar(oml.ap, num.ap, -1.0, 1.0, ALU.mult, ALU.add), reads=[num], writes=[oml])
    else:
        wg = P.alloc(1024, BF16, 'wg', parts=16)
        P.dma('gpsimd', wg.ap.rearrange("p (z n) -> p z n", z=2), I['gla_w_gate'][0].rearrange("z r n -> r z n"),
              writes=[wg])
        nbg = P.alloc(8, F32, 'nbg')
        slow_dma(P, nbg.ap, I['gla_b_gate'][0].rearrange("z (h p) -> p (z h)", p=128), writes=[nbg])
        P.op(V, lambda e: e.tensor_scalar(nbg.ap, nbg.ap, -1.0, None, ALU.mult), reads=[nbg], writes=[nbg])
        lowsb = [[P.alloc(512, BF16, f'low{z}{j}', parts=16) for j in range(2)] for z in range(2)]
    ht = [P.alloc(8 * 512, BF16, f'ght{j}') for j in range(2)]
    stb = [P.alloc(512, BF16, f'stb{j}') for j in range(4)]
    stf = [P.alloc(512, F32, f'stf{j}') for j in range(4)]
    tf = [P.alloc(512, F32, f'tf{j}') for j in range(4)]
    vst = [P.alloc(1024, BF16, f'vst{j}', parts=64) for j in range(2)]
    cnt = {'b': 0, 'f': 0, 't': 0, 'v': 0}

    def nxt(lst, key):
        b_ = lst[cnt[key] % len(lst)]
        cnt[key] += 1
        return b_

    for tix, (t0, w) in enumerate(TILES512):
        H = ht[tix % 2]
        hv = H.ap.rearrange("p (c t) -> p c t", c=8)
        P.dma('sync', hv[:, :, 0:w], k.hT_p[:, :, t0:t0 + w], reads=k.hTb, writes=[H])
        sl = slice(t0, t0 + w)

        def proj(col0, M=128, H=H, w=w):
            ps = P.next_psum()
            for kc in range(8):
                P.op('tensor', lambda e, ps=ps, kc=kc, col0=col0, M=M, H=H, w=w: e.matmul(
                    ps.ap[0:M, 0:w], Wb_k[kc].ap[:, col0:col0 + M], H.ap[:, kc * 512:kc * 512 + w],
                    start=(kc == 0), stop=(kc == 7)), reads=[Wb_k[kc], H], writes=[ps])
            return ps

        def act_store(ps, func, scale, dst_ap, dst_buf, w=w):
            sb = nxt(stb, 'b')
            P.op(A, lambda e, ps=ps, sb=sb, func=func, scale=scale, w=w: e.activation(
                sb.ap[:, 0:w], ps.ap[:, 0:w], func, scale=scale), reads=[ps], writes=[sb])
            P.dma('sync', dst_ap, sb.ap[:, 0:w], reads=[sb], writes=[dst_buf])

        if hg:
            for h in range(8):
                act_store(proj(h * 128), AF.Silu, 1.0, k.pq_d[h, :, sl], k.pqb[h])
                act_store(proj(4096 + h * 128), AF.Silu, 1.0, k.pgate_d[h, :, sl], k.pgateb[h])
                for d, zoff in ((0, 2048), (1, 3072)):
                    ps = proj(zoff + h * 128)
                    a1, a2, sb, sf = nxt(tf, 't'), nxt(tf, 't'), nxt(stb, 'b'), nxt(stf, 'f')
                    P.op(A, lambda e, ps=ps, a1=a1, w=w: e.activation(a1.ap[:, 0:w], ps.ap[:, 0:w], AF.Sigmoid, scale=-1.0),
                         reads=[ps], writes=[a1])
                    P.op(V, lambda e, a1=a1, a2=a2, w=w, d=d, h=h: e.tensor_scalar(
                        a2.ap[:, 0:w], a1.ap[:, 0:w], oml.ap[:, d * 8 + h:d * 8 + h + 1], None, ALU.mult),
                        reads=[a1, oml], writes=[a2])
                    P.op(G, lambda e, a2=a2, sb=sb, w=w: e.tensor_copy(sb.ap[:, 0:w], a2.ap[:, 0:w]),
                         reads=[a2], writes=[sb])
                    P.dma('sync', k.pk_d[d, h, :, sl], sb.ap[:, 0:w], reads=[sb], writes=[k.pkb[d][h]])
                    P.op(A, lambda e, a2=a2, sf=sf, w=w: e.activation(
                        sf.ap[:, 0:w], a2.ap[:, 0:w], AF.Ln, bias=k.onec.ap[:, 0:1], scale=-1.0),
                        reads=[a2, k.onec], writes=[sf])
                    P.dma('sync', k.pg_d[d, h, :, sl], sf.ap[:, 0:w], reads=[sf], writes=[k.pgb[d][h]])
        else:
            for h in range(4):
                act_store(proj(h * 128), AF.Identity, 128.0 ** -0.5, k.pq_d[h, :, sl], k.pqb[h])
                act_store(proj(512 + h * 128), AF.Identity, 1.0, k.pk_d[0, h, :, sl], k.pkb[0][h])
            for b_ in range(8):
                act_store(proj(2048 + b_ * 128), AF.Silu, 1.0, k.pgate_d[b_, :, sl], k.pgateb[b_])
            for z in range(2):
                ps = proj(3072 + z * 16, M=16)
                lw = lowsb[z][tix % 2]
                P.op(A, lambda e, ps=ps, lw=lw, w=w: e.copy(lw.ap[:, 0:w], ps.ap[0:16, 0:w]), reads=[ps], writes=[lw])
                for h in range(4):
                    pg = P.next_psum()
                    P.op('tensor', lambda e, pg=pg, lw=lw, z=z, h=h, w=w: e.matmul(
                        pg.ap[:, 0:w], wg.ap[:, z * 512 + h * 128: z * 512 + (h + 1) * 128], lw.ap[:, 0:w],
                        start=True, stop=True), reads=[wg, lw], writes=[pg])
                    a1, a2, sf = nxt(tf, 't'), nxt(tf, 't'), nxt(stf, 'f')
                    P.op(A, lambda e, pg=pg, a1=a1, z=z, h=h, w=w: e.activation(
                        a1.ap[:, 0:w], pg.ap[:, 0:w], AF.Exp, bias=nbg.ap[:, z * 4 + h:z * 4 + h + 1], scale=-1.0),
                        reads=[pg, nbg], writes=[a1])
                    P.op(A, lambda e, a1=a1, a2=a2, w=w: e.activation(
                        a2.ap[:, 0:w], a1.ap[:, 0:w], AF.Ln, bias=k.onec.ap[:, 0:1], scale=1.0),
                        reads=[a1, k.onec], writes=[a2])
                    P.op(V, lambda e, a2=a2, sf=sf, w=w: e.tensor_scalar(
                        sf.ap[:, 0:w], a2.ap[:, 0:w], -1.0 / 16.0, None, ALU.mult), reads=[a2], writes=[sf])
                    P.dma('sync', k.pg_d[z, h, :, sl], sf.ap[:, 0:w], reads=[sf], writes=[k.pgb[z][h]])
        for ci in range(w // 64):
            vs_ = nxt(vst, 'v')
            for half in range(2):
                ps = P.next_psum()
                for kc in range(8):
                    P.op('tensor', lambda e, ps=ps, kc=kc, ci=ci, half=half, H=H: e.matmul(
                        ps.ap[0:64, 0:512], H.ap[:, kc * 512 + ci * 64: kc * 512 + ci * 64 + 64],
                        Wb_k[kc].ap[:, 1024 + half * 512: 1024 + (half + 1) * 512],
                        start=(kc == 0), stop=(kc == 7)), reads=[Wb_k[kc], H], writes=[ps])
                if half == 0:
                    P.op(V, lambda e, ps=ps, vs_=vs_: e.tensor_copy(vs_.ap[:, 0:512], ps.ap[0:64, 0:512]),
                         reads=[ps], writes=[vs_])
                else:
                    P.op(G if False else A, lambda e, ps=ps, vs_=vs_: e.copy(vs_.ap[:, 512:1024], ps.ap[0:64, 0:512]),
                         reads=[ps], writes=[vs_])
            P.dma('sync', k.pv_d[t0 // 64 + ci], vs_.ap, reads=[vs_], writes=[k.pvb])
    P.release(mk)
    P.barrier()

    mk = P.mark()
    nw = P.alloc(VT, F32, 'nw')
    slow_dma(P, nw.ap, (I['hgrn_norm_w'] if hg else I['gla_norm_w'])[0].rearrange("(v p) -> p v", p=128), writes=[nw])
    rmask = P.alloc(L, F32, 'rmask')
    P.op(G, lambda e: e.memset(rmask.ap, 1.0), writes=[rmask])
    P.op(G, lambda e: e.memset(rmask.ap.rearrange("p (m j) -> p m j", j=64)[:, :, 0:1], 0.0), writes=[rmask])
    qn = P.alloc(LT, BF16, 'qn')
    kn = P.alloc(LT, BF16, 'kn')
    bufA = P.alloc(LT, F32, 'bufA')
    bufB = P.alloc(LT, F32, 'bufB')
    qt = P.alloc(LT, BF16, 'qt')
    ktn = P.alloc(LT, BF16, 'ktn')
    kdn = P.alloc(LT, BF16, 'kdn')
    vsb = P.alloc(NCH * DV, BF16, 'vsb', parts=64)
    oT = [P.alloc(LT, F32, f'oT{v}') for v in range(VT)]
    S32 = [P.alloc(DV, F32, f'S32_{j}') for j in range(2)]
    Sbf = [P.alloc(DV, BF16, f'Sbf{j}') for j in range(2)]
    scs = [P.alloc(64, BF16, f'scs{j}', parts=64) for j in range(2)]
    kdT = [P.alloc(128, BF16, f'kdT{j}', parts=64) for j in range(2)]
    gsb = [P.alloc(512, BF16, f'gsb{j}') for j in range(2)]
    sqb = [P.alloc(512, BF16, f'sqb{j}') for j in range(2)]
    rs = [P.alloc(512, F32, f'rs{j}') for j in range(2)]
    o1 = [P.alloc(512, F32, f'o1{j}') for j in range(2)]
    ogs = [P.alloc(512, BF16, f'ogs{j}') for j in range(2)]
    segs = [(0, LC), (LC, L)]

    for h in range(NH):
        P.dma('sync', qn.ap, k.pq_d[h], reads=[k.pqb[h]], writes=[qn])
        P.dma('sync', vsb.ap.rearrange("j (n c) -> j n c", c=DV),
              k.pv_d.rearrange("n j c -> j n c")[:, :, h * DV:(h + 1) * DV], reads=[k.pvb], writes=[vsb])
        for d in range(2):
            kd = d if hg else 0
            if hg or d == 0:
                P.dma('sync', kn.ap, k.pk_d[kd, h], reads=[k.pkb[kd][h]], writes=[kn])
            if d == 0:
                P.dma('sync', bufB.ap, k.pg_d[d, h], reads=[k.pgb[d][h]], writes=[bufB])
            else:
                P.dma('sync', bufA.ap, k.pg_d[d, h], reads=[k.pgb[d][h]], writes=[bufA])
                for (s0, sw) in segs:
                    P.op(V, lambda e, s0=s0, sw=sw: e.tensor_copy(bufB.ap[:, s0:s0 + sw], tv(bufA.ap, 1, s0, sw)),
                         reads=[bufA], writes=[bufB])
            for (s0, sw) in segs:
                P.op(V, lambda e, s0=s0, sw=sw: e.tensor_tensor_scan(
                    bufA.ap[:, s0:s0 + sw], rmask.ap[:, 0:sw], bufB.ap[:, s0:s0 + sw], 0.0, ALU.mult, ALU.add),
                    reads=[bufB, rmask], writes=[bufA])
            P.op(A, lambda e: e.activation(bufB.ap, bufA.ap, AF.Exp, scale=-1.0), reads=[bufA], writes=[bufB])
            P.op(A, lambda e: e.activation(bufA.ap, bufA.ap, AF.Exp), reads=[bufA], writes=[bufA])
            for (s0, sw) in segs:
                P.op(V, lambda e, s0=s0, sw=sw, d=d: e.tensor_tensor(
                    qt.ap[:, s0:s0 + sw], tv(qn.ap, d, s0, sw), bufA.ap[:, s0:s0 + sw], ALU.mult),
                    reads=[qn, bufA], writes=[qt])
                P.op(V, lambda e, s0=s0, sw=sw, d=d: e.tensor_tensor(
                    tv(ktn.ap, d, s0, sw), tv(kn.ap, d, s0, sw), bufB.ap[:, s0:s0 + sw], ALU.mult),
                    reads=[kn, bufB], writes=[ktn])
                nm = sw // 64
                P.op(V, lambda e, s0=s0, sw=sw, d=d, nm=nm: e.tensor_tensor(
                    tv(kdn.ap, d, s0, sw).rearrange("p (m j) -> p m j", j=64),
                    tv(ktn.ap, d, s0, sw).rearrange("p (m j) -> p m j", j=64),
                    bufA.ap[:, s0:s0 + sw].rearrange("p (m j) -> p m j", j=64)[:, :, 63:64].broadcast_to([128, nm, 64]),
                    ALU.mult), reads=[ktn, bufA], writes=[kdn])
            P.op(G, lambda e: e.memset(S32[0].ap, 0.0), writes=[S32[0]])
            P.op(G, lambda e: e.memset(Sbf[0].ap, 0.0), writes=[Sbf[0]])
            for m in range(NCH):
                tau0 = m * 64
                n0, n1, rev = nat_range(d, tau0, 64)
                nn = n0 // 64
                jb = m % 2
                Sp32, Sn32, Spb, Snb = S32[m % 2], S32[(m + 1) % 2], Sbf[m % 2], Sbf[(m + 1) % 2]
                psS = P.next_psum()
                P.op('tensor', lambda e, psS=psS, n0=n0, n1=n1, tau0=tau0: e.matmul(
                    psS.ap[0:64, 0:64], ktn.ap[:, n0:n1], qt.ap[:, tau0:tau0 + 64], start=True, stop=True),
                    reads=[ktn, qt], writes=[psS])
                sc = scs[jb]
                P.op(V, lambda e, psS=psS, sc=sc, d=d: e.tensor_tensor(
                    sc.ap, psS.ap[0:64, 0:64], k.mask.ap[:, d * 64:(d + 1) * 64], ALU.mult),
                    reads=[psS, k.mask], writes=[sc])
                psK = P.next_psum()
                P.op('tensor', lambda e, psK=psK, n0=n0, n1=n1: e.matmul(
                    psK.ap[0:64, 0:128], kdn.ap[:, n0:n1], k.ident_b.ap, start=True, stop=True),
                    reads=[kdn, k.ident_b], writes=[psK])
                kt_ = kdT[jb]
                P.op(A, lambda e, psK=psK, kt_=kt_: e.copy(kt_.ap, psK.ap[0:64, 0:128]), reads=[psK], writes=[kt_])
                for v in range(VT):
                    psO = P.next_psum()
                    P.op('tensor', lambda e, psO=psO, nn=nn, v=v, sc=sc: e.matmul(
                        psO.ap[:, 0:64], vsb.ap[:, nn * DV + v * 128: nn * DV + (v + 1) * 128], sc.ap,
                        start=True, stop=False), reads=[vsb, sc], writes=[psO])
                    P.op('tensor', lambda e, psO=psO, v=v, Spb=Spb, tau0=tau0: e.matmul(
                        psO.ap[:, 0:64], Spb.ap[:, v * 128:(v + 1) * 128], qt.ap[:, tau0:tau0 + 64],
                        start=False, stop=True), reads=[Spb, qt], writes=[psO])
                    ov = oT[v].ap[:, n0:n1]
                    ov = ov[:, ::-1] if rev else ov
                    if d == 0:
                        P.op(G if False else A, lambda e, psO=psO, ov=ov: e.copy(ov, psO.ap[:, 0:64]),
                             reads=[psO], writes=[oT[v]])
                    else:
                        P.op(V, lambda e, psO=psO, ov=ov: e.tensor_tensor(ov, psO.ap[:, 0:64], ov, ALU.add),
                             reads=[psO, oT[v]], writes=[oT[v]])
                psV = P.next_psum()
                P.op('tensor', lambda e, psV=psV, kt_=kt_, nn=nn: e.matmul(
                    psV.ap[:, 0:DV], kt_.ap, vsb.ap[:, nn * DV:(nn + 1) * DV], start=True, stop=True),
                    reads=[kt_, vsb], writes=[psV])
                P.op(V, lambda e, psV=psV, Sp32=Sp32, Sn32=Sn32, tau0=tau0: e.scalar_tensor_tensor(
                    Sn32.ap, Sp32.ap, bufA.ap[:, tau0 + 63:tau0 + 64], psV.ap[:, 0:DV], ALU.mult, ALU.add),
                    reads=[psV, Sp32, bufA], writes=[Sn32])
                P.op(G, lambda e, Sn32=Sn32, Snb=Snb: e.tensor_copy(Snb.ap, Sn32.ap), reads=[Sn32], writes=[Snb])
        for tix, (t0, w) in enumerate(TILES512):
            jb = tix % 2
            pss = P.next_psum()
            for v in range(VT):
                sq_ = sqb[(tix * VT + v) % 2]
                P.op(A, lambda e, sq_=sq_, v=v, t0=t0, w=w: e.activation(sq_.ap[:, 0:w], oT[v].ap[:, t0:t0 + w], AF.Square),
                     reads=[oT[v]], writes=[sq_])
                P.op('tensor', lambda e, pss=pss, sq_=sq_, v=v, w=w: e.matmul(
                    pss.ap[:, 0:w], k.ones_b.ap, sq_.ap[:, 0:w], start=(v == 0), stop=(v == VT - 1)),
                    reads=[sq_, k.ones_b], writes=[pss])
            r_ = rs[jb]
            P.op(A, lambda e, pss=pss, r_=r_, w=w: e.activation(r_.ap[:, 0:w], pss.ap[:, 0:w], AF.Sqrt, bias=EPS, scale=1.0 / DV),
                 reads=[pss], writes=[r_])
            P.op(V, lambda e, r_=r_, w=w: e.reciprocal(r_.ap[:, 0:w], r_.ap[:, 0:w]), reads=[r_], writes=[r_])
            for v in range(VT):
                blk = h * VT + v
                g_, o_, og_ = gsb[(tix * VT + v) % 2], o1[(tix * VT + v) % 2], ogs[(tix * VT + v) % 2]
                P.dma('sync', g_.ap[:, 0:w], k.pgate_d[blk, :, t0:t0 + w], reads=[k.pgateb[blk]], writes=[g_])
                P.op(V, lambda e, o_=o_, v=v, t0=t0, w=w, r_=r_: e.scalar_tensor_tensor(
                    o_.ap[:, 0:w], oT[v].ap[:, t0:t0 + w], nw.ap[:, v:v + 1], r_.ap[:, 0:w], ALU.mult, ALU.mult),
                    reads=[oT[v], nw, r_], writes=[o_])
                P.op(G, lambda e, o_=o_, g_=g_, og_=og_, w=w: e.tensor_tensor(
                    og_.ap[:, 0:w], o_.ap[:, 0:w], g_.ap[:, 0:w], ALU.mult), reads=[o_, g_], writes=[og_])
                P.dma('sync', k.yg_d[blk, :, t0:t0 + w], og_.ap[:, 0:w], reads=[og_], writes=[k.ogb[blk]])
    P.release(mk)
    P.barrier()

    mk = P.mark()
    Wo = P.alloc(8 * 1024, BF16, 'Wo')
    Wo_k = [sub(Wo, Wo.ap[:, kc * 1024:(kc + 1) * 1024]) for kc in range(8)]
    W_out = I['hgrn_w_out'][0] if hg else I['gla_w_out'][0]
    for kc in range(8):
        P.dma('gpsimd', Wo_k[kc].ap, W_out[kc * 128:(kc + 1) * 128, :], writes=[Wo_k[kc]])
    og = P.alloc(8 * LT, BF16, 'og_all')
    og_k = [sub(og, og.ap[:, kc * LT:(kc + 1) * LT]) for kc in range(8)]
    for kc in range(8):
        P.dma('sync', og_k[kc].ap, k.yg_d[kc], reads=[k.ogb[kc]], writes=[og_k[kc]])
    xc = [P.alloc(LT, F32, f'xc{j}') for j in range(2)]
    for c in range(8):
        X = xc[c % 2]
        P.dma('sync', X.ap, k.xT[c], reads=k.xTb, writes=[X])
        for tix, (t0, w) in enumerate(TILES512):
            s = 1 if tix == 0 else 0
            ps = P.next_psum()
            for kc in range(8):
                P.op('tensor', lambda e, ps=ps, kc=kc, c=c, t0=t0, w=w: e.matmul(
                    ps.ap[:, 0:w], Wo_k[kc].ap[:, c * 128:(c + 1) * 128], og_k[kc].ap[:, t0:t0 + w],
                    start=(kc == 0), stop=(kc == 7)), reads=[Wo_k[kc], og_k[kc]], writes=[ps])
            if hg or tix == 0:
                xv = X.ap[:, t0:t0 + w]
                pv_ = ps.ap[:, 0:w]
            else:
                cc0 = (t0 - LC) // 64
                xv = X.ap[:, LC:LT].rearrange("p (r cc) -> p cc r", cc=64)[:, cc0:cc0 + 8, :]
                pv_ = ps.ap[:, 0:w].rearrange("p (cc r) -> p cc r", r=64)
            P.op(V, lambda e, pv_=pv_, xv=xv, c=c, s=s: e.scalar_tensor_tensor(
                xv, pv_, HG_ap(k, i, 1, s, c), xv, ALU.mult, ALU.add), reads=[ps, X, k.HG], writes=[X])
        P.dma('sync', k.xT[c], X.ap, reads=[X], writes=k.xTb)
    P.release(mk)


def make_consts():
    ident = np.eye(128, dtype=np.float32)
    iota = np.broadcast_to(np.arange(LT, dtype=np.float32)[None, :], (128, LT)).copy()
    jj = np.arange(64)[:, None]
    ii = np.arange(64)[None, :]
    mask = np.stack([(jj <= ii), (jj >= 63 - ii)]).astype(np.float32)
    return {'k_ident': ident, 'k_iota': iota, 'k_mask': mask}


def make_in_maps(inputs, cores):
    consts = make_consts()
    shared = {}
    for n, s in INPUT_SHAPES.items():
        if n in ('x', 'c', 'ctx') or n.startswith('k_'):
            continue
        shared[n] = np.ascontiguousarray(np.asarray(inputs[n], dtype=np.float32).reshape(s))
    maps = []
    for b in cores:
        m = dict(shared)
        m.update(consts)
        m['x'] = np.ascontiguousarray(np.asarray(inputs['x'][b], dtype=np.float32))
        m['ctx'] = np.ascontiguousarray(np.asarray(inputs['ctx'][b], dtype=np.float32))
        m['c'] = np.ascontiguousarray(np.asarray(inputs['c'][b], dtype=np.float32).reshape(1, D))
        maps.append(m)
    return maps


_NC_CACHE = {}


def kernel(**inputs):
    if 'full' not in _NC_CACHE:
        _NC_CACHE['full'] = build_nc()
    nc = _NC_CACHE['full']
    maps = make_in_maps(inputs, list(range(8)))
    res = run_bass_kernel_spmd(nc, maps, core_ids=list(range(8)))
    return np.stack([np.asarray(r["out"], dtype=np.float32) for r in res.results], axis=0)
```

```python
import math
from contextlib import ExitStack

import numpy as np
import concourse.bass as bass
import concourse.mybir as mybir
from concourse.bass_utils import run_bass_kernel_spmd

F32 = mybir.dt.float32
BF16 = mybir.dt.bfloat16
ALU = mybir.AluOpType
AF = mybir.ActivationFunctionType

ENG = ['tensor', 'vector', 'scalar', 'gpsimd', 'sync']
NDMA = 24
SAME_ENG_WINDOW = 10 ** 9

D = 1024
L = 4096
LC = 256
LT = L + LC
DFF = 2816
NKC = 8
NFC = 22
TT = 256
NTT = LT // TT
EPS = 1e-6
PI = math.pi
TILES512 = [(0, 256)] + [(256 + j * 512, 512) for j in range(L // 512)]
TWO_PI = 2.0 * math.pi


class Buf:
    def __init__(self, ap, name=''):
        self.ap = ap
        self.name = name
        self.w = None
        self.r = {}


class Prog:
    def __init__(self, nc, stack, arena_cols_f32=47 * 1024):
        self.nc = nc
        self.stack = stack
        self.q = {e: [] for e in ENG}
        self.cnt = {e: 0 for e in ENG}
        self.epoch = 0
        self.sems = {}
        self.seen = {e: {} for e in ENG}
        self.dma_sems = [stack.enter_context(nc.semaphore(f"dma{j}")) for j in range(NDMA)]
        self.dma_cnt = [0] * NDMA
        self.dma_pool = {'sync': list(range(0, NDMA // 2)), 'gpsimd': list(range(NDMA // 2, NDMA))}
        self.dma_rr = {'sync': 0, 'gpsimd': 0}
        self._new_epoch_sems()
        self.arena = stack.enter_context(nc.sbuf_tensor("arena", [128, arena_cols_f32], F32))
        self.arena_cols = arena_cols_f32
        self.bump = 0
        self.psum = [Buf(stack.enter_context(nc.psum_tensor(f"ps{i}", [128, 512], F32))[:, :], f"ps{i}")
                     for i in range(8)]
        self.ps_rr = 0
        self.n_instr = 0

    def alloc(self, cols, dtype=F32, name='', parts=128):
        nbytes = cols * (4 if dtype == F32 else 2)
        n32 = (nbytes + 3) // 4
        n32 = (n32 + 7) // 8 * 8
        assert self.bump + n32 <= self.arena_cols, f"SBUF arena overflow at {name}: {self.bump}+{n32}"
        v = self.arena[:, self.bump:self.bump + n32]
        self.bump += n32
        if dtype != F32:
            v = v.bitcast(dtype)
        v = v[0:parts, 0:cols]
        return Buf(v, name)

    def mark(self):
        return self.bump

    def release(self, mark):
        self.bump = mark

    def next_psum(self):
        b = self.psum[self.ps_rr]
        self.ps_rr = (self.ps_rr + 1) % 8
        return b

    def _new_epoch_sems(self):
        if not hasattr(self, 'semset'):
            self.semset = [{e: self.stack.enter_context(self.nc.semaphore(f"s_{e}_{j}")) for e in ENG}
                           for j in range(3)]
        for e in ENG:
            self.sems[(e, self.epoch)] = self.semset[self.epoch % 3][e]
        if self.epoch >= 2:
            for e in ENG:
                self.q[e].append(('clear', self.semset[(self.epoch + 1) % 3][e]))

    def _need(self, eng, ev, waits, raw):
        if ev is None:
            return
        if ev[0] == 'e':
            _, f, ep, k = ev
            if ep != self.epoch:
                return
            if f == eng:
                if eng == 'tensor':
                    return
                if k <= self.cnt[eng] - SAME_ENG_WINDOW:
                    return
            key = ('e', f, ep)
        else:
            _, j, k = ev
            key = ('d', j)
        if self.seen[eng].get(key, 0) >= k:
            return
        waits[key] = max(waits.get(key, 0), k)

    def _emit_waits(self, eng, waits):
        for key, k in waits.items():
            self.seen[eng][key] = k
            sem = self.sems[(key[1], key[2])] if key[0] == 'e' else self.dma_sems[key[1]]
            self.q[eng].append(('wait', sem, k))

    def _deps(self, eng, reads, writes):
        waits = {}
        for b in reads:
            self._need(eng, b.w, waits, True)
        for b in writes:
            self._need(eng, b.w, waits, False)
            for ev in b.r.values():
                self._need(eng, ev, waits, False)
        self._emit_waits(eng, waits)

    def _commit(self, ev, rkey, reads, writes):
        for b in writes:
            b.w = ev
            b.r = {}
        for b in reads:
            if b in writes:
                continue
            b.r[rkey] = ev

    def op(self, eng, fn, reads=(), writes=()):
        self._deps(eng, reads, writes)
        self.cnt[eng] += 1
        self.q[eng].append(('op', fn, self.sems[(eng, self.epoch)]))
        ev = ('e', eng, self.epoch, self.cnt[eng])
        self._commit(ev, ('e', eng), reads, writes)
        self.n_instr += 1
        return ev

    def dma(self, eng, out_ap, in_ap, reads=(), writes=(), **kw):
        self._deps(eng, reads, writes)
        pool = self.dma_pool[eng]
        j = pool[self.dma_rr[eng]]
        self.dma_rr[eng] = (self.dma_rr[eng] + 1) % len(pool)
        w = {}
        if self.dma_cnt[j] > 0:
            self._need(eng, ('d', j, self.dma_cnt[j]), w, True)
            self._emit_waits(eng, w)
        self.dma_cnt[j] += 16
        self.q[eng].append(('dma', out_ap, in_ap, kw, self.dma_sems[j]))
        ev = ('d', j, self.dma_cnt[j])
        self._commit(ev, ('d', j), reads, writes)
        self.n_instr += 1
        return ev

    def barrier(self):
        for e in ENG:
            waits = {}
            for f in ENG:
                if f != e and self.cnt[f] > 0:
                    ev = ('e', f, self.epoch, self.cnt[f])
                    self._need(e, ev, waits, True)
            for j in range(NDMA):
                if self.dma_cnt[j] > 0:
                    self._need(e, ('d', j, self.dma_cnt[j]), waits, True)
            self._emit_waits(e, waits)
        self.epoch += 1
        self._new_epoch_sems()
        for e in ENG:
            self.cnt[e] = 0

    def final_wait(self, eng='sync'):
        waits = {}
        for f in ENG:
            if f != eng and self.cnt[f] > 0:
                self._need(eng, ('e', f, self.epoch, self.cnt[f]), waits, True)
        for j in range(NDMA):
            if self.dma_cnt[j] > 0:
                self._need(eng, ('d', j, self.dma_cnt[j]), waits, True)
        self._emit_waits(eng, waits)

    def emit(self):
        nc = self.nc
        with nc.Block() as block:
            def run(engname):
                def body(e):
                    for item in self.q[engname]:
                        if item[0] == 'wait':
                            e.wait_ge(item[1], item[2])
                        elif item[0] == 'clear':
                            e.sem_clear(item[1])
                        elif item[0] == 'op':
                            item[1](e).then_inc(item[2], 1)
                        else:
                            _, o, i, kw, sem = item
                            e.dma_start(out=o, in_=i, **kw).then_inc(sem, 16)
                return body
            block.tensor(run('tensor'))
            block.vector(run('vector'))
            block.scalar(run('scalar'))
            block.gpsimd(run('gpsimd'))
            block.sync(run('sync'))


def sub(buf, ap, name=''):
    return Buf(ap, name or buf.name)


INPUT_SHAPES = {
    'x': [L, D], 'c': [1, D], 'ctx': [LC, D], 'c_ctx': [1, D],
    'ada_w': [4, D, 9 * D], 'ada_b': [4, 9 * D], 'norm_w': [4, 3, D],
    'ffn_w_in': [4, 2, D, 2 * DFF], 'ffn_w_out': [4, 2, DFF, D],
    's5_a_re': [2, 2, 64, 64], 's5_a_im': [2, 2, 64, 64], 's5_log_step': [2, 2, 64],
    's5_b_re': [2, 2, 64, 64, 16], 's5_b_im': [2, 2, 64, 64, 16],
    's5_c_re': [2, 2, 64, 16, 64], 's5_c_im': [2, 2, 64, 16, 64],
    's5_d': [2, D], 's5_w_glu': [2, D, 2 * D], 's5_b_glu': [2, 2 * D],
    'gla_w_in': [1, D, 3104], 'gla_w_gate': [1, 2, 16, 512], 'gla_b_gate': [1, 2, 512],
    'gla_norm_w': [1, 256], 'gla_w_out': [1, D, D],
    'hgrn_w_in': [1, D, 5 * D], 'hgrn_lb_logits': [4, 2, D], 'hgrn_norm_w': [1, 128],
    'hgrn_w_out': [1, D, D], 'final_norm_w': [1, D],
    'k_ident': [128, 128], 'k_iota': [128, LT], 'k_mask': [2, 64, 64],
}


class K:
    pass


def mod_col(i, m, c, s):
    return ((i * 9 + m) * 8 + c) * 2 + s


def build_nc(n_layers=4, mixers=True, dump_xT=False, layer_list=None):
    nc = bass.Bass("TRN2", target_bir_lowering=False)
    I = {n: nc.dram_tensor(n, list(s), F32, kind="ExternalInput").ap() for n, s in INPUT_SHAPES.items()}
    out = nc.dram_tensor("out", [L, D], F32, kind="ExternalOutput").ap()
    xT = nc.dram_tensor("xT_scr", [8, 128, LT], F32, kind="Internal").ap()
    hT_d = nc.dram_tensor("hT_scr", [8, 128, LT], BF16, kind="Internal").ap()
    yg_d = nc.dram_tensor("yg_scr", [8, 128, LT], BF16, kind="Internal").ap()
    pq_d = nc.dram_tensor("pq_scr", [8, 128, LT], BF16, kind="Internal").ap()
    pk_d = nc.dram_tensor("pk_scr", [2, 8, 128, LT], BF16, kind="Internal").ap()
    pg_d = nc.dram_tensor("pg_scr", [2, 8, 128, LT], F32, kind="Internal").ap()
    pgate_d = nc.dram_tensor("pgate_scr", [8, 128, LT], BF16, kind="Internal").ap()
    pv_d = nc.dram_tensor("pv_scr", [LT // 64, 64, 1024], BF16, kind="Internal").ap()
    if dump_xT:
        xdump = nc.dram_tensor("xdump", [8, 128, LT], F32, kind="ExternalOutput").ap()
        dbg = nc.dram_tensor("dbg", [128, 1024], F32, kind="ExternalOutput").ap()
        dbg2 = nc.dram_tensor("dbg2", [32, 128, 512], F32, kind="ExternalOutput").ap()

    with ExitStack() as st:
        P = Prog(nc, st)
        k = K()
        k.P, k.I, k.xT, k.hT_d, k.yg_d = P, I, xT, hT_d, yg_d
        k.dbg2 = dbg2 if dump_xT else None
        k.pq_d, k.pk_d, k.pg_d, k.pgate_d, k.pv_d = pq_d, pk_d, pg_d, pgate_d, pv_d
        k.pqb = [Buf(None) for _ in range(8)]
        k.pkb = [[Buf(None) for _ in range(8)] for _ in range(2)]
        k.pgb = [[Buf(None) for _ in range(8)] for _ in range(2)]
        k.pgateb = [Buf(None) for _ in range(8)]
        k.pvb = Buf(None)
        k.ogb = [Buf(None) for _ in range(8)]
        k.mask = P.alloc(128, F32, 'mask', parts=64)
        P.dma('sync', k.mask.ap.rearrange("p (d i) -> p d i", d=2), I['k_mask'].rearrange("d j i -> j d i"),
              writes=[k.mask])
        k.onec = P.alloc(1, F32, 'onec')
        P.op('gpsimd', lambda e: e.memset(k.onec.ap, 1.0), writes=[k.onec])
        k.dbg_n = 0
        k.xT_p = xT.rearrange("c p t -> p c t")
        k.hT_p = hT_d.rearrange("c p t -> p c t")
        k.yg_p = yg_d.rearrange("c p t -> p c t")
        k.xTb = [Buf(None, f"xT{t}") for t in range(NTT)]
        k.hTb = [Buf(None, f"hT{t}") for t in range(NTT)]
        k.ygb = [Buf(None, f"yg{t}") for t in range(NTT)]

        k.ident_f = P.alloc(128, F32, 'ident_f')
        k.ident_b = P.alloc(128, BF16, 'ident_b')
        k.ones_b = P.alloc(128, BF16, 'ones_b')
        k.modT = P.alloc(4 * 9 * 8 * 2, F32, 'modT')
        k.WS = P.alloc(4 * 3 * 2 * 8, F32, 'WS')
        k.HG = P.alloc(4 * 3 * 2 * 8, F32, 'HG')
        k.normT = P.alloc(4 * 3 * 8, F32, 'normT')
        k.fnT = P.alloc(8, F32, 'fnT')
        P.dma('sync', k.ident_f.ap, I['k_ident'], writes=[k.ident_f])
        P.op('vector', lambda e: e.tensor_copy(k.ident_b.ap, k.ident_f.ap), reads=[k.ident_f], writes=[k.ident_b])
        P.op('gpsimd', lambda e: e.memset(k.ones_b.ap, 1.0), writes=[k.ones_b])

        prologue(k)
        P.barrier()
        for i in (layer_list if layer_list is not None else range(n_layers)):
            ffn_phase(k, i, 0)
            P.barrier()
            if mixers:
                kind = i % 3
                mixer_prep(k, i, permute=(kind == 1))
                P.barrier()
                if kind == 0:
                    s5_phase(k, i)
                elif kind == 1:
                    gated_phase(k, i, 'gla')
                else:
                    gated_phase(k, i, 'hgrn')
                P.barrier()
            ffn_phase(k, i, 1)
            P.barrier()
        epilogue(k, out)
        if dump_xT:
            P.barrier()
            mk = P.mark()
            t = P.alloc(8 * 512, F32, 'dump')
            for n in range(0, LT, 512):
                w = min(512, LT - n)
                P.dma('sync', t.ap.rearrange("p (c t) -> p c t", c=8)[:, :, 0:w], k.xT_p[:, :, n:n + w], writes=[t])
                P.dma('sync', xdump.rearrange("c p t -> p c t")[:, :, n:n + w],
                      t.ap.rearrange("p (c t) -> p c t", c=8)[:, :, 0:w], reads=[t])
            P.release(mk)
            P.dma('sync', dbg[:, 0:576], k.modT.ap, reads=[k.modT])
            P.dma('sync', dbg[:, 576:768], k.WS.ap, reads=[k.WS])
            P.dma('sync', dbg[:, 768:960], k.HG.ap, reads=[k.HG])
        P.final_wait('sync')
        P.emit()
        k.n_instr = P.n_instr
    return nc


def dbg_dump(k, buf, ap=None, label=''):
    if k.dbg2 is None or k.dbg_n >= 32:
        return
    ap = buf.ap if ap is None else ap
    pp, cc = ap.shape[0], ap.shape[1]
    print('DBG slot', k.dbg_n, label, pp, cc)
    k.P.dma('gpsimd', k.dbg2[k.dbg_n, 0:pp, 0:cc], ap, reads=[buf])
    k.dbg_n += 1


def slow_dma(P, out_ap, in_ap, **kw):
    return P.dma('sync', out_ap, in_ap, allow_slow_non_contiguous=True, **kw)


def prologue(k):
    P, I = k.P, k.I
    mk = P.mark()
    adabT = P.alloc(4 * 72, F32, 'adabT')
    for i in range(4):
        slow_dma(P, adabT.ap[:, i * 72:(i + 1) * 72], I['ada_b'][i].rearrange("(m p) -> p m", p=128), writes=[adabT])
    slow_dma(P, k.normT.ap, I['norm_w'].rearrange("i j (c p) -> p (i j c)", p=128), writes=[k.normT])
    slow_dma(P, k.fnT.ap, I['final_norm_w'].rearrange("o (c p) -> p (o c)", p=128), writes=[k.fnT])
    cs32 = P.alloc(16, F32, 'cs32')
    csv = cs32.ap.rearrange("p (k s) -> p k s", s=2)
    slow_dma(P, csv[:, :, 0], I['c'].rearrange("o (k p) -> p (o k)", p=128), writes=[cs32])
    slow_dma(P, csv[:, :, 1], I['c_ctx'].rearrange("o (k p) -> p (o k)", p=128), writes=[cs32])
    csb = P.alloc(16, BF16, 'csb')
    P.op('scalar', lambda e: e.activation(csb.ap, cs32.ap, AF.Silu), reads=[cs32], writes=[csb])

    Wa = [P.alloc(8 * 1024, BF16, f'Wa{j}') for j in range(2)]
    n = 0
    for i in range(4):
        for m in range(9):
            W = Wa[n % 2]
            n += 1
            P.dma('gpsimd', W.ap.rearrange("p (k n) -> p k n", k=8),
                  I['ada_w'][i].rearrange("(k p) n -> p k n", p=128)[:, :, m * 1024:(m + 1) * 1024], writes=[W])
            ps = P.next_psum()
            for oc in range(8):
                for kc in range(8):
                    P.op('tensor', lambda e, W=W, ps=ps, oc=oc, kc=kc: e.matmul(
                        ps.ap[:, oc * 2:oc * 2 + 2], W.ap[:, kc * 1024 + oc * 128: kc * 1024 + (oc + 1) * 128],
                        csb.ap[:, kc * 2:kc * 2 + 2], start=(kc == 0), stop=(kc == 7)),
                        reads=[W, csb], writes=[ps])
            base = mod_col(i, m, 0, 0)
            for s in range(2):
                P.op('vector', lambda e, ps=ps, s=s, base=base, i=i, m=m: e.tensor_tensor(
                    k.modT.ap[:, base + s: base + 16: 2], ps.ap[:, s:16:2],
                    adabT.ap[:, (i * 9 + m) * 8:(i * 9 + m) * 8 + 8], ALU.add),
                    reads=[ps, adabT], writes=[k.modT])
    for i in range(4):
        for j in range(3):
            for s in range(2):
                col = ((i * 3 + j) * 2 + s) * 8
                b_scale = mod_col(i, 3 * j + 1, 0, s)
                b_gate = mod_col(i, 3 * j + 2, 0, s)
                P.op('vector', lambda e, col=col, b=b_scale, i=i, j=j: e.scalar_tensor_tensor(
                    k.WS.ap[:, col:col + 8], k.modT.ap[:, b:b + 15:2], 1.0,
                    k.normT.ap[:, (i * 3 + j) * 8:(i * 3 + j) * 8 + 8], ALU.add, ALU.mult),
                    reads=[k.modT, k.normT], writes=[k.WS])
                P.op('vector', lambda e, col=col, b=b_gate, j=j: e.tensor_scalar(
                    k.HG.ap[:, col:col + 8], k.modT.ap[:, b:b + 15:2], (1.0 if j == 1 else 0.5), None, ALU.mult),
                    reads=[k.modT], writes=[k.HG])

    xin = [P.alloc(1024, F32, f'xin{j}') for j in range(2)]
    stg = [P.alloc(1024, F32, f'stg{j}') for j in range(2)]
    for blk in range(LT // 128):
        src = I['ctx'][blk * 128:(blk + 1) * 128, :] if blk < 2 else I['x'][(blk - 2) * 128:(blk - 1) * 128, :]
        xi = xin[blk % 2]
        sg = stg[blk % 2]
        P.dma('sync', xi.ap, src, writes=[xi])
        for h in range(2):
            ps = P.next_psum()
            for jj in range(4):
                c = h * 4 + jj
                P.op('tensor', lambda e, ps=ps, xi=xi, c=c, jj=jj: e.matmul(
                    ps.ap[:, jj * 128:(jj + 1) * 128], xi.ap[:, c * 128:(c + 1) * 128], k.ident_f.ap,
                    start=True, stop=True), reads=[xi, k.ident_f], writes=[ps])
            eng = 'vector' if h == 0 else 'scalar'
            if h == 0:
                P.op('vector', lambda e, ps=ps, sg=sg: e.tensor_copy(sg.ap[:, 0:512], ps.ap), reads=[ps], writes=[sg])
            else:
                P.op('scalar', lambda e, ps=ps, sg=sg: e.copy(sg.ap[:, 512:1024], ps.ap), reads=[ps], writes=[sg])
        P.dma('sync', k.xT_p[:, :, blk * 128:(blk + 1) * 128], sg.ap.rearrange("p (c t) -> p c t", c=8),
              reads=[sg], writes=[k.xTb[blk // 2]])
    P.release(mk)


def WS_ap(k, i, j, s, c):
    col = ((i * 3 + j) * 2 + s) * 8 + c
    return k.WS.ap[:, col:col + 1]


def HG_ap(k, i, j, s, c):
    col = ((i * 3 + j) * 2 + s) * 8 + c
    return k.HG.ap[:, col:col + 1]


def SH_ap(k, i, j, s, c):
    col = mod_col(i, 3 * j, c, s)
    return k.modT.ap[:, col:col + 1]


def alloc_norm_scratch(k, T):
    P = k.P
    k.nT = T
    sq = P.alloc(8 * T, BF16, 'sq')
    k.sq_c = [sub(sq, sq.ap[:, c * T:(c + 1) * T]) for c in range(8)]
    k.rstd = P.alloc(T, F32, 'rstd')
    k.ntmp = [P.alloc(T, F32, f'ntmp{j}') for j in range(2)]


def norm_tile(k, xt_c, hT_c, ws, sh, T, extra_reads=()):
    P = k.P
    sq_c, rstd, ntmp = k.sq_c, k.rstd, k.ntmp
    wsa = [ws(c) for c in range(8)]
    sha = [sh(c) for c in range(8)] if sh is not None else None
    for c in range(8):
        P.op('scalar', lambda e, c=c: e.activation(sq_c[c].ap[:, :T], xt_c[c].ap[:, :T], AF.Square),
             reads=[xt_c[c]], writes=[sq_c[c]])
    ps = P.next_psum()
    for c in range(8):
        P.op('tensor', lambda e, c=c, ps=ps: e.matmul(ps.ap[:, :T], k.ones_b.ap, sq_c[c].ap[:, :T],
                                                      start=(c == 0), stop=(c == 7)),
             reads=[sq_c[c], k.ones_b], writes=[ps])
    P.op('scalar', lambda e, ps=ps: e.activation(rstd.ap[:, :T], ps.ap[:, :T], AF.Sqrt, bias=EPS, scale=1.0 / D),
         reads=[ps], writes=[rstd])
    P.op('vector', lambda e: e.reciprocal(rstd.ap[:, :T], rstd.ap[:, :T]), reads=[rstd], writes=[rstd])
    for c in range(8):
        tmp = ntmp[c % 2]
        P.op('gpsimd', lambda e, c=c, tmp=tmp: e.tensor_tensor(
            tmp.ap[:, :T], xt_c[c].ap[:, :T], rstd.ap[:, :T], ALU.mult),
            reads=[xt_c[c], rstd], writes=[tmp])
        if sh is None:
            P.op('scalar', lambda e, c=c, tmp=tmp: e.activation(
                hT_c[c].ap[:, :T], tmp.ap[:, :T], AF.Identity, scale=wsa[c]),
                reads=[tmp, k.WS, k.fnT], writes=[hT_c[c]])
        else:
            P.op('scalar', lambda e, c=c, tmp=tmp: e.activation(
                hT_c[c].ap[:, :T], tmp.ap[:, :T], AF.Identity, bias=sha[c], scale=wsa[c]),
                reads=[tmp, k.modT, k.WS, k.fnT], writes=[hT_c[c]])


def ffn_phase(k, i, jf):
    P, I = k.P, k.I
    mk = P.mark()
    jn = 0 if jf == 0 else 2
    T = 512
    Win = P.alloc(8 * 5632, BF16, 'Win')
    Wout = P.alloc(22 * 1024, BF16, 'Wout')
    Win_k = [sub(Win, Win.ap[:, kc * 5632:(kc + 1) * 5632]) for kc in range(8)]
    Wout_f = [sub(Wout, Wout.ap[:, f * 1024:(f + 1) * 1024]) for f in range(22)]
    for kc in range(8):
        for q4 in range(4):
            P.dma('gpsimd', Win_k[kc].ap[:, q4 * 1408:(q4 + 1) * 1408],
                  I['ffn_w_in'][i, jf, kc * 128:(kc + 1) * 128, q4 * 1408:(q4 + 1) * 1408], writes=[Win_k[kc]])
    for f in range(22):
        P.dma('gpsimd', Wout_f[f].ap, I['ffn_w_out'][i, jf, f * 128:(f + 1) * 128, :], writes=[Wout_f[f]])
    actb = P.alloc(22 * T // 2, F32, 'actb')
    act_ap = actb.ap.bitcast(BF16)
    hT = P.alloc(8 * T, BF16, 'hT')
    hT_c = [sub(hT, hT.ap[:, c * T:(c + 1) * T]) for c in range(8)]
    rstd = P.alloc(T, F32, 'rstd')
    sg = [P.alloc(T, F32, f'sg{j}') for j in range(2)]
    xsm = [P.alloc(T, F32, f'xsm{j}') for j in range(2)]

    for tix, (t0, w) in enumerate(TILES512):
        s = 1 if tix == 0 else 0
        xv = actb.ap[:, 0:8 * T].rearrange("p (c t) -> p c t", c=8)
        P.dma('sync', xv[:, :, 0:w], k.xT_p[:, :, t0:t0 + w], writes=[actb])
        for c in range(8):
            P.op('scalar', lambda e, c=c, w=w: e.activation(hT_c[c].ap[:, :w], actb.ap[:, c * T:c * T + w], AF.Square),
                 reads=[actb], writes=[hT_c[c]])
        ps = P.next_psum()
        for c in range(8):
            P.op('tensor', lambda e, c=c, ps=ps, w=w: e.matmul(ps.ap[:, :w], k.ones_b.ap, hT_c[c].ap[:, :w],
                                                               start=(c == 0), stop=(c == 7)),
                 reads=[hT_c[c], k.ones_b], writes=[ps])
        P.op('scalar', lambda e, ps=ps, w=w: e.activation(rstd.ap[:, :w], ps.ap[:, :w], AF.Sqrt, bias=EPS, scale=1.0 / D),
             reads=[ps], writes=[rstd])
        P.op('vector', lambda e, w=w: e.reciprocal(rstd.ap[:, :w], rstd.ap[:, :w]), reads=[rstd], writes=[rstd])
        for c in range(8):
            tmp = sg[c % 2]
            ws_ap, sh_ap = WS_ap(k, i, jn, s, c), SH_ap(k, i, jn, s, c)
            P.op('gpsimd', lambda e, c=c, tmp=tmp, w=w: e.tensor_tensor(
                tmp.ap[:, :w], actb.ap[:, c * T:c * T + w], rstd.ap[:, :w], ALU.mult),
                reads=[actb, rstd], writes=[tmp])
            P.op('scalar', lambda e, c=c, tmp=tmp, w=w, ws_ap=ws_ap, sh_ap=sh_ap: e.activation(
                hT_c[c].ap[:, :w], tmp.ap[:, :w], AF.Identity, bias=sh_ap, scale=ws_ap),
                reads=[tmp, k.modT, k.WS], writes=[hT_c[c]])
        for f in range(22):
            pg, pu = P.next_psum(), P.next_psum()
            for (pp, off) in ((pg, 0), (pu, DFF)):
                for kc in range(8):
                    P.op('tensor', lambda e, pp=pp, off=off, kc=kc, f=f, w=w: e.matmul(
                        pp.ap[:, :w], Win_k[kc].ap[:, off + f * 128: off + (f + 1) * 128], hT_c[kc].ap[:, :w],
                        start=(kc == 0), stop=(kc == 7)), reads=[Win_k[kc], hT_c[kc]], writes=[pp])
            sgb = sg[f % 2]
            P.op('scalar', lambda e, pg=pg, sgb=sgb, w=w: e.activation(sgb.ap[:, :w], pg.ap[:, :w], AF.Silu),
                 reads=[pg], writes=[sgb])
            P.op('vector', lambda e, pu=pu, sgb=sgb, f=f, w=w: e.tensor_tensor(
                act_ap[:, f * T:f * T + w], sgb.ap[:, :w], pu.ap[:, :w], ALU.mult),
                reads=[pu, sgb], writes=[actb])
        for c in range(8):
            xs = xsm[c % 2]
            P.dma('sync', xs.ap[:, :w], k.xT[c, :, t0:t0 + w], writes=[xs])
            po = P.next_psum()
            for f in range(22):
                P.op('tensor', lambda e, po=po, f=f, c=c, w=w: e.matmul(
                    po.ap[:, :w], Wout_f[f].ap[:, c * 128:(c + 1) * 128], act_ap[:, f * T:f * T + w],
                    start=(f == 0), stop=(f == 21)), reads=[Wout_f[f], actb], writes=[po])
            hg_ap = HG_ap(k, i, jn, s, c)
            P.op('vector', lambda e, po=po, xs=xs, w=w, hg_ap=hg_ap: e.scalar_tensor_tensor(
                xs.ap[:, :w], po.ap[:, :w], hg_ap, xs.ap[:, :w], ALU.mult, ALU.add),
                reads=[po, xs, k.HG], writes=[xs])
            P.dma('sync', k.xT[c, :, t0:t0 + w], xs.ap[:, :w], reads=[xs])
    P.release(mk)


def epilogue(k, out):
    P = k.P
    mk = P.mark()
    T = TT
    alloc_norm_scratch(k, T)
    xt = [P.alloc(8 * T, F32, f'ext{j}') for j in range(2)]
    xt_c = [[sub(b, b.ap[:, c * T:(c + 1) * T]) for c in range(8)] for b in xt]
    yT = [P.alloc(8 * T, F32, f'eyT{j}') for j in range(2)]
    yT_c = [[sub(b, b.ap[:, c * T:(c + 1) * T]) for c in range(8)] for b in yT]
    stg = [P.alloc(1024, F32, f'estg{j}') for j in range(2)]
    n = 0
    for tix in range(1, NTT):
        t0 = tix * T
        X, Xc, Yc = xt[tix % 2], xt_c[tix % 2], yT_c[tix % 2]
        P.dma('sync', X.ap.rearrange("p (c t) -> p c t", c=8), k.xT_p[:, :, t0:t0 + T],
              reads=[k.xTb[tix]], writes=Xc)
        norm_tile(k, Xc, Yc, lambda c: k.fnT.ap[:, c:c + 1], None, T)
        for tb in range(T // 128):
            sgb = stg[n % 2]
            n += 1
            for h in range(2):
                ps = P.next_psum()
                for jj in range(4):
                    c = h * 4 + jj
                    P.op('tensor', lambda e, ps=ps, c=c, jj=jj, tb=tb, Yc=Yc: e.matmul(
                        ps.ap[:, jj * 128:(jj + 1) * 128], Yc[c].ap[:, tb * 128:(tb + 1) * 128], k.ident_f.ap,
                        start=True, stop=True), reads=[Yc[c], k.ident_f], writes=[ps])
                if h == 0:
                    P.op('vector', lambda e, ps=ps, sgb=sgb: e.tensor_copy(sgb.ap[:, 0:512], ps.ap),
                         reads=[ps], writes=[sgb])
                else:
                    P.op('scalar', lambda e, ps=ps, sgb=sgb: e.copy(sgb.ap[:, 512:1024], ps.ap),
                         reads=[ps], writes=[sgb])
            r0 = t0 - LC + tb * 128
            P.dma('sync', out[r0:r0 + 128, :], sgb.ap, reads=[sgb])
    P.release(mk)


def mixer_prep(k, i, permute):
    P = k.P
    mk = P.mark()
    T = TT
    alloc_norm_scratch(k, T)
    xt = [P.alloc(8 * T, F32, f'pxt{j}') for j in range(2)]
    xt_c = [[sub(b, b.ap[:, c * T:(c + 1) * T]) for c in range(8)] for b in xt]
    if permute:
        hall = P.alloc(8 * L, BF16, 'hall')
        hc0 = P.alloc(8 * T, BF16, 'hc0')
        hc0_c = [sub(hc0, hc0.ap[:, c * T:(c + 1) * T]) for c in range(8)]
        hall_c = [sub(hall, hall.ap[:, c * L:(c + 1) * L]) for c in range(8)]
        perm = [P.alloc(L, BF16, f'perm{j}') for j in range(2)]
    else:
        hT = [P.alloc(8 * T, BF16, f'phT{j}') for j in range(2)]
        hT_c = [[sub(b, b.ap[:, c * T:(c + 1) * T]) for c in range(8)] for b in hT]
    for tix in range(NTT):
        s = 1 if tix == 0 else 0
        t0 = tix * T
        X, Xc = xt[tix % 2], xt_c[tix % 2]
        P.dma('sync', X.ap.rearrange("p (c t) -> p c t", c=8), k.xT_p[:, :, t0:t0 + T],
              reads=[k.xTb[tix]], writes=Xc)
        if permute and tix > 0:
            Hc = [Buf(hall_c[c].ap[:, t0 - LC:t0 - LC + T]) for c in range(8)]
        elif permute:
            Hc = hc0_c
        else:
            Hc = hT_c[tix % 2]
        norm_tile(k, Xc, Hc, lambda c, s=s: WS_ap(k, i, 1, s, c), lambda c, s=s: SH_ap(k, i, 1, s, c), T)
        if permute and tix > 0:
            for c in range(8):
                hall_c[c].w = Hc[c].w
        elif permute:
            P.dma('sync', k.hT_p[:, :, 0:T], hc0.ap.rearrange("p (c t) -> p c t", c=8), reads=hc0_c,
                  writes=[k.hTb[0]])
        else:
            P.dma('sync', k.hT_p[:, :, t0:t0 + T], hT[tix % 2].ap.rearrange("p (c t) -> p c t", c=8),
                  reads=Hc, writes=[k.hTb[tix]])
    if permute:
        for c in range(8):
            pb = perm[c % 2]
            eng = 'vector' if c % 2 == 0 else 'gpsimd'
            P.op(eng, lambda e, c=c, pb=pb: e.tensor_copy(
                pb.ap.rearrange("p (cc r) -> p cc r", r=64),
                hall_c[c].ap.rearrange("p (r cc) -> p cc r", cc=64)), reads=[hall_c[c]], writes=[pb])
            P.dma('sync', k.hT_d[c, :, LC:LT], pb.ap, reads=[pb], writes=k.hTb[1:])
    P.release(mk)


def s5_phase(k, i):
    P, I = k.P, k.I
    js = i // 3
    mk = P.mark()
    V, G, A = 'vector', 'gpsimd', 'scalar'

    def tt(out, a, b, op, eng=V):
        P.op(eng, lambda e: e.tensor_tensor(out.ap, a.ap, b.ap, op), reads=[a, b], writes=[out])

    def ts(out, a, s1, op0, s2=None, op1=None, eng=V):
        if op1 is None:
            P.op(eng, lambda e: e.tensor_scalar(out.ap, a.ap, s1, None, op0), reads=[a], writes=[out])
        else:
            P.op(eng, lambda e: e.tensor_scalar(out.ap, a.ap, s1, s2, op0, op1), reads=[a], writes=[out])

    def act(out, a, func, scale=1.0):
        P.op(A, lambda e: e.activation(out.ap, a.ap, func, scale=scale), reads=[a], writes=[out])

    def sm(name=''):
        return P.alloc(32, F32, name)

    tmp = [sm(f't{j}') for j in range(4)]

    def csq(outr, outi, r, im):
        tt(tmp[0], r, r, ALU.mult)
        tt(tmp[1], im, im, ALU.mult)
        tt(outr, tmp[0], tmp[1], ALU.subtract)
        tt(tmp[2], r, im, ALU.mult)
        ts(outi, tmp[2], 2.0, ALU.mult)

    NLEV = 13
    par = []
    for d in range(2):
        are, aim, ls = sm(), sm(), sm()
        slow_dma(P, are.ap, I['s5_a_re'][js, d].rearrange("(q g2) p -> (g2 p) q", g2=2), writes=[are])
        slow_dma(P, aim.ap, I['s5_a_im'][js, d].rearrange("(q g2) p -> (g2 p) q", g2=2), writes=[aim])
        for g2 in range(2):
            slow_dma(P, ls.ap[g2 * 64:(g2 + 1) * 64, :],
                     I['s5_log_step'][js, d].rearrange("(q g2) -> g2 q", g2=2)[g2].partition_broadcast(64),
                     writes=[ls])
        dt, xr, th, rho = sm(), sm(), sm(), sm()
        act(dt, ls, AF.Exp)
        tt(xr, are, dt, ALU.mult)
        tt(th, aim, dt, ALU.mult)
        act(rho, xr, AF.Exp)
        zr, zi, th2 = sm(), sm(), sm()
        act(zi, th, AF.Sin, scale=1.0 / 16)
        ts(th2, th, 1.0 / 16, ALU.mult, PI / 2, ALU.add)
        act(zr, th2, AF.Sin)
        for _ in range(4):
            nr, ni = sm(), sm()
            csq(nr, ni, zr, zi)
            zr, zi = nr, ni
        cr, ci = zr, zi
        abr, abi, den, cfr, cfi = sm(), sm(), sm(), sm(), sm()
        tt(abr, rho, cr, ALU.mult)
        tt(abi, rho, ci, ALU.mult)
        tt(tmp[0], are, are, ALU.mult)
        tt(tmp[1], aim, aim, ALU.mult)
        tt(den, tmp[0], tmp[1], ALU.add)
        P.op(V, lambda e, den=den: e.reciprocal(den.ap, den.ap), reads=[den], writes=[den])
        zr_ = sm()
        ts(zr_, abr, -1.0, ALU.add)
        tt(tmp[0], zr_, are, ALU.mult)
        tt(tmp[1], abi, aim, ALU.mult)
        tt(tmp[2], tmp[0], tmp[1], ALU.add)
        tt(cfr, tmp[2], den, ALU.mult)
        tt(tmp[0], abi, are, ALU.mult)
        tt(tmp[1], zr_, aim, ALU.mult)
        tt(tmp[2], tmp[0], tmp[1], ALU.subtract)
        tt(cfi, tmp[2], den, ALU.mult)
        U = []
        ur, ui = cr, sm()
        ts(ui, ci, -1.0, ALU.mult)
        for m in range(NLEV):
            nui = sm()
            ts(nui, ui, -1.0, ALU.mult)
            U.append((ur, ui, nui))
            if m < NLEV - 1:
                nr, ni = sm(), sm()
                csq(nr, ni, ur, ui)
                ur, ui = nr, ni
        par.append(dict(rho=rho, cfr=cfr, cfi=cfi, U=U))

    Bn = [[P.alloc(32 * 32, F32, f'Bn{d}{r}') for r in range(2)] for d in range(2)]
    Cn = [[P.alloc(32 * 32, F32, f'Cn{d}{r}') for r in range(2)] for d in range(2)]
    mk2 = P.mark()
    Xz = P.alloc(8 * 128, F32, 'Xz')
    for d in range(2):
        for r, nm in enumerate(('s5_b_re', 's5_b_im')):
            T_ = Bn[d][r]
            P.op(G, lambda e, T_=T_: e.memset(T_.ap, 0.0), writes=[T_])
            v = T_.ap.rearrange("p (q c) -> p q c", c=32)
            for g2 in range(2):
                slow_dma(P, v[g2 * 64:(g2 + 1) * 64, :, g2 * 16:(g2 + 1) * 16],
                         I[nm][js, d].rearrange("(q g2) p c -> g2 p q c", g2=2)[g2], writes=[T_])
        for r, nm in enumerate(('s5_c_re', 's5_c_im')):
            T_ = Cn[d][r]
            for q8 in range(4):
                P.op(G, lambda e: e.memset(Xz.ap, 0.0), writes=[Xz])
                xv = Xz.ap.rearrange("p (q m) -> p q m", m=128)
                for g2 in range(2):
                    slow_dma(P, xv[g2 * 16:(g2 + 1) * 16, :, g2 * 64:(g2 + 1) * 64],
                             I[nm][js, d].rearrange("(q g2) c p -> g2 c q p", g2=2)[g2][:, q8 * 8:(q8 + 1) * 8, :],
                             writes=[Xz])
                ps = P.next_psum()
                for q in range(8):
                    P.op('tensor', lambda e, ps=ps, q=q: e.matmul(
                        ps.ap[:, q * 32:(q + 1) * 32], Xz.ap[0:32, q * 128:(q + 1) * 128], k.ident_f.ap[0:32, 0:32],
                        start=True, stop=True), reads=[Xz, k.ident_f], writes=[ps])
                P.op(V, lambda e, ps=ps, T_=T_, q8=q8: e.tensor_copy(
                    T_.ap[:, q8 * 256:(q8 + 1) * 256], ps.ap[:, 0:256]), reads=[ps], writes=[T_])
    P.release(mk2)
    P.barrier()

    Bw = [[P.alloc(128, BF16, f'Bw{q}{r}') for r in range(2)] for q in range(4)]
    Cw = [[P.alloc(128, BF16, f'Cw{q}{r}') for r in range(2)] for q in range(4)]
    for q in range(4):
        for r in range(2):
            P.op(G, lambda e, b=Bw[q][r]: e.memset(b.ap, 0.0), writes=[Bw[q][r]])
            P.op(G, lambda e, b=Cw[q][r]: e.memset(b.ap, 0.0), writes=[Cw[q][r]])
    BT = [[P.alloc(128, BF16, f'BT{j}{r}') for r in range(2)] for j in range(2)]
    ctmp = [P.alloc(32, F32, f'ctmp{j}') for j in range(3)]

    uT = [P.alloc(LT, BF16, f'uT{j}') for j in range(2)]
    Tr = P.alloc(LT, F32, 'Tr')
    Ti = P.alloc(LT, F32, 'Ti')
    yT = P.alloc(LT, F32, 'yT')
    WT = 2048
    prs, pis = P.alloc(WT, F32, 'prs'), P.alloc(WT, F32, 'pis')
    t1, t2, t3, t4 = (P.alloc(WT, F32, f't{j}w') for j in range(4))
    hr, hi = P.alloc(WT, BF16, 'hr'), P.alloc(WT, BF16, 'hi')
    car = [P.alloc(1, F32, f'car{j}') for j in range(2)]
    ttmp = [t1, t2]
    dT = P.alloc(8, F32, 'dT')
    slow_dma(P, dT.ap, I['s5_d'][js].rearrange("(c p) -> p c", p=128), writes=[dT])
    tiles3 = [(0, 256), (256, 2048), (2304, 2048)]

    def rv(ap, rev):
        return ap[:, ::-1] if rev else ap

    for fc in range(8):
        u = uT[fc % 2]
        P.dma('sync', u.ap, k.hT_d[fc], reads=k.hTb, writes=[u])
        for d in range(2):
            pr_ = par[d]
            for ql in range(4):
                pq = fc * 4 + ql
                blk = slice(pq * 32, pq * 32 + 32)
                for r in range(2):
                    P.op(G, lambda e, r=r, ql=ql, blk=blk, d=d: e.tensor_copy(
                        Bw[ql][r].ap[:, ql * 32:ql * 32 + 32], Bn[d][r].ap[:, blk]),
                        reads=[Bn[d][r]], writes=[Bw[ql][r]])
                bt = BT[pq % 2]
                for r in range(2):
                    ps = P.next_psum()
                    P.op('tensor', lambda e, ps=ps, r=r, ql=ql: e.matmul(
                        ps.ap[:, 0:128], Bw[ql][r].ap, k.ident_b.ap, start=True, stop=True),
                        reads=[Bw[ql][r], k.ident_b], writes=[ps])
                    P.op(A, lambda e, ps=ps, r=r, bt=bt: e.copy(bt[r].ap, ps.ap[:, 0:128]),
                         reads=[ps], writes=[bt[r]])
                cfr_ap = pr_['cfr'].ap[:, pq:pq + 1]
                cfi_ap = pr_['cfi'].ap[:, pq:pq + 1]
                cre, cim = Cn[d][0], Cn[d][1]
                P.op(V, lambda e, cim=cim, blk=blk, cfi_ap=cfi_ap: e.tensor_scalar(
                    ctmp[0].ap, cim.ap[:, blk], cfi_ap, None, ALU.mult), reads=[cim, pr_['cfi']], writes=[ctmp[0]])
                P.op(V, lambda e, cre=cre, blk=blk, cfr_ap=cfr_ap, ql=ql: e.scalar_tensor_tensor(
                    Cw[ql][0].ap[:, ql * 32:ql * 32 + 32], cre.ap[:, blk], cfr_ap, ctmp[0].ap, ALU.mult, ALU.subtract),
                    reads=[cre, ctmp[0], pr_['cfr']], writes=[Cw[ql][0]])
                P.op(V, lambda e, cim=cim, blk=blk, cfr_ap=cfr_ap: e.tensor_scalar(
                    ctmp[1].ap, cim.ap[:, blk], cfr_ap, None, ALU.mult), reads=[cim, pr_['cfr']], writes=[ctmp[1]])
                P.op(V, lambda e, cre=cre, blk=blk, cfi_ap=cfi_ap: e.scalar_tensor_tensor(
                    ctmp[2].ap, cre.ap[:, blk], cfi_ap, ctmp[1].ap, ALU.mult, ALU.add),
                    reads=[cre, ctmp[1], pr_['cfi']], writes=[ctmp[2]])
                P.op(V, lambda e, ql=ql: e.tensor_scalar(
                    Cw[ql][1].ap[:, ql * 32:ql * 32 + 32], ctmp[2].ap, -1.0, None, ALU.mult),
                    reads=[ctmp[2]], writes=[Cw[ql][1]])
                P.op(G, lambda e: e.memset(Tr.ap[:, 0:1], 1.0), writes=[Tr])
                P.op(G, lambda e: e.memset(Ti.ap[:, 0:1], 0.0), writes=[Ti])
                for m in range(NLEV):
                    n = 1 << m
                    cnt = min(n, LT - n)
                    ur, ui, nui = pr_['U'][m]
                    ur_ap, ui_ap, nui_ap = ur.ap[:, pq:pq + 1], ui.ap[:, pq:pq + 1], nui.ap[:, pq:pq + 1]
                    for c0 in range(0, cnt, 2048):
                        cw = min(2048, cnt - c0)
                        a0, a1 = c0, c0 + cw
                        b0, b1 = n + c0, n + c0 + cw
                        if cw >= 256:
                            P.op(A, lambda e, a0=a0, a1=a1, cw=cw, nui_ap=nui_ap: e.activation(
                                ttmp[0].ap[:, 0:cw], Ti.ap[:, a0:a1], AF.Copy, scale=nui_ap),
                                reads=[Ti, nui], writes=[ttmp[0]])
                        else:
                            P.op(V, lambda e, a0=a0, a1=a1, cw=cw, nui_ap=nui_ap: e.tensor_scalar(
                                ttmp[0].ap[:, 0:cw], Ti.ap[:, a0:a1], nui_ap, None, ALU.mult),
                                reads=[Ti, nui], writes=[ttmp[0]])
                        P.op(V, lambda e, a0=a0, a1=a1, b0=b0, b1=b1, cw=cw, ur_ap=ur_ap: e.scalar_tensor_tensor(
                            Tr.ap[:, b0:b1], Tr.ap[:, a0:a1], ur_ap, ttmp[0].ap[:, 0:cw], ALU.mult, ALU.add),
                            reads=[Tr, ttmp[0], ur], writes=[Tr])
                        if cw >= 256:
                            P.op(A, lambda e, a0=a0, a1=a1, cw=cw, ur_ap=ur_ap: e.activation(
                                ttmp[1].ap[:, 0:cw], Ti.ap[:, a0:a1], AF.Copy, scale=ur_ap),
                                reads=[Ti, ur], writes=[ttmp[1]])
                        else:
                            P.op(V, lambda e, a0=a0, a1=a1, cw=cw, ur_ap=ur_ap: e.tensor_scalar(
                                ttmp[1].ap[:, 0:cw], Ti.ap[:, a0:a1], ur_ap, None, ALU.mult),
                                reads=[Ti, ur], writes=[ttmp[1]])
                        P.op(V, lambda e, a0=a0, a1=a1, b0=b0, b1=b1, cw=cw, ui_ap=ui_ap: e.scalar_tensor_tensor(
                            Ti.ap[:, b0:b1], Tr.ap[:, a0:a1], ui_ap, ttmp[1].ap[:, 0:cw], ALU.mult, ALU.add),
                            reads=[Tr, ttmp[1], ui], writes=[Ti])
                rho_ap = pr_['rho'].ap[:, pq:pq + 1]
                for tix, (tau0, W) in enumerate(tiles3):
                    n0, n1, rev = nat_range(d, tau0, W)
                    subs = [(x0, min(512, W - x0)) for x0 in range(0, W, 512)]
                    for (x0, w) in subs:
                        pA, pB = P.next_psum(), P.next_psum()
                        for (pp, r, dst) in ((pA, 0, prs), (pB, 1, pis)):
                            P.op('tensor', lambda e, pp=pp, r=r, bt=bt, a=n0 + x0, w=w, u=u: e.matmul(
                                pp.ap[:, 0:w], bt[r].ap, u.ap[:, a:a + w], start=True, stop=True),
                                reads=[bt[r], u], writes=[pp])
                            P.op(A, lambda e, pp=pp, dst=dst, x0=x0, w=w: e.copy(dst.ap[:, x0:x0 + w], pp.ap[:, 0:w]),
                                 reads=[pp], writes=[dst])
                    trs, tis = Tr.ap[:, tau0:tau0 + W], Ti.ap[:, tau0:tau0 + W]
                    prv, piv = rv(prs.ap[:, 0:W], rev), rv(pis.ap[:, 0:W], rev)
                    P.op(V, lambda e, W=W, prv=prv, trs=trs: e.tensor_tensor(t1.ap[:, 0:W], prv, trs, ALU.mult),
                         reads=[prs, Tr], writes=[t1])
                    P.op(V, lambda e, W=W, piv=piv, tis=tis: e.tensor_tensor(t2.ap[:, 0:W], piv, tis, ALU.mult),
                         reads=[pis, Ti], writes=[t2])
                    P.op(V, lambda e, W=W, piv=piv, trs=trs: e.tensor_tensor(t3.ap[:, 0:W], piv, trs, ALU.mult),
                         reads=[pis, Tr], writes=[t3])
                    P.op(V, lambda e, W=W, prv=prv, tis=tis: e.tensor_tensor(t4.ap[:, 0:W], prv, tis, ALU.mult),
                         reads=[prs, Ti], writes=[t4])
                    P.op(G, lambda e, W=W: e.tensor_tensor(t1.ap[:, 0:W], t1.ap[:, 0:W], t2.ap[:, 0:W], ALU.subtract),
                         reads=[t1, t2], writes=[t1])
                    P.op(G, lambda e, W=W: e.tensor_tensor(t3.ap[:, 0:W], t3.ap[:, 0:W], t4.ap[:, 0:W], ALU.add),
                         reads=[t3, t4], writes=[t3])
                    for (src, dst, cj) in ((t1, t2, 0), (t3, t4, 1)):
                        init = 0.0 if tix == 0 else car[cj].ap[:, 0:1]
                        rd = [] if tix == 0 else [car[cj]]
                        P.op(V, lambda e, src=src, dst=dst, W=W, init=init, rho_ap=rho_ap: e.tensor_tensor_scan(
                            dst.ap[:, 0:W], rho_ap.broadcast_to([128, W]), src.ap[:, 0:W], init, ALU.mult, ALU.add),
                            reads=[src, pr_['rho']] + rd, writes=[dst])
                        if tix < len(tiles3) - 1:
                            P.op(A, lambda e, dst=dst, cj=cj, W=W: e.copy(car[cj].ap, dst.ap[:, W - 1:W]),
                                 reads=[dst], writes=[car[cj]])
                    P.op(G, lambda e, W=W, trs=trs: e.tensor_tensor(prs.ap[:, 0:W], t2.ap[:, 0:W], trs, ALU.mult),
                         reads=[t2, Tr], writes=[prs])
                    P.op(G, lambda e, W=W, tis=tis: e.tensor_tensor(pis.ap[:, 0:W], t4.ap[:, 0:W], tis, ALU.mult),
                         reads=[t4, Ti], writes=[pis])
                    P.op(G, lambda e, W=W, trs=trs: e.tensor_tensor(t1.ap[:, 0:W], t4.ap[:, 0:W], trs, ALU.mult),
                         reads=[t4, Tr], writes=[t1])
                    P.op(G, lambda e, W=W, tis=tis: e.tensor_tensor(t3.ap[:, 0:W], t2.ap[:, 0:W], tis, ALU.mult),
                         reads=[t2, Ti], writes=[t3])
                    P.op(V, lambda e, W=W, rev=rev: e.tensor_tensor(
                        rv(hr.ap[:, 0:W], rev), prs.ap[:, 0:W], pis.ap[:, 0:W], ALU.add),
                        reads=[prs, pis], writes=[hr])
                    P.op(V, lambda e, W=W, rev=rev: e.tensor_tensor(
                        rv(hi.ap[:, 0:W], rev), t1.ap[:, 0:W], t3.ap[:, 0:W], ALU.subtract),
                        reads=[t1, t3], writes=[hi])
                    for (x0, w) in subs:
                        a = n0 + x0
                        py = P.next_psum()
                        P.op('tensor', lambda e, py=py, ql=ql, x0=x0, w=w: e.matmul(
                            py.ap[:, 0:w], Cw[ql][0].ap, hr.ap[:, x0:x0 + w], start=True, stop=False),
                            reads=[Cw[ql][0], hr], writes=[py])
                        P.op('tensor', lambda e, py=py, ql=ql, x0=x0, w=w: e.matmul(
                            py.ap[:, 0:w], Cw[ql][1].ap, hi.ap[:, x0:x0 + w], start=False, stop=True),
                            reads=[Cw[ql][1], hi], writes=[py])
                        if d == 0 and ql == 0:
                            P.op(A, lambda e, py=py, a=a, w=w: e.copy(yT.ap[:, a:a + w], py.ap[:, 0:w]),
                                 reads=[py], writes=[yT])
                        else:
                            P.op(V, lambda e, py=py, a=a, w=w: e.tensor_tensor(
                                yT.ap[:, a:a + w], py.ap[:, 0:w], yT.ap[:, a:a + w], ALU.add),
                                reads=[py, yT], writes=[yT])
        for (tau0, W) in tiles3:
            sl = slice(tau0, tau0 + W)
            P.op(V, lambda e, W=W, sl=sl, u=u, fc=fc: e.scalar_tensor_tensor(
                t1.ap[:, 0:W], u.ap[:, sl], dT.ap[:, fc:fc + 1], yT.ap[:, sl], ALU.mult, ALU.add),
                reads=[u, yT, dT], writes=[t1])
            P.op(G, lambda e, W=W: e.tensor_tensor(t2.ap[:, 0:W], t1.ap[:, 0:W], t1.ap[:, 0:W], ALU.mult),
                 reads=[t1], writes=[t2])
            P.op(A, lambda e, W=W: e.activation(t2.ap[:, 0:W], t2.ap[:, 0:W], AF.Identity,
                                                 bias=k.onec.ap[:, 0:1], scale=0.044715),
                 reads=[t2, k.onec], writes=[t2])
            P.op(G, lambda e, W=W: e.tensor_tensor(t3.ap[:, 0:W], t2.ap[:, 0:W], t1.ap[:, 0:W], ALU.mult),
                 reads=[t1, t2], writes=[t3])
            P.op(A, lambda e, W=W: e.activation(t4.ap[:, 0:W], t3.ap[:, 0:W], AF.Sigmoid, scale=1.5957691216057308),
                 reads=[t3], writes=[t4])
            P.op(V, lambda e, W=W: e.tensor_tensor(hr.ap[:, 0:W], t1.ap[:, 0:W], t4.ap[:, 0:W], ALU.mult),
                 reads=[t1, t4], writes=[hr])
            P.dma('sync', k.yg_d[fc, :, sl], hr.ap[:, 0:W], reads=[hr],
                  writes=[k.ygb[g] for g in range(tau0 // TT, (tau0 + W) // TT)])
    P.release(mk)
    P.barrier()
    s5_glu(k, i)


def s5_glu(k, i):
    P, I = k.P, k.I
    js = i // 3
    mk = P.mark()
    T = TT
    Wg = P.alloc(8 * 2048, BF16, 'Wg')
    Wg_k = [sub(Wg, Wg.ap[:, kc * 2048:(kc + 1) * 2048]) for kc in range(8)]
    for kc in range(8):
        for h in range(2):
            P.dma('gpsimd', Wg_k[kc].ap[:, h * 1024:(h + 1) * 1024],
                  I['s5_w_glu'][js, kc * 128:(kc + 1) * 128, h * 1024:(h + 1) * 1024], writes=[Wg_k[kc]])
    bT = P.alloc(16, F32, 'bgluT')
    slow_dma(P, bT.ap, I['s5_b_glu'][js].rearrange("(c p) -> p c", p=128), writes=[bT])
    xt = [P.alloc(8 * T, F32, f'gxt{j}') for j in range(2)]
    xt_c = [[sub(b, b.ap[:, c * T:(c + 1) * T]) for c in range(8)] for b in xt]
    yg = [P.alloc(8 * T, BF16, f'gyg{j}') for j in range(2)]
    sg = [P.alloc(T, F32, f'gsg{j}') for j in range(2)]
    ob = [P.alloc(T, F32, f'gob{j}') for j in range(2)]
    for tix in range(NTT):
        s = 1 if tix == 0 else 0
        t0 = tix * T
        X, Xc, Y = xt[tix % 2], xt_c[tix % 2], yg[tix % 2]
        P.dma('sync', X.ap.rearrange("p (c t) -> p c t", c=8), k.xT_p[:, :, t0:t0 + T],
              reads=[k.xTb[tix]], writes=Xc)
        P.dma('sync', Y.ap.rearrange("p (c t) -> p c t", c=8), k.yg_p[:, :, t0:t0 + T],
              reads=[k.ygb[tix]], writes=[Y])
        for c in range(8):
            pv, pg = P.next_psum(), P.next_psum()
            for (pp, off) in ((pv, 0), (pg, 1024)):
                for kc in range(8):
                    P.op('tensor', lambda e, pp=pp, off=off, kc=kc, c=c, Y=Y: e.matmul(
                        pp.ap[:, :T], Wg_k[kc].ap[:, off + c * 128: off + (c + 1) * 128],
                        Y.ap[:, kc * T:(kc + 1) * T], start=(kc == 0), stop=(kc == 7)),
                        reads=[Wg_k[kc], Y], writes=[pp])
            sgb, obb = sg[c % 2], ob[c % 2]
            P.op('scalar', lambda e, pg=pg, sgb=sgb, c=c: e.activation(
                sgb.ap, pg.ap[:, :T], AF.Sigmoid, bias=bT.ap[:, 8 + c:9 + c], scale=1.0),
                reads=[pg, bT], writes=[sgb])
            P.op('vector', lambda e, pv=pv, sgb=sgb, obb=obb, c=c: e.scalar_tensor_tensor(
                obb.ap, pv.ap[:, :T], bT.ap[:, c:c + 1], sgb.ap, ALU.add, ALU.mult),
                reads=[pv, sgb, bT], writes=[obb])
            P.op('vector', lambda e, obb=obb, c=c, Xc=Xc, s=s: e.scalar_tensor_tensor(
                Xc[c].ap, obb.ap, HG_ap(k, i, 1, s, c), Xc[c].ap, ALU.mult, ALU.add),
                reads=[obb, Xc[c], k.HG], writes=[Xc[c]])
        P.dma('sync', k.xT_p[:, :, t0:t0 + T], X.ap.rearrange("p (c t) -> p c t", c=8),
              reads=Xc, writes=[k.xTb[tix]])
    P.release(mk)


NCH = LT // 64


def nat_range(d, tau0, w):
    if d == 0:
        return tau0, tau0 + w, False
    hi_ = (LC - 1 - tau0) if tau0 < LC else (LT + LC - 1 - tau0)
    return hi_ - w + 1, hi_ + 1, True


def tv(ap, d, tau0, w):
    n0, n1, rev = nat_range(d, tau0, w)
    v = ap[:, n0:n1]
    return v[:, ::-1] if rev else v


def gated_phase(k, i, kind):
    P, I = k.P, k.I
    hg = kind == 'hgrn'
    NH = 8 if hg else 4
    DV = 128 if hg else 256
    VT = DV // 128
    W_in = I['hgrn_w_in'][0] if hg else I['gla_w_in'][0]
    NCOL = 5120 if hg else 3104
    V, G, A = 'vector', 'gpsimd', 'scalar'
    mk = P.mark()
    Wb = P.alloc(8 * NCOL, BF16, 'Wb')
    Wb_k = [sub(Wb, Wb.ap[:, kc * NCOL:(kc + 1) * NCOL]) for kc in range(8)]
    for kc in range(8):
        for c0 in range(0, NCOL, 1024):
            cw = min(1024, NCOL - c0)
            P.dma('gpsimd', Wb_k[kc].ap[:, c0:c0 + cw], W_in[kc * 128:(kc + 1) * 128, c0:c0 + cw], writes=[Wb_k[kc]])
    if hg:
        lg = P.alloc(64, F32, 'lg')
        slow_dma(P, lg.ap, I['hgrn_lb_logits'].rearrange("l d (c p) -> p (l d c)", p=128), writes=[lg])
        E = P.alloc(64, F32, 'lgE')
        P.op(A, lambda e: e.activation(E.ap, lg.ap, AF.Exp), reads=[lg], writes=[E])
        den, num, oml = P.alloc(16, F32, 'den'), P.alloc(16, F32, 'num'), P.alloc(16, F32, 'oml')
        P.op(V, lambda e: e.tensor_tensor(den.ap, E.ap[:, 0:16], E.ap[:, 16:32], ALU.add), reads=[E], writes=[den])
        P.op(V, lambda e: e.tensor_tensor(den.ap, den.ap, E.ap[:, 32:48], ALU.add), reads=[E, den], writes=[den])
        P.op(V, lambda e: e.tensor_tensor(den.ap, den.ap, E.ap[:, 48:64], ALU.add), reads=[E, den], writes=[den])
        P.op(G, lambda e: e.memset(num.ap, 0.0), writes=[num])
        for l in range(1, i + 1):
            P.op(V, lambda e, l=l: e.tensor_tensor(num.ap, num.ap, E.ap[:, l * 16:(l + 1) * 16], ALU.add),
                 reads=[E, num], writes=[num])
        P.op(V, lambda e: e.reciprocal(den.ap, den.ap), reads=[den], writes=[den])
        P.op(V, lambda e: e.tensor_tensor(num.ap, num.ap, den.ap, ALU.mult), reads=[num, den], writes=[num])
        P.op(V, lambda e: e.tensor_scalar(oml.ap, num.ap, -1.0, 1.0, ALU.mult, ALU.add), reads=[num], writes=[oml])
    else:
        wg = P.alloc(1024, BF16, 'wg', parts=16)
        P.dma('gpsimd', wg.ap.rearrange("p (z n) -> p z n", z=2), I['gla_w_gate'][0].rearrange("z r n -> r z n"),
              writes=[wg])
        nbg = P.alloc(8, F32, 'nbg')
        slow_dma(P, nbg.ap, I['gla_b_gate'][0].rearrange("z (h p) -> p (z h)", p=128), writes=[nbg])
        P.op(V, lambda e: e.tensor_scalar(nbg.ap, nbg.ap, -1.0, None, ALU.mult), reads=[nbg], writes=[nbg])
        lowsb = [[P.alloc(512, BF16, f'low{z}{j}', parts=16) for j in range(2)] for z in range(2)]
    ht = [P.alloc(8 * 512, BF16, f'ght{j}') for j in range(2)]
    stb = [P.alloc(512, BF16, f'stb{j}') for j in range(4)]
    stf = [P.alloc(512, F32, f'stf{j}') for j in range(4)]
    tf = [P.alloc(512, F32, f'tf{j}') for j in range(4)]
    vst = [P.alloc(1024, BF16, f'vst{j}', parts=64) for j in range(2)]
    cnt = {'b': 0, 'f': 0, 't': 0, 'v': 0}

    def nxt(lst, key):
        b_ = lst[cnt[key] % len(lst)]
        cnt[key] += 1
        return b_

    for tix, (t0, w) in enumerate(TILES512):
        H = ht[tix % 2]
        hv = H.ap.rearrange("p (c t) -> p c t", c=8)
        P.dma('sync', hv[:, :, 0:w], k.hT_p[:, :, t0:t0 + w], reads=k.hTb, writes=[H])
        sl = slice(t0, t0 + w)

        def proj(col0, M=128, H=H, w=w):
            ps = P.next_psum()
            for kc in range(8):
                P.op('tensor', lambda e, ps=ps, kc=kc, col0=col0, M=M, H=H, w=w: e.matmul(
                    ps.ap[0:M, 0:w], Wb_k[kc].ap[:, col0:col0 + M], H.ap[:, kc * 512:kc * 512 + w],
                    start=(kc == 0), stop=(kc == 7)), reads=[Wb_k[kc], H], writes=[ps])
            return ps

        def act_store(ps, func, scale, dst_ap, dst_buf, w=w):
            sb = nxt(stb, 'b')
            P.op(A, lambda e, ps=ps, sb=sb, func=func, scale=scale, w=w: e.activation(
                sb.ap[:, 0:w], ps.ap[:, 0:w], func, scale=scale), reads=[ps], writes=[sb])
            P.dma('sync', dst_ap, sb.ap[:, 0:w], reads=[sb], writes=[dst_buf])

        if hg:
            for h in range(8):
                act_store(proj(h * 128), AF.Silu, 1.0, k.pq_d[h, :, sl], k.pqb[h])
                act_store(proj(4096 + h * 128), AF.Silu, 1.0, k.pgate_d[h, :, sl], k.pgateb[h])
                for d, zoff in ((0, 2048), (1, 3072)):
                    ps = proj(zoff + h * 128)
                    a1, a2, sb, sf = nxt(tf, 't'), nxt(tf, 't'), nxt(stb, 'b'), nxt(stf, 'f')
                    P.op(A, lambda e, ps=ps, a1=a1, w=w: e.activation(a1.ap[:, 0:w], ps.ap[:, 0:w], AF.Sigmoid, scale=-1.0),
                         reads=[ps], writes=[a1])
                    P.op(V, lambda e, a1=a1, a2=a2, w=w, d=d, h=h: e.tensor_scalar(
                        a2.ap[:, 0:w], a1.ap[:, 0:w], oml.ap[:, d * 8 + h:d * 8 + h + 1], None, ALU.mult),
                        reads=[a1, oml], writes=[a2])
                    P.op(G, lambda e, a2=a2, sb=sb, w=w: e.tensor_copy(sb.ap[:, 0:w], a2.ap[:, 0:w]),
                         reads=[a2], writes=[sb])
                    P.dma('sync', k.pk_d[d, h, :, sl], sb.ap[:, 0:w], reads=[sb], writes=[k.pkb[d][h]])
                    P.op(A, lambda e, a2=a2, sf=sf, w=w: e.activation(
                        sf.ap[:, 0:w], a2.ap[:, 0:w], AF.Ln, bias=k.onec.ap[:, 0:1], scale=-1.0),
                        reads=[a2, k.onec], writes=[sf])
                    P.dma('sync', k.pg_d[d, h, :, sl], sf.ap[:, 0:w], reads=[sf], writes=[k.pgb[d][h]])
        else:
            for h in range(4):
                act_store(proj(h * 128), AF.Identity, 128.0 ** -0.5, k.pq_d[h, :, sl], k.pqb[h])
                act_store(proj(512 + h * 128), AF.Identity, 1.0, k.pk_d[0, h, :, sl], k.pkb[0][h])
            for b_ in range(8):
                act_store(proj(2048 + b_ * 128), AF.Silu, 1.0, k.pgate_d[b_, :, sl], k.pgateb[b_])
            for z in range(2):
                ps = proj(3072 + z * 16, M=16)
                lw = lowsb[z][tix % 2]
                P.op(A, lambda e, ps=ps, lw=lw, w=w: e.copy(lw.ap[:, 0:w], ps.ap[0:16, 0:w]), reads=[ps], writes=[lw])
                for h in range(4):
                    pg = P.next_psum()
                    P.op('tensor', lambda e, pg=pg, lw=lw, z=z, h=h, w=w: e.matmul(
                        pg.ap[:, 0:w], wg.ap[:, z * 512 + h * 128: z * 512 + (h + 1) * 128], lw.ap[:, 0:w],
                        start=True, stop=True), reads=[wg, lw], writes=[pg])
                    a1, a2, sf = nxt(tf, 't'), nxt(tf, 't'), nxt(stf, 'f')
                    P.op(A, lambda e, pg=pg, a1=a1, z=z, h=h, w=w: e.activation(
                        a1.ap[:, 0:w], pg.ap[:, 0:w], AF.Exp, bias=nbg.ap[:, z * 4 + h:z * 4 + h + 1], scale=-1.0),
                        reads=[pg, nbg], writes=[a1])
                    P.op(A, lambda e, a1=a1, a2=a2, w=w: e.activation(
                        a2.ap[:, 0:w], a1.ap[:, 0:w], AF.Ln, bias=k.onec.ap[:, 0:1], scale=1.0),
                        reads=[a1, k.onec], writes=[a2])
                    P.op(V, lambda e, a2=a2, sf=sf, w=w: e.tensor_scalar(
                        sf.ap[:, 0:w], a2.ap[:, 0:w], -1.0 / 16.0, None, ALU.mult), reads=[a2], writes=[sf])
                    P.dma('sync', k.pg_d[z, h, :, sl], sf.ap[:, 0:w], reads=[sf], writes=[k.pgb[z][h]])
        for ci in range(w // 64):
            vs_ = nxt(vst, 'v')
            for half in range(2):
                ps = P.next_psum()
                for kc in range(8):
                    P.op('tensor', lambda e, ps=ps, kc=kc, ci=ci, half=half, H=H: e.matmul(
                        ps.ap[0:64, 0:512], H.ap[:, kc * 512 + ci * 64: kc * 512 + ci * 64 + 64],
                        Wb_k[kc].ap[:, 1024 + half * 512: 1024 + (half + 1) * 512],
                        start=(kc == 0), stop=(kc == 7)), reads=[Wb_k[kc], H], writes=[ps])
                if half == 0:
                    P.op(V, lambda e, ps=ps, vs_=vs_: e.tensor_copy(vs_.ap[:, 0:512], ps.ap[0:64, 0:512]),
                         reads=[ps], writes=[vs_])
                else:
                    P.op(G if False else A, lambda e, ps=ps, vs_=vs_: e.copy(vs_.ap[:, 512:1024], ps.ap[0:64, 0:512]),
                         reads=[ps], writes=[vs_])
            P.dma('sync', k.pv_d[t0 // 64 + ci], vs_.ap, reads=[vs_], writes=[k.pvb])
    P.release(mk)
    P.barrier()

    mk = P.mark()
    nw = P.alloc(VT, F32, 'nw')
    slow_dma(P, nw.ap, (I['hgrn_norm_w'] if hg else I['gla_norm_w'])[0].rearrange("(v p) -> p v", p=128), writes=[nw])
    rmask = P.alloc(L, F32, 'rmask')
    P.op(G, lambda e: e.memset(rmask.ap, 1.0), writes=[rmask])
    P.op(G, lambda e: e.memset(rmask.ap.rearrange("p (m j) -> p m j", j=64)[:, :, 0:1], 0.0), writes=[rmask])
    qn = P.alloc(LT, BF16, 'qn')
    kn = P.alloc(LT, BF16, 'kn')
    bufA = P.alloc(LT, F32, 'bufA')
    bufB = P.alloc(LT, F32, 'bufB')
    qt = P.alloc(LT, BF16, 'qt')
    ktn = P.alloc(LT, BF16, 'ktn')
    kdn = P.alloc(LT, BF16, 'kdn')
    vsb = P.alloc(NCH * DV, BF16, 'vsb', parts=64)
    oT = [P.alloc(LT, F32, f'oT{v}') for v in range(VT)]
    S32 = [P.alloc(DV, F32, f'S32_{j}') for j in range(2)]
    Sbf = [P.alloc(DV, BF16, f'Sbf{j}') for j in range(2)]
    scs = [P.alloc(64, BF16, f'scs{j}', parts=64) for j in range(2)]
    kdT = [P.alloc(128, BF16, f'kdT{j}', parts=64) for j in range(2)]
    gsb = [P.alloc(512, BF16, f'gsb{j}') for j in range(2)]
    sqb = [P.alloc(512, BF16, f'sqb{j}') for j in range(2)]
    rs = [P.alloc(512, F32, f'rs{j}') for j in range(2)]
    o1 = [P.alloc(512, F32, f'o1{j}') for j in range(2)]
    ogs = [P.alloc(512, BF16, f'ogs{j}') for j in range(2)]
    segs = [(0, LC), (LC, L)]

    for h in range(NH):
        P.dma('sync', qn.ap, k.pq_d[h], reads=[k.pqb[h]], writes=[qn])
        P.dma('sync', vsb.ap.rearrange("j (n c) -> j n c", c=DV),
              k.pv_d.rearrange("n j c -> j n c")[:, :, h * DV:(h + 1) * DV], reads=[k.pvb], writes=[vsb])
        for d in range(2):
            kd = d if hg else 0
            if hg or d == 0:
                P.dma('sync', kn.ap, k.pk_d[kd, h], reads=[k.pkb[kd][h]], writes=[kn])
            if d == 0:
                P.dma('sync', bufB.ap, k.pg_d[d, h], reads=[k.pgb[d][h]], writes=[bufB])
            else:
                P.dma('sync', bufA.ap, k.pg_d[d, h], reads=[k.pgb[d][h]], writes=[bufA])
                for (s0, sw) in segs:
                    P.op(V, lambda e, s0=s0, sw=sw: e.tensor_copy(bufB.ap[:, s0:s0 + sw], tv(bufA.ap, 1, s0, sw)),
                         reads=[bufA], writes=[bufB])
            for (s0, sw) in segs:
                P.op(V, lambda e, s0=s0, sw=sw: e.tensor_tensor_scan(
                    bufA.ap[:, s0:s0 + sw], rmask.ap[:, 0:sw], bufB.ap[:, s0:s0 + sw], 0.0, ALU.mult, ALU.add),
                    reads=[bufB, rmask], writes=[bufA])
            P.op(A, lambda e: e.activation(bufB.ap, bufA.ap, AF.Exp, scale=-1.0), reads=[bufA], writes=[bufB])
            P.op(A, lambda e: e.activation(bufA.ap, bufA.ap, AF.Exp), reads=[bufA], writes=[bufA])
            for (s0, sw) in segs:
                P.op(V, lambda e, s0=s0, sw=sw, d=d: e.tensor_tensor(
                    qt.ap[:, s0:s0 + sw], tv(qn.ap, d, s0, sw), bufA.ap[:, s0:s0 + sw], ALU.mult),
                    reads=[qn, bufA], writes=[qt])
                P.op(V, lambda e, s0=s0, sw=sw, d=d: e.tensor_tensor(
                    tv(ktn.ap, d, s0, sw), tv(kn.ap, d, s0, sw), bufB.ap[:, s0:s0 + sw], ALU.mult),
                    reads=[kn, bufB], writes=[ktn])
                nm = sw // 64
                P.op(V, lambda e, s0=s0, sw=sw, d=d, nm=nm: e.tensor_tensor(
                    tv(kdn.ap, d, s0, sw).rearrange("p (m j) -> p m j", j=64),
                    tv(ktn.ap, d, s0, sw).rearrange("p (m j) -> p m j", j=64),
                    bufA.ap[:, s0:s0 + sw].rearrange("p (m j) -> p m j", j=64)[:, :, 63:64].broadcast_to([128, nm, 64]),
                    ALU.mult), reads=[ktn, bufA], writes=[kdn])
            P.op(G, lambda e: e.memset(S32[0].ap, 0.0), writes=[S32[0]])
            P.op(G, lambda e: e.memset(Sbf[0].ap, 0.0), writes=[Sbf[0]])
            for m in range(NCH):
                tau0 = m * 64
                n0, n1, rev = nat_range(d, tau0, 64)
                nn = n0 // 64
                jb = m % 2
                Sp32, Sn32, Spb, Snb = S32[m % 2], S32[(m + 1) % 2], Sbf[m % 2], Sbf[(m + 1) % 2]
                psS = P.next_psum()
                P.op('tensor', lambda e, psS=psS, n0=n0, n1=n1, tau0=tau0: e.matmul(
                    psS.ap[0:64, 0:64], ktn.ap[:, n0:n1], qt.ap[:, tau0:tau0 + 64], start=True, stop=True),
                    reads=[ktn, qt], writes=[psS])
                sc = scs[jb]
                P.op(V, lambda e, psS=psS, sc=sc, d=d: e.tensor_tensor(
                    sc.ap, psS.ap[0:64, 0:64], k.mask.ap[:, d * 64:(d + 1) * 64], ALU.mult),
                    reads=[psS, k.mask], writes=[sc])
                psK = P.next_psum()
                P.op('tensor', lambda e, psK=psK, n0=n0, n1=n1: e.matmul(
                    psK.ap[0:64, 0:128], kdn.ap[:, n0:n1], k.ident_b.ap, start=True, stop=True),
                    reads=[kdn, k.ident_b], writes=[psK])
                kt_ = kdT[jb]
                P.op(A, lambda e, psK=psK, kt_=kt_: e.copy(kt_.ap, psK.ap[0:64, 0:128]), reads=[psK], writes=[kt_])
                for v in range(VT):
                    psO = P.next_psum()
                    P.op('tensor', lambda e, psO=psO, nn=nn, v=v, sc=sc: e.matmul(
                        psO.ap[:, 0:64], vsb.ap[:, nn * DV + v * 128: nn * DV + (v + 1) * 128], sc.ap,
                        start=True, stop=False), reads=[vsb, sc], writes=[psO])
                    P.op('tensor', lambda e, psO=psO, v=v, Spb=Spb, tau0=tau0: e.matmul(
                        psO.ap[:, 0:64], Spb.ap[:, v * 128:(v + 1) * 128], qt.ap[:, tau0:tau0 + 64],
                        start=False, stop=True), reads=[Spb, qt], writes=[psO])
                    ov = oT[v].ap[:, n0:n1]
                    ov = ov[:, ::-1] if rev else ov
                    if d == 0:
                        P.op(G if False else A, lambda e, psO=psO, ov=ov: e.copy(ov, psO.ap[:, 0:64]),
                             reads=[psO], writes=[oT[v]])
                    else:
                        P.op(V, lambda e, psO=psO, ov=ov: e.tensor_tensor(ov, psO.ap[:, 0:64], ov, ALU.add),
                             reads=[psO, oT[v]], writes=[oT[v]])
                psV = P.next_psum()
                P.op('tensor', lambda e, psV=psV, kt_=kt_, nn=nn: e.matmul(
                    psV.ap[:, 0:DV], kt_.ap, vsb.ap[:, nn * DV:(nn + 1) * DV], start=True, stop=True),
                    reads=[kt_, vsb], writes=[psV])
                P.op(V, lambda e, psV=psV, Sp32=Sp32, Sn32=Sn32, tau0=tau0: e.scalar_tensor_tensor(
                    Sn32.ap, Sp32.ap, bufA.ap[:, tau0 + 63:tau0 + 64], psV.ap[:, 0:DV], ALU.mult, ALU.add),
                    reads=[psV, Sp32, bufA], writes=[Sn32])
                P.op(G, lambda e, Sn32=Sn32, Snb=Snb: e.tensor_copy(Snb.ap, Sn32.ap), reads=[Sn32], writes=[Snb])
        for tix, (t0, w) in enumerate(TILES512):
            jb = tix % 2
            pss = P.next_psum()
            for v in range(VT):
                sq_ = sqb[(tix * VT + v) % 2]
                P.op(A, lambda e, sq_=sq_, v=v, t0=t0, w=w: e.activation(sq_.ap[:, 0:w], oT[v].ap[:, t0:t0 + w], AF.Square),
                     reads=[oT[v]], writes=[sq_])
                P.op('tensor', lambda e, pss=pss, sq_=sq_, v=v, w=w: e.matmul(
                    pss.ap[:, 0:w], k.ones_b.ap, sq_.ap[:, 0:w], start=(v == 0), stop=(v == VT - 1)),
                    reads=[sq_, k.ones_b], writes=[pss])
            r_ = rs[jb]
            P.op(A, lambda e, pss=pss, r_=r_, w=w: e.activation(r_.ap[:, 0:w], pss.ap[:, 0:w], AF.Sqrt, bias=EPS, scale=1.0 / DV),
                 reads=[pss], writes=[r_])
            P.op(V, lambda e, r_=r_, w=w: e.reciprocal(r_.ap[:, 0:w], r_.ap[:, 0:w]), reads=[r_], writes=[r_])
            for v in range(VT):
                blk = h * VT + v
                g_, o_, og_ = gsb[(tix * VT + v) % 2], o1[(tix * VT + v) % 2], ogs[(tix * VT + v) % 2]
                P.dma('sync', g_.ap[:, 0:w], k.pgate_d[blk, :, t0:t0 + w], reads=[k.pgateb[blk]], writes=[g_])
                P.op(V, lambda e, o_=o_, v=v, t0=t0, w=w, r_=r_: e.scalar_tensor_tensor(
                    o_.ap[:, 0:w], oT[v].ap[:, t0:t0 + w], nw.ap[:, v:v + 1], r_.ap[:, 0:w], ALU.mult, ALU.mult),
                    reads=[oT[v], nw, r_], writes=[o_])
                P.op(G, lambda e, o_=o_, g_=g_, og_=og_, w=w: e.tensor_tensor(
                    og_.ap[:, 0:w], o_.ap[:, 0:w], g_.ap[:, 0:w], ALU.mult), reads=[o_, g_], writes=[og_])
                P.dma('sync', k.yg_d[blk, :, t0:t0 + w], og_.ap[:, 0:w], reads=[og_], writes=[k.ogb[blk]])
    P.release(mk)
    P.barrier()

    mk = P.mark()
    Wo = P.alloc(8 * 1024, BF16, 'Wo')
    Wo_k = [sub(Wo, Wo.ap[:, kc * 1024:(kc + 1) * 1024]) for kc in range(8)]
    W_out = I['hgrn_w_out'][0] if hg else I['gla_w_out'][0]
    for kc in range(8):
        P.dma('gpsimd', Wo_k[kc].ap, W_out[kc * 128:(kc + 1) * 128, :], writes=[Wo_k[kc]])
    og = P.alloc(8 * LT, BF16, 'og_all')
    og_k = [sub(og, og.ap[:, kc * LT:(kc + 1) * LT]) for kc in range(8)]
    for kc in range(8):
        P.dma('sync', og_k[kc].ap, k.yg_d[kc], reads=[k.ogb[kc]], writes=[og_k[kc]])
    xc = [P.alloc(LT, F32, f'xc{j}') for j in range(2)]
    for c in range(8):
        X = xc[c % 2]
        P.dma('sync', X.ap, k.xT[c], reads=k.xTb, writes=[X])
        for tix, (t0, w) in enumerate(TILES512):
            s = 1 if tix == 0 else 0
            ps = P.next_psum()
            for kc in range(8):
                P.op('tensor', lambda e, ps=ps, kc=kc, c=c, t0=t0, w=w: e.matmul(
                    ps.ap[:, 0:w], Wo_k[kc].ap[:, c * 128:(c + 1) * 128], og_k[kc].ap[:, t0:t0 + w],
                    start=(kc == 0), stop=(kc == 7)), reads=[Wo_k[kc], og_k[kc]], writes=[ps])
            if hg or tix == 0:
                xv = X.ap[:, t0:t0 + w]
                pv_ = ps.ap[:, 0:w]
            else:
                cc0 = (t0 - LC) // 64
                xv = X.ap[:, LC:LT].rearrange("p (r cc) -> p cc r", cc=64)[:, cc0:cc0 + 8, :]
                pv_ = ps.ap[:, 0:w].rearrange("p (cc r) -> p cc r", r=64)
            P.op(V, lambda e, pv_=pv_, xv=xv, c=c, s=s: e.scalar_tensor_tensor(
                xv, pv_, HG_ap(k, i, 1, s, c), xv, ALU.mult, ALU.add), reads=[ps, X, k.HG], writes=[X])
        P.dma('sync', k.xT[c], X.ap, reads=[X], writes=k.xTb)
    P.release(mk)


def make_consts():
    ident = np.eye(128, dtype=np.float32)
    iota = np.broadcast_to(np.arange(LT, dtype=np.float32)[None, :], (128, LT)).copy()
    jj = np.arange(64)[:, None]
    ii = np.arange(64)[None, :]
    mask = np.stack([(jj <= ii), (jj >= 63 - ii)]).astype(np.float32)
    return {'k_ident': ident, 'k_iota': iota, 'k_mask': mask}


def make_in_maps(inputs, cores):
    consts = make_consts()
    shared = {}
    for n, s in INPUT_SHAPES.items():
        if n in ('x', 'c', 'ctx') or n.startswith('k_'):
            continue
        shared[n] = np.ascontiguousarray(np.asarray(inputs[n], dtype=np.float32).reshape(s))
    maps = []
    for b in cores:
        m = dict(shared)
        m.update(consts)
        m['x'] = np.ascontiguousarray(np.asarray(inputs['x'][b], dtype=np.float32))
        m['ctx'] = np.ascontiguousarray(np.asarray(inputs['ctx'][b], dtype=np.float32))
        m['c'] = np.ascontiguousarray(np.asarray(inputs['c'][b], dtype=np.float32).reshape(1, D))
        maps.append(m)
    return maps


_NC_CACHE = {}


def kernel(**inputs):
    if 'full' not in _NC_CACHE:
        _NC_CACHE['full'] = build_nc()
    nc = _NC_CACHE['full']
    maps = make_in_maps(inputs, list(range(8)))
    res = run_bass_kernel_spmd(nc, maps, core_ids=list(range(8)))
    return np.stack([np.asarray(r["out"], dtype=np.float32) for r in res.results], axis=0)
```

```python
import math
from contextlib import ExitStack

import numpy as np
import concourse.bass as bass
import concourse.mybir as mybir
from concourse.bass_utils import run_bass_kernel_spmd

F32 = mybir.dt.float32
BF16 = mybir.dt.bfloat16
ALU = mybir.AluOpType
AF = mybir.ActivationFunctionType

ENG = ['tensor', 'vector', 'scalar', 'gpsimd', 'sync']
NDMA = 24
SAME_ENG_WINDOW = 10 ** 9

D = 1024
L = 4096
LC = 256
LT = L + LC
DFF = 2816
NKC = 8
NFC = 22
TT = 256
NTT = LT // TT
EPS = 1e-6
PI = math.pi
TILES512 = [(0, 256)] + [(256 + j * 512, 512) for j in range(L // 512)]
TWO_PI = 2.0 * math.pi


class Buf:
    def __init__(self, ap, name=''):
        self.ap = ap
        self.name = name
        self.w = None
        self.r = {}


class Prog:
    def __init__(self, nc, stack, arena_cols_f32=47 * 1024):
        self.nc = nc
        self.stack = stack
        self.q = {e: [] for e in ENG}
        self.cnt = {e: 0 for e in ENG}
        self.epoch = 0
        self.sems = {}
        self.seen = {e: {} for e in ENG}
        self.dma_sems = [stack.enter_context(nc.semaphore(f"dma{j}")) for j in range(NDMA)]
        self.dma_cnt = [0] * NDMA
        self.dma_pool = {'sync': list(range(0, NDMA // 2)), 'gpsimd': list(range(NDMA // 2, NDMA))}
        self.dma_rr = {'sync': 0, 'gpsimd': 0}
        self._new_epoch_sems()
        self.arena = stack.enter_context(nc.sbuf_tensor("arena", [128, arena_cols_f32], F32))
        self.arena_cols = arena_cols_f32
        self.bump = 0
        self.psum = [Buf(stack.enter_context(nc.psum_tensor(f"ps{i}", [128, 512], F32))[:, :], f"ps{i}")
                     for i in range(8)]
        self.ps_rr = 0
        self.n_instr = 0

    def alloc(self, cols, dtype=F32, name='', parts=128):
        nbytes = cols * (4 if dtype == F32 else 2)
        n32 = (nbytes + 3) // 4
        n32 = (n32 + 7) // 8 * 8
        assert self.bump + n32 <= self.arena_cols, f"SBUF arena overflow at {name}: {self.bump}+{n32}"
        v = self.arena[:, self.bump:self.bump + n32]
        self.bump += n32
        if dtype != F32:
            v = v.bitcast(dtype)
        v = v[0:parts, 0:cols]
        return Buf(v, name)

    def mark(self):
        return self.bump

    def release(self, mark):
        self.bump = mark

    def next_psum(self):
        b = self.psum[self.ps_rr]
        self.ps_rr = (self.ps_rr + 1) % 8
        return b

    def _new_epoch_sems(self):
        if not hasattr(self, 'semset'):
            self.semset = [{e: self.stack.enter_context(self.nc.semaphore(f"s_{e}_{j}")) for e in ENG}
                           for j in range(3)]
        for e in ENG:
            self.sems[(e, self.epoch)] = self.semset[self.epoch % 3][e]
        if self.epoch >= 2:
            for e in ENG:
                self.q[e].append(('clear', self.semset[(self.epoch + 1) % 3][e]))

    def _need(self, eng, ev, waits, raw):
        if ev is None:
            return
        if ev[0] == 'e':
            _, f, ep, k = ev
            if ep != self.epoch:
                return
            if f == eng:
                if eng == 'tensor':
                    return
                if k <= self.cnt[eng] - SAME_ENG_WINDOW:
                    return
            key = ('e', f, ep)
        else:
            _, j, k = ev
            key = ('d', j)
        if self.seen[eng].get(key, 0) >= k:
            return
        waits[key] = max(waits.get(key, 0), k)

    def _emit_waits(self, eng, waits):
        for key, k in waits.items():
            self.seen[eng][key] = k
            sem = self.sems[(key[1], key[2])] if key[0] == 'e' else self.dma_sems[key[1]]
            self.q[eng].append(('wait', sem, k))

    def _deps(self, eng, reads, writes):
        waits = {}
        for b in reads:
            self._need(eng, b.w, waits, True)
        for b in writes:
            self._need(eng, b.w, waits, False)
            for ev in b.r.values():
                self._need(eng, ev, waits, False)
        self._emit_waits(eng, waits)

    def _commit(self, ev, rkey, reads, writes):
        for b in writes:
            b.w = ev
            b.r = {}
        for b in reads:
            if b in writes:
                continue
            b.r[rkey] = ev

    def op(self, eng, fn, reads=(), writes=()):
        self._deps(eng, reads, writes)
        self.cnt[eng] += 1
        self.q[eng].append(('op', fn, self.sems[(eng, self.epoch)]))
        ev = ('e', eng, self.epoch, self.cnt[eng])
        self._commit(ev, ('e', eng), reads, writes)
        self.n_instr += 1
        return ev

    def dma(self, eng, out_ap, in_ap, reads=(), writes=(), **kw):
        self._deps(eng, reads, writes)
        pool = self.dma_pool[eng]
        j = pool[self.dma_rr[eng]]
        self.dma_rr[eng] = (self.dma_rr[eng] + 1) % len(pool)
        w = {}
        if self.dma_cnt[j] > 0:
            self._need(eng, ('d', j, self.dma_cnt[j]), w, True)
            self._emit_waits(eng, w)
        self.dma_cnt[j] += 16
        self.q[eng].append(('dma', out_ap, in_ap, kw, self.dma_sems[j]))
        ev = ('d', j, self.dma_cnt[j])
        self._commit(ev, ('d', j), reads, writes)
        self.n_instr += 1
        return ev

    def barrier(self):
        for e in ENG:
            waits = {}
            for f in ENG:
                if f != e and self.cnt[f] > 0:
                    ev = ('e', f, self.epoch, self.cnt[f])
                    self._need(e, ev, waits, True)
            for j in range(NDMA):
                if self.dma_cnt[j] > 0:
                    self._need(e, ('d', j, self.dma_cnt[j]), waits, True)
            self._emit_waits(e, waits)
        self.epoch += 1
        self._new_epoch_sems()
        for e in ENG:
            self.cnt[e] = 0

    def final_wait(self, eng='sync'):
        waits = {}
        for f in ENG:
            if f != eng and self.cnt[f] > 0:
                self._need(eng, ('e', f, self.epoch, self.cnt[f]), waits, True)
        for j in range(NDMA):
            if self.dma_cnt[j] > 0:
                self._need(eng, ('d', j, self.dma_cnt[j]), waits, True)
        self._emit_waits(eng, waits)

    def emit(self):
        nc = self.nc
        with nc.Block() as block:
            def run(engname):
                def body(e):
                    for item in self.q[engname]:
                        if item[0] == 'wait':
                            e.wait_ge(item[1], item[2])
                        elif item[0] == 'clear':
                            e.sem_clear(item[1])
                        elif item[0] == 'op':
                            item[1](e).then_inc(item[2], 1)
                        else:
                            _, o, i, kw, sem = item
                            e.dma_start(out=o, in_=i, **kw).then_inc(sem, 16)
                return body
            block.tensor(run('tensor'))
            block.vector(run('vector'))
            block.scalar(run('scalar'))
            block.gpsimd(run('gpsimd'))
            block.sync(run('sync'))


def sub(buf, ap, name=''):
    return Buf(ap, name or buf.name)


INPUT_SHAPES = {
    'x': [L, D], 'c': [1, D], 'ctx': [LC, D], 'c_ctx': [1, D],
    'ada_w': [4, D, 9 * D], 'ada_b': [4, 9 * D], 'norm_w': [4, 3, D],
    'ffn_w_in': [4, 2, D, 2 * DFF], 'ffn_w_out': [4, 2, DFF, D],
    's5_a_re': [2, 2, 64, 64], 's5_a_im': [2, 2, 64, 64], 's5_log_step': [2, 2, 64],
    's5_b_re': [2, 2, 64, 64, 16], 's5_b_im': [2, 2, 64, 64, 16],
    's5_c_re': [2, 2, 64, 16, 64], 's5_c_im': [2, 2, 64, 16, 64],
    's5_d': [2, D], 's5_w_glu': [2, D, 2 * D], 's5_b_glu': [2, 2 * D],
    'gla_w_in': [1, D, 3104], 'gla_w_gate': [1, 2, 16, 512], 'gla_b_gate': [1, 2, 512],
    'gla_norm_w': [1, 256], 'gla_w_out': [1, D, D],
    'hgrn_w_in': [1, D, 5 * D], 'hgrn_lb_logits': [4, 2, D], 'hgrn_norm_w': [1, 128],
    'hgrn_w_out': [1, D, D], 'final_norm_w': [1, D],
    'k_ident': [128, 128], 'k_iota': [128, LT], 'k_mask': [2, 64, 64],
}


class K:
    pass


def mod_col(i, m, c, s):
    return ((i * 9 + m) * 8 + c) * 2 + s


def build_nc(n_layers=4, mixers=True, dump_xT=False, layer_list=None):
    nc = bass.Bass("TRN2", target_bir_lowering=False)
    I = {n: nc.dram_tensor(n, list(s), F32, kind="ExternalInput").ap() for n, s in INPUT_SHAPES.items()}
    out = nc.dram_tensor("out", [L, D], F32, kind="ExternalOutput").ap()
    xT = nc.dram_tensor("xT_scr", [8, 128, LT], F32, kind="Internal").ap()
    hT_d = nc.dram_tensor("hT_scr", [8, 128, LT], BF16, kind="Internal").ap()
    yg_d = nc.dram_tensor("yg_scr", [8, 128, LT], BF16, kind="Internal").ap()
    pq_d = nc.dram_tensor("pq_scr", [8, 128, LT], BF16, kind="Internal").ap()
    pk_d = nc.dram_tensor("pk_scr", [2, 8, 128, LT], BF16, kind="Internal").ap()
    pg_d = nc.dram_tensor("pg_scr", [2, 8, 128, LT], F32, kind="Internal").ap()
    pgate_d = nc.dram_tensor("pgate_scr", [8, 128, LT], BF16, kind="Internal").ap()
    pv_d = nc.dram_tensor("pv_scr", [LT // 64, 64, 1024], BF16, kind="Internal").ap()
    if dump_xT:
        xdump = nc.dram_tensor("xdump", [8, 128, LT], F32, kind="ExternalOutput").ap()
        dbg = nc.dram_tensor("dbg", [128, 1024], F32, kind="ExternalOutput").ap()
        dbg2 = nc.dram_tensor("dbg2", [32, 128, 512], F32, kind="ExternalOutput").ap()

    with ExitStack() as st:
        P = Prog(nc, st)
        k = K()
        k.P, k.I, k.xT, k.hT_d, k.yg_d = P, I, xT, hT_d, yg_d
        k.dbg2 = dbg2 if dump_xT else None
        k.pq_d, k.pk_d, k.pg_d, k.pgate_d, k.pv_d = pq_d, pk_d, pg_d, pgate_d, pv_d
        k.pqb = [Buf(None) for _ in range(8)]
        k.pkb = [[Buf(None) for _ in range(8)] for _ in range(2)]
        k.pgb = [[Buf(None) for _ in range(8)] for _ in range(2)]
        k.pgateb = [Buf(None) for _ in range(8)]
        k.pvb = Buf(None)
        k.ogb = [Buf(None) for _ in range(8)]
        k.mask = P.alloc(128, F32, 'mask', parts=64)
        P.dma('sync', k.mask.ap.rearrange("p (d i) -> p d i", d=2), I['k_mask'].rearrange("d j i -> j d i"),
              writes=[k.mask])
        k.onec = P.alloc(1, F32, 'onec')
        P.op('gpsimd', lambda e: e.memset(k.onec.ap, 1.0), writes=[k.onec])
        k.dbg_n = 0
        k.xT_p = xT.rearrange("c p t -> p c t")
        k.hT_p = hT_d.rearrange("c p t -> p c t")
        k.yg_p = yg_d.rearrange("c p t -> p c t")
        k.xTb = [Buf(None, f"xT{t}") for t in range(NTT)]
        k.hTb = [Buf(None, f"hT{t}") for t in range(NTT)]
        k.ygb = [Buf(None, f"yg{t}") for t in range(NTT)]

        k.ident_f = P.alloc(128, F32, 'ident_f')
        k.ident_b = P.alloc(128, BF16, 'ident_b')
        k.ones_b = P.alloc(128, BF16, 'ones_b')
        k.modT = P.alloc(4 * 9 * 8 * 2, F32, 'modT')
        k.WS = P.alloc(4 * 3 * 2 * 8, F32, 'WS')
        k.HG = P.alloc(4 * 3 * 2 * 8, F32, 'HG')
        k.normT = P.alloc(4 * 3 * 8, F32, 'normT')
        k.fnT = P.alloc(8, F32, 'fnT')
        P.dma('sync', k.ident_f.ap, I['k_ident'], writes=[k.ident_f])
        P.op('vector', lambda e: e.tensor_copy(k.ident_b.ap, k.ident_f.ap), reads=[k.ident_f], writes=[k.ident_b])
        P.op('gpsimd', lambda e: e.memset(k.ones_b.ap, 1.0), writes=[k.ones_b])

        prologue(k)
        P.barrier()
        for i in (layer_list if layer_list is not None else range(n_layers)):
            ffn_phase(k, i, 0)
            P.barrier()
            if mixers:
                kind = i % 3
                mixer_prep(k, i, permute=(kind == 1))
                P.barrier()
                if kind == 0:
                    s5_phase(k, i)
                elif kind == 1:
                    gated_phase(k, i, 'gla')
                else:
                    gated_phase(k, i, 'hgrn')
                P.barrier()
            ffn_phase(k, i, 1)
            P.barrier()
        epilogue(k, out)
        if dump_xT:
            P.barrier()
            mk = P.mark()
            t = P.alloc(8 * 512, F32, 'dump')
            for n in range(0, LT, 512):
                w = min(512, LT - n)
                P.dma('sync', t.ap.rearrange("p (c t) -> p c t", c=8)[:, :, 0:w], k.xT_p[:, :, n:n + w], writes=[t])
                P.dma('sync', xdump.rearrange("c p t -> p c t")[:, :, n:n + w],
                      t.ap.rearrange("p (c t) -> p c t", c=8)[:, :, 0:w], reads=[t])
            P.release(mk)
            P.dma('sync', dbg[:, 0:576], k.modT.ap, reads=[k.modT])
            P.dma('sync', dbg[:, 576:768], k.WS.ap, reads=[k.WS])
            P.dma('sync', dbg[:, 768:960], k.HG.ap, reads=[k.HG])
        P.final_wait('sync')
        P.emit()
        k.n_instr = P.n_instr
    return nc


def dbg_dump(k, buf, ap=None, label=''):
    if k.dbg2 is None or k.dbg_n >= 32:
        return
    ap = buf.ap if ap is None else ap
    pp, cc = ap.shape[0], ap.shape[1]
    print('DBG slot', k.dbg_n, label, pp, cc)
    k.P.dma('gpsimd', k.dbg2[k.dbg_n, 0:pp, 0:cc], ap, reads=[buf])
    k.dbg_n += 1


def slow_dma(P, out_ap, in_ap, **kw):
    return P.dma('sync', out_ap, in_ap, allow_slow_non_contiguous=True, **kw)


def prologue(k):
    P, I = k.P, k.I
    mk = P.mark()
    adabT = P.alloc(4 * 72, F32, 'adabT')
    for i in range(4):
        slow_dma(P, adabT.ap[:, i * 72:(i + 1) * 72], I['ada_b'][i].rearrange("(m p) -> p m", p=128), writes=[adabT])
    slow_dma(P, k.normT.ap, I['norm_w'].rearrange("i j (c p) -> p (i j c)", p=128), writes=[k.normT])
    slow_dma(P, k.fnT.ap, I['final_norm_w'].rearrange("o (c p) -> p (o c)", p=128), writes=[k.fnT])
    cs32 = P.alloc(16, F32, 'cs32')
    csv = cs32.ap.rearrange("p (k s) -> p k s", s=2)
    slow_dma(P, csv[:, :, 0], I['c'].rearrange("o (k p) -> p (o k)", p=128), writes=[cs32])
    slow_dma(P, csv[:, :, 1], I['c_ctx'].rearrange("o (k p) -> p (o k)", p=128), writes=[cs32])
    csb = P.alloc(16, BF16, 'csb')
    P.op('scalar', lambda e: e.activation(csb.ap, cs32.ap, AF.Silu), reads=[cs32], writes=[csb])

    Wa = [P.alloc(8 * 1024, BF16, f'Wa{j}') for j in range(2)]
    n = 0
    for i in range(4):
        for m in range(9):
            W = Wa[n % 2]
            n += 1
            P.dma('gpsimd', W.ap.rearrange("p (k n) -> p k n", k=8),
                  I['ada_w'][i].rearrange("(k p) n -> p k n", p=128)[:, :, m * 1024:(m + 1) * 1024], writes=[W])
            ps = P.next_psum()
            for oc in range(8):
                for kc in range(8):
                    P.op('tensor', lambda e, W=W, ps=ps, oc=oc, kc=kc: e.matmul(
                        ps.ap[:, oc * 2:oc * 2 + 2], W.ap[:, kc * 1024 + oc * 128: kc * 1024 + (oc + 1) * 128],
                        csb.ap[:, kc * 2:kc * 2 + 2], start=(kc == 0), stop=(kc == 7)),
                        reads=[W, csb], writes=[ps])
            base = mod_col(i, m, 0, 0)
            for s in range(2):
                P.op('vector', lambda e, ps=ps, s=s, base=base, i=i, m=m: e.tensor_tensor(
                    k.modT.ap[:, base + s: base + 16: 2], ps.ap[:, s:16:2],
                    adabT.ap[:, (i * 9 + m) * 8:(i * 9 + m) * 8 + 8], ALU.add),
                    reads=[ps, adabT], writes=[k.modT])
    for i in range(4):
        for j in range(3):
            for s in range(2):
                col = ((i * 3 + j) * 2 + s) * 8
                b_scale = mod_col(i, 3 * j + 1, 0, s)
                b_gate = mod_col(i, 3 * j + 2, 0, s)
                P.op('vector', lambda e, col=col, b=b_scale, i=i, j=j: e.scalar_tensor_tensor(
                    k.WS.ap[:, col:col + 8], k.modT.ap[:, b:b + 15:2], 1.0,
                    k.normT.ap[:, (i * 3 + j) * 8:(i * 3 + j) * 8 + 8], ALU.add, ALU.mult),
                    reads=[k.modT, k.normT], writes=[k.WS])
                P.op('vector', lambda e, col=col, b=b_gate, j=j: e.tensor_scalar(
                    k.HG.ap[:, col:col + 8], k.modT.ap[:, b:b + 15:2], (1.0 if j == 1 else 0.5), None, ALU.mult),
                    reads=[k.modT], writes=[k.HG])

    xin = [P.alloc(1024, F32, f'xin{j}') for j in range(2)]
    stg = [P.alloc(1024, F32, f'stg{j}') for j in range(2)]
    for blk in range(LT // 128):
        src = I['ctx'][blk * 128:(blk + 1) * 128, :] if blk < 2 else I['x'][(blk - 2) * 128:(blk - 1) * 128, :]
        xi = xin[blk % 2]
        sg = stg[blk % 2]
        P.dma('sync', xi.ap, src, writes=[xi])
        for h in range(2):
            ps = P.next_psum()
            for jj in range(4):
                c = h * 4 + jj
                P.op('tensor', lambda e, ps=ps, xi=xi, c=c, jj=jj: e.matmul(
                    ps.ap[:, jj * 128:(jj + 1) * 128], xi.ap[:, c * 128:(c + 1) * 128], k.ident_f.ap,
                    start=True, stop=True), reads=[xi, k.ident_f], writes=[ps])
            eng = 'vector' if h == 0 else 'scalar'
            if h == 0:
                P.op('vector', lambda e, ps=ps, sg=sg: e.tensor_copy(sg.ap[:, 0:512], ps.ap), reads=[ps], writes=[sg])
            else:
                P.op('scalar', lambda e, ps=ps, sg=sg: e.copy(sg.ap[:, 512:1024], ps.ap), reads=[ps], writes=[sg])
        P.dma('sync', k.xT_p[:, :, blk * 128:(blk + 1) * 128], sg.ap.rearrange("p (c t) -> p c t", c=8),
              reads=[sg], writes=[k.xTb[blk // 2]])
    P.release(mk)


def WS_ap(k, i, j, s, c):
    col = ((i * 3 + j) * 2 + s) * 8 + c
    return k.WS.ap[:, col:col + 1]


def HG_ap(k, i, j, s, c):
    col = ((i * 3 + j) * 2 + s) * 8 + c
    return k.HG.ap[:, col:col + 1]


def SH_ap(k, i, j, s, c):
    col = mod_col(i, 3 * j, c, s)
    return k.modT.ap[:, col:col + 1]


def alloc_norm_scratch(k, T):
    P = k.P
    k.nT = T
    sq = P.alloc(8 * T, BF16, 'sq')
    k.sq_c = [sub(sq, sq.ap[:, c * T:(c + 1) * T]) for c in range(8)]
    k.rstd = P.alloc(T, F32, 'rstd')
    k.ntmp = [P.alloc(T, F32, f'ntmp{j}') for j in range(2)]


def norm_tile(k, xt_c, hT_c, ws, sh, T, extra_reads=()):
    P = k.P
    sq_c, rstd, ntmp = k.sq_c, k.rstd, k.ntmp
    wsa = [ws(c) for c in range(8)]
    sha = [sh(c) for c in range(8)] if sh is not None else None
    for c in range(8):
        P.op('scalar', lambda e, c=c: e.activation(sq_c[c].ap[:, :T], xt_c[c].ap[:, :T], AF.Square),
             reads=[xt_c[c]], writes=[sq_c[c]])
    ps = P.next_psum()
    for c in range(8):
        P.op('tensor', lambda e, c=c, ps=ps: e.matmul(ps.ap[:, :T], k.ones_b.ap, sq_c[c].ap[:, :T],
                                                      start=(c == 0), stop=(c == 7)),
             reads=[sq_c[c], k.ones_b], writes=[ps])
    P.op('scalar', lambda e, ps=ps: e.activation(rstd.ap[:, :T], ps.ap[:, :T], AF.Sqrt, bias=EPS, scale=1.0 / D),
         reads=[ps], writes=[rstd])
    P.op('vector', lambda e: e.reciprocal(rstd.ap[:, :T], rstd.ap[:, :T]), reads=[rstd], writes=[rstd])
    for c in range(8):
        tmp = ntmp[c % 2]
        P.op('gpsimd', lambda e, c=c, tmp=tmp: e.tensor_tensor(
            tmp.ap[:, :T], xt_c[c].ap[:, :T], rstd.ap[:, :T], ALU.mult),
            reads=[xt_c[c], rstd], writes=[tmp])
        if sh is None:
            P.op('scalar', lambda e, c=c, tmp=tmp: e.activation(
                hT_c[c].ap[:, :T], tmp.ap[:, :T], AF.Identity, scale=wsa[c]),
                reads=[tmp, k.WS, k.fnT], writes=[hT_c[c]])
        else:
            P.op('scalar', lambda e, c=c, tmp=tmp: e.activation(
                hT_c[c].ap[:, :T], tmp.ap[:, :T], AF.Identity, bias=sha[c], scale=wsa[c]),
                reads=[tmp, k.modT, k.WS, k.fnT], writes=[hT_c[c]])


def ffn_phase(k, i, jf):
    P, I = k.P, k.I
    mk = P.mark()
    jn = 0 if jf == 0 else 2
    T = 512
    Win = P.alloc(8 * 5632, BF16, 'Win')
    Wout = P.alloc(22 * 1024, BF16, 'Wout')
    Win_k = [sub(Win, Win.ap[:, kc * 5632:(kc + 1) * 5632]) for kc in range(8)]
    Wout_f = [sub(Wout, Wout.ap[:, f * 1024:(f + 1) * 1024]) for f in range(22)]
    for kc in range(8):
        for q4 in range(4):
            P.dma('gpsimd', Win_k[kc].ap[:, q4 * 1408:(q4 + 1) * 1408],
                  I['ffn_w_in'][i, jf, kc * 128:(kc + 1) * 128, q4 * 1408:(q4 + 1) * 1408], writes=[Win_k[kc]])
    for f in range(22):
        P.dma('gpsimd', Wout_f[f].ap, I['ffn_w_out'][i, jf, f * 128:(f + 1) * 128, :], writes=[Wout_f[f]])
    actb = P.alloc(22 * T // 2, F32, 'actb')
    act_ap = actb.ap.bitcast(BF16)
    hT = P.alloc(8 * T, BF16, 'hT')
    hT_c = [sub(hT, hT.ap[:, c * T:(c + 1) * T]) for c in range(8)]
    rstd = P.alloc(T, F32, 'rstd')
    sg = [P.alloc(T, F32, f'sg{j}') for j in range(2)]
    xsm = [P.alloc(T, F32, f'xsm{j}') for j in range(2)]

    for tix, (t0, w) in enumerate(TILES512):
        s = 1 if tix == 0 else 0
        xv = actb.ap[:, 0:8 * T].rearrange("p (c t) -> p c t", c=8)
        P.dma('sync', xv[:, :, 0:w], k.xT_p[:, :, t0:t0 + w], writes=[actb])
        for c in range(8):
            P.op('scalar', lambda e, c=c, w=w: e.activation(hT_c[c].ap[:, :w], actb.ap[:, c * T:c * T + w], AF.Square),
                 reads=[actb], writes=[hT_c[c]])
        ps = P.next_psum()
        for c in range(8):
            P.op('tensor', lambda e, c=c, ps=ps, w=w: e.matmul(ps.ap[:, :w], k.ones_b.ap, hT_c[c].ap[:, :w],
                                                               start=(c == 0), stop=(c == 7)),
                 reads=[hT_c[c], k.ones_b], writes=[ps])
        P.op('scalar', lambda e, ps=ps, w=w: e.activation(rstd.ap[:, :w], ps.ap[:, :w], AF.Sqrt, bias=EPS, scale=1.0 / D),
             reads=[ps], writes=[rstd])
        P.op('vector', lambda e, w=w: e.reciprocal(rstd.ap[:, :w], rstd.ap[:, :w]), reads=[rstd], writes=[rstd])
        for c in range(8):
            tmp = sg[c % 2]
            ws_ap, sh_ap = WS_ap(k, i, jn, s, c), SH_ap(k, i, jn, s, c)
            P.op('gpsimd', lambda e, c=c, tmp=tmp, w=w: e.tensor_tensor(
                tmp.ap[:, :w], actb.ap[:, c * T:c * T + w], rstd.ap[:, :w], ALU.mult),
                reads=[actb, rstd], writes=[tmp])
            P.op('scalar', lambda e, c=c, tmp=tmp, w=w, ws_ap=ws_ap, sh_ap=sh_ap: e.activation(
                hT_c[c].ap[:, :w], tmp.ap[:, :w], AF.Identity, bias=sh_ap, scale=ws_ap),
                reads=[tmp, k.modT, k.WS], writes=[hT_c[c]])
        for f in range(22):
            pg, pu = P.next_psum(), P.next_psum()
            for (pp, off) in ((pg, 0), (pu, DFF)):
                for kc in range(8):
                    P.op('tensor', lambda e, pp=pp, off=off, kc=kc, f=f, w=w: e.matmul(
                        pp.ap[:, :w], Win_k[kc].ap[:, off + f * 128: off + (f + 1) * 128], hT_c[kc].ap[:, :w],
                        start=(kc == 0), stop=(kc == 7)), reads=[Win_k[kc], hT_c[kc]], writes=[pp])
            sgb = sg[f % 2]
            P.op('scalar', lambda e, pg=pg, sgb=sgb, w=w: e.activation(sgb.ap[:, :w], pg.ap[:, :w], AF.Silu),
                 reads=[pg], writes=[sgb])
            P.op('vector', lambda e, pu=pu, sgb=sgb, f=f, w=w: e.tensor_tensor(
                act_ap[:, f * T:f * T + w], sgb.ap[:, :w], pu.ap[:, :w], ALU.mult),
                reads=[pu, sgb], writes=[actb])
        for c in range(8):
            xs = xsm[c % 2]
            P.dma('sync', xs.ap[:, :w], k.xT[c, :, t0:t0 + w], writes=[xs])
            po = P.next_psum()
            for f in range(22):
                P.op('tensor', lambda e, po=po, f=f, c=c, w=w: e.matmul(
                    po.ap[:, :w], Wout_f[f].ap[:, c * 128:(c + 1) * 128], act_ap[:, f * T:f * T + w],
                    start=(f == 0), stop=(f == 21)), reads=[Wout_f[f], actb], writes=[po])
            hg_ap = HG_ap(k, i, jn, s, c)
            P.op('vector', lambda e, po=po, xs=xs, w=w, hg_ap=hg_ap: e.scalar_tensor_tensor(
                xs.ap[:, :w], po.ap[:, :w], hg_ap, xs.ap[:, :w], ALU.mult, ALU.add),
                reads=[po, xs, k.HG], writes=[xs])
            P.dma('sync', k.xT[c, :, t0:t0 + w], xs.ap[:, :w], reads=[xs])
    P.release(mk)


def epilogue(k, out):
    P = k.P
    mk = P.mark()
    T = TT
    alloc_norm_scratch(k, T)
    xt = [P.alloc(8 * T, F32, f'ext{j}') for j in range(2)]
    xt_c = [[sub(b, b.ap[:, c * T:(c + 1) * T]) for c in range(8)] for b in xt]
    yT = [P.alloc(8 * T, F32, f'eyT{j}') for j in range(2)]
    yT_c = [[sub(b, b.ap[:, c * T:(c + 1) * T]) for c in range(8)] for b in yT]
    stg = [P.alloc(1024, F32, f'estg{j}') for j in range(2)]
    n = 0
    for tix in range(1, NTT):
        t0 = tix * T
        X, Xc, Yc = xt[tix % 2], xt_c[tix % 2], yT_c[tix % 2]
        P.dma('sync', X.ap.rearrange("p (c t) -> p c t", c=8), k.xT_p[:, :, t0:t0 + T],
              reads=[k.xTb[tix]], writes=Xc)
        norm_tile(k, Xc, Yc, lambda c: k.fnT.ap[:, c:c + 1], None, T)
        for tb in range(T // 128):
            sgb = stg[n % 2]
            n += 1
            for h in range(2):
                ps = P.next_psum()
                for jj in range(4):
                    c = h * 4 + jj
                    P.op('tensor', lambda e, ps=ps, c=c, jj=jj, tb=tb, Yc=Yc: e.matmul(
                        ps.ap[:, jj * 128:(jj + 1) * 128], Yc[c].ap[:, tb * 128:(tb + 1) * 128], k.ident_f.ap,
                        start=True, stop=True), reads=[Yc[c], k.ident_f], writes=[ps])
                if h == 0:
                    P.op('vector', lambda e, ps=ps, sgb=sgb: e.tensor_copy(sgb.ap[:, 0:512], ps.ap),
                         reads=[ps], writes=[sgb])
                else:
                    P.op('scalar', lambda e, ps=ps, sgb=sgb: e.copy(sgb.ap[:, 512:1024], ps.ap),
                         reads=[ps], writes=[sgb])
            r0 = t0 - LC + tb * 128
            P.dma('sync', out[r0:r0 + 128, :], sgb.ap, reads=[sgb])
    P.release(mk)


def mixer_prep(k, i, permute):
    P = k.P
    mk = P.mark()
    T = TT
    alloc_norm_scratch(k, T)
    xt = [P.alloc(8 * T, F32, f'pxt{j}') for j in range(2)]
    xt_c = [[sub(b, b.ap[:, c * T:(c + 1) * T]) for c in range(8)] for b in xt]
    if permute:
        hall = P.alloc(8 * L, BF16, 'hall')
        hc0 = P.alloc(8 * T, BF16, 'hc0')
        hc0_c = [sub(hc0, hc0.ap[:, c * T:(c + 1) * T]) for c in range(8)]
        hall_c = [sub(hall, hall.ap[:, c * L:(c + 1) * L]) for c in range(8)]
        perm = [P.alloc(L, BF16, f'perm{j}') for j in range(2)]
    else:
        hT = [P.alloc(8 * T, BF16, f'phT{j}') for j in range(2)]
        hT_c = [[sub(b, b.ap[:, c * T:(c + 1) * T]) for c in range(8)] for b in hT]
    for tix in range(NTT):
        s = 1 if tix == 0 else 0
        t0 = tix * T
        X, Xc = xt[tix % 2], xt_c[tix % 2]
        P.dma('sync', X.ap.rearrange("p (c t) -> p c t", c=8), k.xT_p[:, :, t0:t0 + T],
              reads=[k.xTb[tix]], writes=Xc)
        if permute and tix > 0:
            Hc = [Buf(hall_c[c].ap[:, t0 - LC:t0 - LC + T]) for c in range(8)]
        elif permute:
            Hc = hc0_c
        else:
            Hc = hT_c[tix % 2]
        norm_tile(k, Xc, Hc, lambda c, s=s: WS_ap(k, i, 1, s, c), lambda c, s=s: SH_ap(k, i, 1, s, c), T)
        if permute and tix > 0:
            for c in range(8):
                hall_c[c].w = Hc[c].w
        elif permute:
            P.dma('sync', k.hT_p[:, :, 0:T], hc0.ap.rearrange("p (c t) -> p c t", c=8), reads=hc0_c,
                  writes=[k.hTb[0]])
        else:
            P.dma('sync', k.hT_p[:, :, t0:t0 + T], hT[tix % 2].ap.rearrange("p (c t) -> p c t", c=8),
                  reads=Hc, writes=[k.hTb[tix]])
    if permute:
        for c in range(8):
            pb = perm[c % 2]
            eng = 'vector' if c % 2 == 0 else 'gpsimd'
            P.op(eng, lambda e, c=c, pb=pb: e.tensor_copy(
                pb.ap.rearrange("p (cc r) -> p cc r", r=64),
                hall_c[c].ap.rearrange("p (r cc) -> p cc r", cc=64)), reads=[hall_c[c]], writes=[pb])
            P.dma('sync', k.hT_d[c, :, LC:LT], pb.ap, reads=[pb], writes=k.hTb[1:])
    P.release(mk)


def s5_phase(k, i):
    P, I = k.P, k.I
    js = i // 3
    mk = P.mark()
    V, G, A = 'vector', 'gpsimd', 'scalar'

    def tt(out, a, b, op, eng=V):
        P.op(eng, lambda e: e.tensor_tensor(out.ap, a.ap, b.ap, op), reads=[a, b], writes=[out])

    def ts(out, a, s1, op0, s2=None, op1=None, eng=V):
        if op1 is None:
            P.op(eng, lambda e: e.tensor_scalar(out.ap, a.ap, s1, None, op0), reads=[a], writes=[out])
        else:
            P.op(eng, lambda e: e.tensor_scalar(out.ap, a.ap, s1, s2, op0, op1), reads=[a], writes=[out])

    def act(out, a, func, scale=1.0):
        P.op(A, lambda e: e.activation(out.ap, a.ap, func, scale=scale), reads=[a], writes=[out])

    def sm(name=''):
        return P.alloc(32, F32, name)

    tmp = [sm(f't{j}') for j in range(4)]

    def csq(outr, outi, r, im):
        tt(tmp[0], r, r, ALU.mult)
        tt(tmp[1], im, im, ALU.mult)
        tt(outr, tmp[0], tmp[1], ALU.subtract)
        tt(tmp[2], r, im, ALU.mult)
        ts(outi, tmp[2], 2.0, ALU.mult)

    NLEV = 13
    par = []
    for d in range(2):
        are, aim, ls = sm(), sm(), sm()
        slow_dma(P, are.ap, I['s5_a_re'][js, d].rearrange("(q g2) p -> (g2 p) q", g2=2), writes=[are])
        slow_dma(P, aim.ap, I['s5_a_im'][js, d].rearrange("(q g2) p -> (g2 p) q", g2=2), writes=[aim])
        for g2 in range(2):
            slow_dma(P, ls.ap[g2 * 64:(g2 + 1) * 64, :],
                     I['s5_log_step'][js, d].rearrange("(q g2) -> g2 q", g2=2)[g2].partition_broadcast(64),
                     writes=[ls])
        dt, xr, th, rho = sm(), sm(), sm(), sm()
        act(dt, ls, AF.Exp)
        tt(xr, are, dt, ALU.mult)
        tt(th, aim, dt, ALU.mult)
        act(rho, xr, AF.Exp)
        zr, zi, th2 = sm(), sm(), sm()
        act(zi, th, AF.Sin, scale=1.0 / 16)
        ts(th2, th, 1.0 / 16, ALU.mult, PI / 2, ALU.add)
        act(zr, th2, AF.Sin)
        for _ in range(4):
            nr, ni = sm(), sm()
            csq(nr, ni, zr, zi)
            zr, zi = nr, ni
        cr, ci = zr, zi
        abr, abi, den, cfr, cfi = sm(), sm(), sm(), sm(), sm()
        tt(abr, rho, cr, ALU.mult)
        tt(abi, rho, ci, ALU.mult)
        tt(tmp[0], are, are, ALU.mult)
        tt(tmp[1], aim, aim, ALU.mult)
        tt(den, tmp[0], tmp[1], ALU.add)
        P.op(V, lambda e, den=den: e.reciprocal(den.ap, den.ap), reads=[den], writes=[den])
        zr_ = sm()
        ts(zr_, abr, -1.0, ALU.add)
        tt(tmp[0], zr_, are, ALU.mult)
        tt(tmp[1], abi, aim, ALU.mult)
        tt(tmp[2], tmp[0], tmp[1], ALU.add)
        tt(cfr, tmp[2], den, ALU.mult)
        tt(tmp[0], abi, are, ALU.mult)
        tt(tmp[1], zr_, aim, ALU.mult)
        tt(tmp[2], tmp[0], tmp[1], ALU.subtract)
        tt(cfi, tmp[2], den, ALU.mult)
        U = []
        ur, ui = cr, sm()
        ts(ui, ci, -1.0, ALU.mult)
        for m in range(NLEV):
            nui = sm()
            ts(nui, ui, -1.0, ALU.mult)
            U.append((ur, ui, nui))
            if m < NLEV - 1:
                nr, ni = sm(), sm()
                csq(nr, ni, ur, ui)
                ur, ui = nr, ni
        par.append(dict(rho=rho, cfr=cfr, cfi=cfi, U=U))

    Bn = [[P.alloc(32 * 32, F32, f'Bn{d}{r}') for r in range(2)] for d in range(2)]
    Cn = [[P.alloc(32 * 32, F32, f'Cn{d}{r}') for r in range(2)] for d in range(2)]
    mk2 = P.mark()
    Xz = P.alloc(8 * 128, F32, 'Xz')
    for d in range(2):
        for r, nm in enumerate(('s5_b_re', 's5_b_im')):
            T_ = Bn[d][r]
            P.op(G, lambda e, T_=T_: e.memset(T_.ap, 0.0), writes=[T_])
            v = T_.ap.rearrange("p (q c) -> p q c", c=32)
            for g2 in range(2):
                slow_dma(P, v[g2 * 64:(g2 + 1) * 64, :, g2 * 16:(g2 + 1) * 16],
                         I[nm][js, d].rearrange("(q g2) p c -> g2 p q c", g2=2)[g2], writes=[T_])
        for r, nm in enumerate(('s5_c_re', 's5_c_im')):
            T_ = Cn[d][r]
            for q8 in range(4):
                P.op(G, lambda e: e.memset(Xz.ap, 0.0), writes=[Xz])
                xv = Xz.ap.rearrange("p (q m) -> p q m", m=128)
                for g2 in range(2):
                    slow_dma(P, xv[g2 * 16:(g2 + 1) * 16, :, g2 * 64:(g2 + 1) * 64],
                             I[nm][js, d].rearrange("(q g2) c p -> g2 c q p", g2=2)[g2][:, q8 * 8:(q8 + 1) * 8, :],
                             writes=[Xz])
                ps = P.next_psum()
                for q in range(8):
                    P.op('tensor', lambda e, ps=ps, q=q: e.matmul(
                        ps.ap[:, q * 32:(q + 1) * 32], Xz.ap[0:32, q * 128:(q + 1) * 128], k.ident_f.ap[0:32, 0:32],
                        start=True, stop=True), reads=[Xz, k.ident_f], writes=[ps])
                P.op(V, lambda e, ps=ps, T_=T_, q8=q8: e.tensor_copy(
                    T_.ap[:, q8 * 256:(q8 + 1) * 256], ps.ap[:, 0:256]), reads=[ps], writes=[T_])
    P.release(mk2)
    P.barrier()

    Bw = [[P.alloc(128, BF16, f'Bw{q}{r}') for r in range(2)] for q in range(4)]
    Cw = [[P.alloc(128, BF16, f'Cw{q}{r}') for r in range(2)] for q in range(4)]
    for q in range(4):
        for r in range(2):
            P.op(G, lambda e, b=Bw[q][r]: e.memset(b.ap, 0.0), writes=[Bw[q][r]])
            P.op(G, lambda e, b=Cw[q][r]: e.memset(b.ap, 0.0), writes=[Cw[q][r]])
    BT = [[P.alloc(128, BF16, f'BT{j}{r}') for r in range(2)] for j in range(2)]
    ctmp = [P.alloc(32, F32, f'ctmp{j}') for j in range(3)]

    uT = [P.alloc(LT, BF16, f'uT{j}') for j in range(2)]
    Tr = P.alloc(LT, F32, 'Tr')
    Ti = P.alloc(LT, F32, 'Ti')
    yT = P.alloc(LT, F32, 'yT')
    WT = 2048
    prs, pis = P.alloc(WT, F32, 'prs'), P.alloc(WT, F32, 'pis')
    t1, t2, t3, t4 = (P.alloc(WT, F32, f't{j}w') for j in range(4))
    hr, hi = P.alloc(WT, BF16, 'hr'), P.alloc(WT, BF16, 'hi')
    car = [P.alloc(1, F32, f'car{j}') for j in range(2)]
    ttmp = [t1, t2]
    dT = P.alloc(8, F32, 'dT')
    slow_dma(P, dT.ap, I['s5_d'][js].rearrange("(c p) -> p c", p=128), writes=[dT])
    tiles3 = [(0, 256), (256, 2048), (2304, 2048)]

    def rv(ap, rev):
        return ap[:, ::-1] if rev else ap

    for fc in range(8):
        u = uT[fc % 2]
        P.dma('sync', u.ap, k.hT_d[fc], reads=k.hTb, writes=[u])
        for d in range(2):
            pr_ = par[d]
            for ql in range(4):
                pq = fc * 4 + ql
                blk = slice(pq * 32, pq * 32 + 32)
                for r in range(2):
                    P.op(G, lambda e, r=r, ql=ql, blk=blk, d=d: e.tensor_copy(
                        Bw[ql][r].ap[:, ql * 32:ql * 32 + 32], Bn[d][r].ap[:, blk]),
                        reads=[Bn[d][r]], writes=[Bw[ql][r]])
                bt = BT[pq % 2]
                for r in range(2):
                    ps = P.next_psum()
                    P.op('tensor', lambda e, ps=ps, r=r, ql=ql: e.matmul(
                        ps.ap[:, 0:128], Bw[ql][r].ap, k.ident_b.ap, start=True, stop=True),
                        reads=[Bw[ql][r], k.ident_b], writes=[ps])
                    P.op(A, lambda e, ps=ps, r=r, bt=bt: e.copy(bt[r].ap, ps.ap[:, 0:128]),
                         reads=[ps], writes=[bt[r]])
                cfr_ap = pr_['cfr'].ap[:, pq:pq + 1]
                cfi_ap = pr_['cfi'].ap[:, pq:pq + 1]
                cre, cim = Cn[d][0], Cn[d][1]
                P.op(V, lambda e, cim=cim, blk=blk, cfi_ap=cfi_ap: e.tensor_scalar(
                    ctmp[0].ap, cim.ap[:, blk], cfi_ap, None, ALU.mult), reads=[cim, pr_['cfi']], writes=[ctmp[0]])
                P.op(V, lambda e, cre=cre, blk=blk, cfr_ap=cfr_ap, ql=ql: e.scalar_tensor_tensor(
                    Cw[ql][0].ap[:, ql * 32:ql * 32 + 32], cre.ap[:, blk], cfr_ap, ctmp[0].ap, ALU.mult, ALU.subtract),
                    reads=[cre, ctmp[0], pr_['cfr']], writes=[Cw[ql][0]])
                P.op(V, lambda e, cim=cim, blk=blk, cfr_ap=cfr_ap: e.tensor_scalar(
                    ctmp[1].ap, cim.ap[:, blk], cfr_ap, None, ALU.mult), reads=[cim, pr_['cfr']], writes=[ctmp[1]])
                P.op(V, lambda e, cre=cre, blk=blk, cfi_ap=cfi_ap: e.scalar_tensor_tensor(
                    ctmp[2].ap, cre.ap[:, blk], cfi_ap, ctmp[1].ap, ALU.mult, ALU.add),
                    reads=[cre, ctmp[1], pr_['cfi']], writes=[ctmp[2]])
                P.op(V, lambda e, ql=ql: e.tensor_scalar(
                    Cw[ql][1].ap[:, ql * 32:ql * 32 + 32], ctmp[2].ap, -1.0, None, ALU.mult),
                    reads=[ctmp[2]], writes=[Cw[ql][1]])
                P.op(G, lambda e: e.memset(Tr.ap[:, 0:1], 1.0), writes=[Tr])
                P.op(G, lambda e: e.memset(Ti.ap[:, 0:1], 0.0), writes=[Ti])
                for m in range(NLEV):
                    n = 1 << m
                    cnt = min(n, LT - n)
                    ur, ui, nui = pr_['U'][m]
                    ur_ap, ui_ap, nui_ap = ur.ap[:, pq:pq + 1], ui.ap[:, pq:pq + 1], nui.ap[:, pq:pq + 1]
                    for c0 in range(0, cnt, 2048):
                        cw = min(2048, cnt - c0)
                        a0, a1 = c0, c0 + cw
                        b0, b1 = n + c0, n + c0 + cw
                        if cw >= 256:
                            P.op(A, lambda e, a0=a0, a1=a1, cw=cw, nui_ap=nui_ap: e.activation(
                                ttmp[0].ap[:, 0:cw], Ti.ap[:, a0:a1], AF.Copy, scale=nui_ap),
                                reads=[Ti, nui], writes=[ttmp[0]])
                        else:
                            P.op(V, lambda e, a0=a0, a1=a1, cw=cw, nui_ap=nui_ap: e.tensor_scalar(
                                ttmp[0].ap[:, 0:cw], Ti.ap[:, a0:a1], nui_ap, None, ALU.mult),
                                reads=[Ti, nui], writes=[ttmp[0]])
                        P.op(V, lambda e, a0=a0, a1=a1, b0=b0, b1=b1, cw=cw, ur_ap=ur_ap: e.scalar_tensor_tensor(
                            Tr.ap[:, b0:b1], Tr.ap[:, a0:a1], ur_ap, ttmp[0].ap[:, 0:cw], ALU.mult, ALU.add),
                            reads=[Tr, ttmp[0], ur], writes=[Tr])
                        if cw >= 256:
                            P.op(A, lambda e, a0=a0, a1=a1, cw=cw, ur_ap=ur_ap: e.activation(
                                ttmp[1].ap[:, 0:cw], Ti.ap[:, a0:a1], AF.Copy, scale=ur_ap),
                                reads=[Ti, ur], writes=[ttmp[1]])
                        else:
                            P.op(V, lambda e, a0=a0, a1=a1, cw=cw, ur_ap=ur_ap: e.tensor_scalar(
                                ttmp[1].ap[:, 0:cw], Ti.ap[:, a0:a1], ur_ap, None, ALU.mult),
                                reads=[Ti, ur], writes=[ttmp[1]])
                        P.op(V, lambda e, a0=a0, a1=a1, b0=b0, b1=b1, cw=cw, ui_ap=ui_ap: e.scalar_tensor_tensor(
                            Ti.ap[:, b0:b1], Tr.ap[:, a0:a1], ui_ap, ttmp[1].ap[:, 0:cw], ALU.mult, ALU.add),
                            reads=[Tr, ttmp[1], ui], writes=[Ti])
                rho_ap = pr_['rho'].ap[:, pq:pq + 1]
                for tix, (tau0, W) in enumerate(tiles3):
                    n0, n1, rev = nat_range(d, tau0, W)
                    subs = [(x0, min(512, W - x0)) for x0 in range(0, W, 512)]
                    for (x0, w) in subs:
                        pA, pB = P.next_psum(), P.next_psum()
                        for (pp, r, dst) in ((pA, 0, prs), (pB, 1, pis)):
                            P.op('tensor', lambda e, pp=pp, r=r, bt=bt, a=n0 + x0, w=w, u=u: e.matmul(
                                pp.ap[:, 0:w], bt[r].ap, u.ap[:, a:a + w], start=True, stop=True),
                                reads=[bt[r], u], writes=[pp])
                            P.op(A, lambda e, pp=pp, dst=dst, x0=x0, w=w: e.copy(dst.ap[:, x0:x0 + w], pp.ap[:, 0:w]),
                                 reads=[pp], writes=[dst])
                    trs, tis = Tr.ap[:, tau0:tau0 + W], Ti.ap[:, tau0:tau0 + W]
                    prv, piv = rv(prs.ap[:, 0:W], rev), rv(pis.ap[:, 0:W], rev)
                    P.op(V, lambda e, W=W, prv=prv, trs=trs: e.tensor_tensor(t1.ap[:, 0:W], prv, trs, ALU.mult),
                         reads=[prs, Tr], writes=[t1])
                    P.op(V, lambda e, W=W, piv=piv, tis=tis: e.tensor_tensor(t2.ap[:, 0:W], piv, tis, ALU.mult),
                         reads=[pis, Ti], writes=[t2])
                    P.op(V, lambda e, W=W, piv=piv, trs=trs: e.tensor_tensor(t3.ap[:, 0:W], piv, trs, ALU.mult),
                         reads=[pis, Tr], writes=[t3])
                    P.op(V, lambda e, W=W, prv=prv, tis=tis: e.tensor_tensor(t4.ap[:, 0:W], prv, tis, ALU.mult),
                         reads=[prs, Ti], writes=[t4])
                    P.op(G, lambda e, W=W: e.tensor_tensor(t1.ap[:, 0:W], t1.ap[:, 0:W], t2.ap[:, 0:W], ALU.subtract),
                         reads=[t1, t2], writes=[t1])
                    P.op(G, lambda e, W=W: e.tensor_tensor(t3.ap[:, 0:W], t3.ap[:, 0:W], t4.ap[:, 0:W], ALU.add),
                         reads=[t3, t4], writes=[t3])
                    for (src, dst, cj) in ((t1, t2, 0), (t3, t4, 1)):
                        init = 0.0 if tix == 0 else car[cj].ap[:, 0:1]
                        rd = [] if tix == 0 else [car[cj]]
                        P.op(V, lambda e, src=src, dst=dst, W=W, init=init, rho_ap=rho_ap: e.tensor_tensor_scan(
                            dst.ap[:, 0:W], rho_ap.broadcast_to([128, W]), src.ap[:, 0:W], init, ALU.mult, ALU.add),
                            reads=[src, pr_['rho']] + rd, writes=[dst])
                        if tix < len(tiles3) - 1:
                            P.op(A, lambda e, dst=dst, cj=cj, W=W: e.copy(car[cj].ap, dst.ap[:, W - 1:W]),
                                 reads=[dst], writes=[car[cj]])
                    P.op(G, lambda e, W=W, trs=trs: e.tensor_tensor(prs.ap[:, 0:W], t2.ap[:, 0:W], trs, ALU.mult),
                         reads=[t2, Tr], writes=[prs])
                    P.op(G, lambda e, W=W, tis=tis: e.tensor_tensor(pis.ap[:, 0:W], t4.ap[:, 0:W], tis, ALU.mult),
                         reads=[t4, Ti], writes=[pis])
                    P.op(V, lambda e, W=W, trs=trs: e.tensor_tensor(t1.ap[:, 0:W], t4.ap[:, 0:W], trs, ALU.mult),
                         reads=[t4, Tr], writes=[t1])
                    P.op(V, lambda e, W=W, tis=tis: e.tensor_tensor(t3.ap[:, 0:W], t2.ap[:, 0:W], tis, ALU.mult),
                         reads=[t2, Ti], writes=[t3])
                    P.op(V, lambda e, W=W, rev=rev: e.tensor_tensor(
                        rv(hr.ap[:, 0:W], rev), prs.ap[:, 0:W], pis.ap[:, 0:W], ALU.add),
                        reads=[prs, pis], writes=[hr])
                    P.op(V, lambda e, W=W, rev=rev: e.tensor_tensor(
                        rv(hi.ap[:, 0:W], rev), t1.ap[:, 0:W], t3.ap[:, 0:W], ALU.subtract),
                        reads=[t1, t3], writes=[hi])
                    for (x0, w) in subs:
                        a = n0 + x0
                        py = P.next_psum()
                        P.op('tensor', lambda e, py=py, ql=ql, x0=x0, w=w: e.matmul(
                            py.ap[:, 0:w], Cw[ql][0].ap, hr.ap[:, x0:x0 + w], start=True, stop=False),
                            reads=[Cw[ql][0], hr], writes=[py])
                        P.op('tensor', lambda e, py=py, ql=ql, x0=x0, w=w: e.matmul(
                            py.ap[:, 0:w], Cw[ql][1].ap, hi.ap[:, x0:x0 + w], start=False, stop=True),
                            reads=[Cw[ql][1], hi], writes=[py])
                        if d == 0 and ql == 0:
                            P.op(A, lambda e, py=py, a=a, w=w: e.copy(yT.ap[:, a:a + w], py.ap[:, 0:w]),
                                 reads=[py], writes=[yT])
                        else:
                            P.op(V, lambda e, py=py, a=a, w=w: e.tensor_tensor(
                                yT.ap[:, a:a + w], py.ap[:, 0:w], yT.ap[:, a:a + w], ALU.add),
                                reads=[py, yT], writes=[yT])
        for (tau0, W) in tiles3:
            sl = slice(tau0, tau0 + W)
            P.op(V, lambda e, W=W, sl=sl, u=u, fc=fc: e.scalar_tensor_tensor(
                t1.ap[:, 0:W], u.ap[:, sl], dT.ap[:, fc:fc + 1], yT.ap[:, sl], ALU.mult, ALU.add),
                reads=[u, yT, dT], writes=[t1])
            P.op(G, lambda e, W=W: e.tensor_tensor(t2.ap[:, 0:W], t1.ap[:, 0:W], t1.ap[:, 0:W], ALU.mult),
                 reads=[t1], writes=[t2])
            P.op(A, lambda e, W=W: e.activation(t2.ap[:, 0:W], t2.ap[:, 0:W], AF.Identity,
                                                 bias=k.onec.ap[:, 0:1], scale=0.044715),
                 reads=[t2, k.onec], writes=[t2])
            P.op(G, lambda e, W=W: e.tensor_tensor(t3.ap[:, 0:W], t2.ap[:, 0:W], t1.ap[:, 0:W], ALU.mult),
                 reads=[t1, t2], writes=[t3])
            P.op(A, lambda e, W=W: e.activation(t4.ap[:, 0:W], t3.ap[:, 0:W], AF.Sigmoid, scale=1.5957691216057308),
                 reads=[t3], writes=[t4])
            P.op(V, lambda e, W=W: e.tensor_tensor(hr.ap[:, 0:W], t1.ap[:, 0:W], t4.ap[:, 0:W], ALU.mult),
                 reads=[t1, t4], writes=[hr])
            P.dma('sync', k.yg_d[fc, :, sl], hr.ap[:, 0:W], reads=[hr],
                  writes=[k.ygb[g] for g in range(tau0 // TT, (tau0 + W) // TT)])
    P.release(mk)
    P.barrier()
    s5_glu(k, i)


def s5_glu(k, i):
    P, I = k.P, k.I
    js = i // 3
    mk = P.mark()
    T = TT
    Wg = P.alloc(8 * 2048, BF16, 'Wg')
    Wg_k = [sub(Wg, Wg.ap[:, kc * 2048:(kc + 1) * 2048]) for kc in range(8)]
    for kc in range(8):
        for h in range(2):
            P.dma('gpsimd', Wg_k[kc].ap[:, h * 1024:(h + 1) * 1024],
                  I['s5_w_glu'][js, kc * 128:(kc + 1) * 128, h * 1024:(h + 1) * 1024], writes=[Wg_k[kc]])
    bT = P.alloc(16, F32, 'bgluT')
    slow_dma(P, bT.ap, I['s5_b_glu'][js].rearrange("(c p) -> p c", p=128), writes=[bT])
    xt = [P.alloc(8 * T, F32, f'gxt{j}') for j in range(2)]
    xt_c = [[sub(b, b.ap[:, c * T:(c + 1) * T]) for c in range(8)] for b in xt]
    yg = [P.alloc(8 * T, BF16, f'gyg{j}') for j in range(2)]
    sg = [P.alloc(T, F32, f'gsg{j}') for j in range(2)]
    ob = [P.alloc(T, F32, f'gob{j}') for j in range(2)]
    for tix in range(NTT):
        s = 1 if tix == 0 else 0
        t0 = tix * T
        X, Xc, Y = xt[tix % 2], xt_c[tix % 2], yg[tix % 2]
        P.dma('sync', X.ap.rearrange("p (c t) -> p c t", c=8), k.xT_p[:, :, t0:t0 + T],
              reads=[k.xTb[tix]], writes=Xc)
        P.dma('sync', Y.ap.rearrange("p (c t) -> p c t", c=8), k.yg_p[:, :, t0:t0 + T],
              reads=[k.ygb[tix]], writes=[Y])
        for c in range(8):
            pv, pg = P.next_psum(), P.next_psum()
            for (pp, off) in ((pv, 0), (pg, 1024)):
                for kc in range(8):
                    P.op('tensor', lambda e, pp=pp, off=off, kc=kc, c=c, Y=Y: e.matmul(
                        pp.ap[:, :T], Wg_k[kc].ap[:, off + c * 128: off + (c + 1) * 128],
                        Y.ap[:, kc * T:(kc + 1) * T], start=(kc == 0), stop=(kc == 7)),
                        reads=[Wg_k[kc], Y], writes=[pp])
            sgb, obb = sg[c % 2], ob[c % 2]
            P.op('scalar', lambda e, pg=pg, sgb=sgb, c=c: e.activation(
                sgb.ap, pg.ap[:, :T], AF.Sigmoid, bias=bT.ap[:, 8 + c:9 + c], scale=1.0),
                reads=[pg, bT], writes=[sgb])
            P.op('vector', lambda e, pv=pv, sgb=sgb, obb=obb, c=c: e.scalar_tensor_tensor(
                obb.ap, pv.ap[:, :T], bT.ap[:, c:c + 1], sgb.ap, ALU.add, ALU.mult),
                reads=[pv, sgb, bT], writes=[obb])
            P.op('vector', lambda e, obb=obb, c=c, Xc=Xc, s=s: e.scalar_tensor_tensor(
                Xc[c].ap, obb.ap, HG_ap(k, i, 1, s, c), Xc[c].ap, ALU.mult, ALU.add),
                reads=[obb, Xc[c], k.HG], writes=[Xc[c]])
        P.dma('sync', k.xT_p[:, :, t0:t0 + T], X.ap.rearrange("p (c t) -> p c t", c=8),
              reads=Xc, writes=[k.xTb[tix]])
    P.release(mk)


NCH = LT // 64


def nat_range(d, tau0, w):
    if d == 0:
        return tau0, tau0 + w, False
    hi_ = (LC - 1 - tau0) if tau0 < LC else (LT + LC - 1 - tau0)
    return hi_ - w + 1, hi_ + 1, True


def tv(ap, d, tau0, w):
    n0, n1, rev = nat_range(d, tau0, w)
    v = ap[:, n0:n1]
    return v[:, ::-1] if rev else v


def gated_phase(k, i, kind):
    P, I = k.P, k.I
    hg = kind == 'hgrn'
    NH = 8 if hg else 4
    DV = 128 if hg else 256
    VT = DV // 128
    W_in = I['hgrn_w_in'][0] if hg else I['gla_w_in'][0]
    NCOL = 5120 if hg else 3104
    V, G, A = 'vector', 'gpsimd', 'scalar'
    mk = P.mark()
    Wb = P.alloc(8 * NCOL, BF16, 'Wb')
    Wb_k = [sub(Wb, Wb.ap[:, kc * NCOL:(kc + 1) * NCOL]) for kc in range(8)]
    for kc in range(8):
        for c0 in range(0, NCOL, 1024):
            cw = min(1024, NCOL - c0)
            P.dma('gpsimd', Wb_k[kc].ap[:, c0:c0 + cw], W_in[kc * 128:(kc + 1) * 128, c0:c0 + cw], writes=[Wb_k[kc]])
    if hg:
        lg = P.alloc(64, F32, 'lg')
        slow_dma(P, lg.ap, I['hgrn_lb_logits'].rearrange("l d (c p) -> p (l d c)", p=128), writes=[lg])
        E = P.alloc(64, F32, 'lgE')
        P.op(A, lambda e: e.activation(E.ap, lg.ap, AF.Exp), reads=[lg], writes=[E])
        den, num, oml = P.alloc(16, F32, 'den'), P.alloc(16, F32, 'num'), P.alloc(16, F32, 'oml')
        P.op(V, lambda e: e.tensor_tensor(den.ap, E.ap[:, 0:16], E.ap[:, 16:32], ALU.add), reads=[E], writes=[den])
        P.op(V, lambda e: e.tensor_tensor(den.ap, den.ap, E.ap[:, 32:48], ALU.add), reads=[E, den], writes=[den])
        P.op(V, lambda e: e.tensor_tensor(den.ap, den.ap, E.ap[:, 48:64], ALU.add), reads=[E, den], writes=[den])
        P.op(G, lambda e: e.memset(num.ap, 0.0), writes=[num])
        for l in range(1, i + 1):
            P.op(V, lambda e, l=l: e.tensor_tensor(num.ap, num.ap, E.ap[:, l * 16:(l + 1) * 16], ALU.add),
                 reads=[E, num], writes=[num])
        P.op(V, lambda e: e.reciprocal(den.ap, den.ap), reads=[den], writes=[den])
        P.op(V, lambda e: e.tensor_tensor(num.ap, num.ap, den.ap, ALU.mult), reads=[num, den], writes=[num])
        P.op(V, lambda e: e.tensor_scalar(oml.ap, num.ap, -1.0, 1.0, ALU.mult, ALU.add), reads=[num], writes=[oml])
    else:
        wg = P.alloc(1024, BF16, 'wg', parts=16)
        P.dma('gpsimd', wg.ap.rearrange("p (z n) -> p z n", z=2), I['gla_w_gate'][0].rearrange("z r n -> r z n"),
              writes=[wg])
        nbg = P.alloc(8, F32, 'nbg')
        slow_dma(P, nbg.ap, I['gla_b_gate'][0].rearrange("z (h p) -> p (z h)", p=128), writes=[nbg])
        P.op(V, lambda e: e.tensor_scalar(nbg.ap, nbg.ap, -1.0, None, ALU.mult), reads=[nbg], writes=[nbg])
        lowsb = [[P.alloc(512, BF16, f'low{z}{j}', parts=16) for j in range(2)] for z in range(2)]
    ht = [P.alloc(8 * 512, BF16, f'ght{j}') for j in range(2)]
    stb = [P.alloc(512, BF16, f'stb{j}') for j in range(4)]
    stf = [P.alloc(512, F32, f'stf{j}') for j in range(4)]
    tf = [P.alloc(512, F32, f'tf{j}') for j in range(4)]
    vst = [P.alloc(1024, BF16, f'vst{j}', parts=64) for j in range(2)]
    cnt = {'b': 0, 'f': 0, 't': 0, 'v': 0}

    def nxt(lst, key):
        b_ = lst[cnt[key] % len(lst)]
        cnt[key] += 1
        return b_

    for tix, (t0, w) in enumerate(TILES512):
        H = ht[tix % 2]
        hv = H.ap.rearrange("p (c t) -> p c t", c=8)
        P.dma('sync', hv[:, :, 0:w], k.hT_p[:, :, t0:t0 + w], reads=k.hTb, writes=[H])
        sl = slice(t0, t0 + w)

        def proj(col0, M=128, H=H, w=w):
            ps = P.next_psum()
            for kc in range(8):
                P.op('tensor', lambda e, ps=ps, kc=kc, col0=col0, M=M, H=H, w=w: e.matmul(
                    ps.ap[0:M, 0:w], Wb_k[kc].ap[:, col0:col0 + M], H.ap[:, kc * 512:kc * 512 + w],
                    start=(kc == 0), stop=(kc == 7)), reads=[Wb_k[kc], H], writes=[ps])
            return ps

        def act_store(ps, func, scale, dst_ap, dst_buf, w=w):
            sb = nxt(stb, 'b')
            P.op(A, lambda e, ps=ps, sb=sb, func=func, scale=scale, w=w: e.activation(
                sb.ap[:, 0:w], ps.ap[:, 0:w], func, scale=scale), reads=[ps], writes=[sb])
            P.dma('sync', dst_ap, sb.ap[:, 0:w], reads=[sb], writes=[dst_buf])

        if hg:
            for h in range(8):
                act_store(proj(h * 128), AF.Silu, 1.0, k.pq_d[h, :, sl], k.pqb[h])
                act_store(proj(4096 + h * 128), AF.Silu, 1.0, k.pgate_d[h, :, sl], k.pgateb[h])
                for d, zoff in ((0, 2048), (1, 3072)):
                    ps = proj(zoff + h * 128)
                    a1, a2, sb, sf = nxt(tf, 't'), nxt(tf, 't'), nxt(stb, 'b'), nxt(stf, 'f')
                    P.op(A, lambda e, ps=ps, a1=a1, w=w: e.activation(a1.ap[:, 0:w], ps.ap[:, 0:w], AF.Sigmoid, scale=-1.0),
                         reads=[ps], writes=[a1])
                    P.op(V, lambda e, a1=a1, a2=a2, w=w, d=d, h=h: e.tensor_scalar(
                        a2.ap[:, 0:w], a1.ap[:, 0:w], oml.ap[:, d * 8 + h:d * 8 + h + 1], None, ALU.mult),
                        reads=[a1, oml], writes=[a2])
                    P.op(G, lambda e, a2=a2, sb=sb, w=w: e.tensor_copy(sb.ap[:, 0:w], a2.ap[:, 0:w]),
                         reads=[a2], writes=[sb])
                    P.dma('sync', k.pk_d[d, h, :, sl], sb.ap[:, 0:w], reads=[sb], writes=[k.pkb[d][h]])
                    P.op(A, lambda e, a2=a2, sf=sf, w=w: e.activation(
                        sf.ap[:, 0:w], a2.ap[:, 0:w], AF.Ln, bias=k.onec.ap[:, 0:1], scale=-1.0),
                        reads=[a2, k.onec], writes=[sf])
                    P.dma('sync', k.pg_d[d, h, :, sl], sf.ap[:, 0:w], reads=[sf], writes=[k.pgb[d][h]])
        else:
            for h in range(4):
                act_store(proj(h * 128), AF.Identity, 128.0 ** -0.5, k.pq_d[h, :, sl], k.pqb[h])
                act_store(proj(512 + h * 128), AF.Identity, 1.0, k.pk_d[0, h, :, sl], k.pkb[0][h])
            for b_ in range(8):
                act_store(proj(2048 + b_ * 128), AF.Silu, 1.0, k.pgate_d[b_, :, sl], k.pgateb[b_])
            for z in range(2):
                ps = proj(3072 + z * 16, M=16)
                lw = lowsb[z][tix % 2]
                P.op(A, lambda e, ps=ps, lw=lw, w=w: e.copy(lw.ap[:, 0:w], ps.ap[0:16, 0:w]), reads=[ps], writes=[lw])
                for h in range(4):
                    pg = P.next_psum()
                    P.op('tensor', lambda e, pg=pg, lw=lw, z=z, h=h, w=w: e.matmul(
                        pg.ap[:, 0:w], wg.ap[:, z * 512 + h * 128: z * 512 + (h + 1) * 128], lw.ap[:, 0:w],
                        start=True, stop=True), reads=[wg, lw], writes=[pg])
                    a1, a2, sf = nxt(tf, 't'), nxt(tf, 't'), nxt(stf, 'f')
                    P.op(A, lambda e, pg=pg, a1=a1, z=z, h=h, w=w: e.activation(
                        a1.ap[:, 0:w], pg.ap[:, 0:w], AF.Exp, bias=nbg.ap[:, z * 4 + h:z * 4 + h + 1], scale=-1.0),
                        reads=[pg, nbg], writes=[a1])
                    P.op(A, lambda e, a1=a1, a2=a2, w=w: e.activation(
                        a2.ap[:, 0:w], a1.ap[:, 0:w], AF.Ln, bias=k.onec.ap[:, 0:1], scale=1.0),
                        reads=[a1, k.onec], writes=[a2])
                    P.op(V, lambda e, a2=a2, sf=sf, w=w: e.tensor_scalar(
                        sf.ap[:, 0:w], a2.ap[:, 0:w], -1.0 / 16.0, None, ALU.mult), reads=[a2], writes=[sf])
                    P.dma('sync', k.pg_d[z, h, :, sl], sf.ap[:, 0:w], reads=[sf], writes=[k.pgb[z][h]])
        for ci in range(w // 64):
            vs_ = nxt(vst, 'v')
            for half in range(2):
                ps = P.next_psum()
                for kc in range(8):
                    P.op('tensor', lambda e, ps=ps, kc=kc, ci=ci, half=half, H=H: e.matmul(
                        ps.ap[0:64, 0:512], H.ap[:, kc * 512 + ci * 64: kc * 512 + ci * 64 + 64],
                        Wb_k[kc].ap[:, 1024 + half * 512: 1024 + (half + 1) * 512],
                        start=(kc == 0), stop=(kc == 7)), reads=[Wb_k[kc], H], writes=[ps])
                if half == 0:
                    P.op(V, lambda e, ps=ps, vs_=vs_: e.tensor_copy(vs_.ap[:, 0:512], ps.ap[0:64, 0:512]),
                         reads=[ps], writes=[vs_])
                else:
                    P.op(G if False else A, lambda e, ps=ps, vs_=vs_: e.copy(vs_.ap[:, 512:1024], ps.ap[0:64, 0:512]),
                         reads=[ps], writes=[vs_])
            P.dma('sync', k.pv_d[t0 // 64 + ci], vs_.ap, reads=[vs_], writes=[k.pvb])
    P.release(mk)
    P.barrier()

    mk = P.mark()
    nw = P.alloc(VT, F32, 'nw')
    slow_dma(P, nw.ap, (I['hgrn_norm_w'] if hg else I['gla_norm_w'])[0].rearrange("(v p) -> p v", p=128), writes=[nw])
    rmask = P.alloc(L, F32, 'rmask')
    P.op(G, lambda e: e.memset(rmask.ap, 1.0), writes=[rmask])
    P.op(G, lambda e: e.memset(rmask.ap.rearrange("p (m j) -> p m j", j=64)[:, :, 0:1], 0.0), writes=[rmask])
    qn = P.alloc(LT, BF16, 'qn')
    kn = P.alloc(LT, BF16, 'kn')
    bufA = P.alloc(LT, F32, 'bufA')
    bufB = P.alloc(LT, F32, 'bufB')
    qt = P.alloc(LT, BF16, 'qt')
    ktn = P.alloc(LT, BF16, 'ktn')
    kdn = P.alloc(LT, BF16, 'kdn')
    vsb = P.alloc(NCH * DV, BF16, 'vsb', parts=64)
    oT = [P.alloc(LT, F32, f'oT{v}') for v in range(VT)]
    S32 = [P.alloc(DV, F32, f'S32_{j}') for j in range(2)]
    Sbf = [P.alloc(DV, BF16, f'Sbf{j}') for j in range(2)]
    scs = [P.alloc(64, BF16, f'scs{j}', parts=64) for j in range(2)]
    kdT = [P.alloc(128, BF16, f'kdT{j}', parts=64) for j in range(2)]
    gsb = [P.alloc(512, BF16, f'gsb{j}') for j in range(2)]
    sqb = [P.alloc(512, BF16, f'sqb{j}') for j in range(2)]
    rs = [P.alloc(512, F32, f'rs{j}') for j in range(2)]
    o1 = [P.alloc(512, F32, f'o1{j}') for j in range(2)]
    ogs = [P.alloc(512, BF16, f'ogs{j}') for j in range(2)]
    segs = [(0, LC), (LC, L)]

    for h in range(NH):
        P.dma('sync', qn.ap, k.pq_d[h], reads=[k.pqb[h]], writes=[qn])
        P.dma('sync', vsb.ap.rearrange("j (n c) -> j n c", c=DV),
              k.pv_d.rearrange("n j c -> j n c")[:, :, h * DV:(h + 1) * DV], reads=[k.pvb], writes=[vsb])
        for d in range(2):
            kd = d if hg else 0
            if hg or d == 0:
                P.dma('sync', kn.ap, k.pk_d[kd, h], reads=[k.pkb[kd][h]], writes=[kn])
            if d == 0:
                P.dma('sync', bufB.ap, k.pg_d[d, h], reads=[k.pgb[d][h]], writes=[bufB])
            else:
                P.dma('sync', bufA.ap, k.pg_d[d, h], reads=[k.pgb[d][h]], writes=[bufA])
                for (s0, sw) in segs:
                    P.op(V, lambda e, s0=s0, sw=sw: e.tensor_copy(bufB.ap[:, s0:s0 + sw], tv(bufA.ap, 1, s0, sw)),
                         reads=[bufA], writes=[bufB])
            for (s0, sw) in segs:
                P.op(V, lambda e, s0=s0, sw=sw: e.tensor_tensor_scan(
                    bufA.ap[:, s0:s0 + sw], rmask.ap[:, 0:sw], bufB.ap[:, s0:s0 + sw], 0.0, ALU.mult, ALU.add),
                    reads=[bufB, rmask], writes=[bufA])
            P.op(A, lambda e: e.activation(bufB.ap, bufA.ap, AF.Exp, scale=-1.0), reads=[bufA], writes=[bufB])
            P.op(A, lambda e: e.activation(bufA.ap, bufA.ap, AF.Exp), reads=[bufA], writes=[bufA])
            for (s0, sw) in segs:
                P.op(V, lambda e, s0=s0, sw=sw, d=d: e.tensor_tensor(
                    qt.ap[:, s0:s0 + sw], tv(qn.ap, d, s0, sw), bufA.ap[:, s0:s0 + sw], ALU.mult),
                    reads=[qn, bufA], writes=[qt])
                P.op(V, lambda e, s0=s0, sw=sw, d=d: e.tensor_tensor(
                    tv(ktn.ap, d, s0, sw), tv(kn.ap, d, s0, sw), bufB.ap[:, s0:s0 + sw], ALU.mult),
                    reads=[kn, bufB], writes=[ktn])
                nm = sw // 64
                P.op(V, lambda e, s0=s0, sw=sw, d=d, nm=nm: e.tensor_tensor(
                    tv(kdn.ap, d, s0, sw).rearrange("p (m j) -> p m j", j=64),
                    tv(ktn.ap, d, s0, sw).rearrange("p (m j) -> p m j", j=64),
                    bufA.ap[:, s0:s0 + sw].rearrange("p (m j) -> p m j", j=64)[:, :, 63:64].broadcast_to([128, nm, 64]),
                    ALU.mult), reads=[ktn, bufA], writes=[kdn])
            P.op(G, lambda e: e.memset(S32[0].ap, 0.0), writes=[S32[0]])
            P.op(G, lambda e: e.memset(Sbf[0].ap, 0.0), writes=[Sbf[0]])
            for m in range(NCH):
                tau0 = m * 64
                n0, n1, rev = nat_range(d, tau0, 64)
                nn = n0 // 64
                jb = m % 2
                Sp32, Sn32, Spb, Snb = S32[m % 2], S32[(m + 1) % 2], Sbf[m % 2], Sbf[(m + 1) % 2]
                psS = P.next_psum()
                P.op('tensor', lambda e, psS=psS, n0=n0, n1=n1, tau0=tau0: e.matmul(
                    psS.ap[0:64, 0:64], ktn.ap[:, n0:n1], qt.ap[:, tau0:tau0 + 64], start=True, stop=True),
                    reads=[ktn, qt], writes=[psS])
                sc = scs[jb]
                P.op(V, lambda e, psS=psS, sc=sc, d=d: e.tensor_tensor(
                    sc.ap, psS.ap[0:64, 0:64], k.mask.ap[:, d * 64:(d + 1) * 64], ALU.mult),
                    reads=[psS, k.mask], writes=[sc])
                psK = P.next_psum()
                P.op('tensor', lambda e, psK=psK, n0=n0, n1=n1: e.matmul(
                    psK.ap[0:64, 0:128], kdn.ap[:, n0:n1], k.ident_b.ap, start=True, stop=True),
                    reads=[kdn, k.ident_b], writes=[psK])
                kt_ = kdT[jb]
                P.op(A, lambda e, psK=psK, kt_=kt_: e.copy(kt_.ap, psK.ap[0:64, 0:128]), reads=[psK], writes=[kt_])
                for v in range(VT):
                    psO = P.next_psum()
                    P.op('tensor', lambda e, psO=psO, nn=nn, v=v, sc=sc: e.matmul(
                        psO.ap[:, 0:64], vsb.ap[:, nn * DV + v * 128: nn * DV + (v + 1) * 128], sc.ap,
                        start=True, stop=False), reads=[vsb, sc], writes=[psO])
                    P.op('tensor', lambda e, psO=psO, v=v, Spb=Spb, tau0=tau0: e.matmul(
                        psO.ap[:, 0:64], Spb.ap[:, v * 128:(v + 1) * 128], qt.ap[:, tau0:tau0 + 64],
                        start=False, stop=True), reads=[Spb, qt], writes=[psO])
                    ov = oT[v].ap[:, n0:n1]
                    ov = ov[:, ::-1] if rev else ov
                    if d == 0:
                        P.op(G if False else A, lambda e, psO=psO, ov=ov: e.copy(ov, psO.ap[:, 0:64]),
                             reads=[psO], writes=[oT[v]])
                    else:
                        P.op(V, lambda e, psO=psO, ov=ov: e.tensor_tensor(ov, psO.ap[:, 0:64], ov, ALU.add),
                             reads=[psO, oT[v]], writes=[oT[v]])
                psV = P.next_psum()
                P.op('tensor', lambda e, psV=psV, kt_=kt_, nn=nn: e.matmul(
                    psV.ap[:, 0:DV], kt_.ap, vsb.ap[:, nn * DV:(nn + 1) * DV], start=True, stop=True),
                    reads=[kt_, vsb], writes=[psV])
                P.op(V, lambda e, psV=psV, Sp32=Sp32, Sn32=Sn32, tau0=tau0: e.scalar_tensor_tensor(
                    Sn32.ap, Sp32.ap, bufA.ap[:, tau0 + 63:tau0 + 64], psV.ap[:, 0:DV], ALU.mult, ALU.add),
                    reads=[psV, Sp32, bufA], writes=[Sn32])
                P.op(G, lambda e, Sn32=Sn32, Snb=Snb: e.tensor_copy(Snb.ap, Sn32.ap), reads=[Sn32], writes=[Snb])
        for tix, (t0, w) in enumerate(TILES512):
            jb = tix % 2
            pss = P.next_psum()
            for v in range(VT):
                sq_ = sqb[(tix * VT + v) % 2]
                P.op(A, lambda e, sq_=sq_, v=v, t0=t0, w=w: e.activation(sq_.ap[:, 0:w], oT[v].ap[:, t0:t0 + w], AF.Square),
                     reads=[oT[v]], writes=[sq_])
                P.op('tensor', lambda e, pss=pss, sq_=sq_, v=v, w=w: e.matmul(
                    pss.ap[:, 0:w], k.ones_b.ap, sq_.ap[:, 0:w], start=(v == 0), stop=(v == VT - 1)),
                    reads=[sq_, k.ones_b], writes=[pss])
            r_ = rs[jb]
            P.op(A, lambda e, pss=pss, r_=r_, w=w: e.activation(r_.ap[:, 0:w], pss.ap[:, 0:w], AF.Sqrt, bias=EPS, scale=1.0 / DV),
                 reads=[pss], writes=[r_])
            P.op(V, lambda e, r_=r_, w=w: e.reciprocal(r_.ap[:, 0:w], r_.ap[:, 0:w]), reads=[r_], writes=[r_])
            for v in range(VT):
                blk = h * VT + v
                g_, o_, og_ = gsb[(tix * VT + v) % 2], o1[(tix * VT + v) % 2], ogs[(tix * VT + v) % 2]
                P.dma('sync', g_.ap[:, 0:w], k.pgate_d[blk, :, t0:t0 + w], reads=[k.pgateb[blk]], writes=[g_])
                P.op(V, lambda e, o_=o_, v=v, t0=t0, w=w, r_=r_: e.scalar_tensor_tensor(
                    o_.ap[:, 0:w], oT[v].ap[:, t0:t0 + w], nw.ap[:, v:v + 1], r_.ap[:, 0:w], ALU.mult, ALU.mult),
                    reads=[oT[v], nw, r_], writes=[o_])
                P.op(G, lambda e, o_=o_, g_=g_, og_=og_, w=w: e.tensor_tensor(
                    og_.ap[:, 0:w], o_.ap[:, 0:w], g_.ap[:, 0:w], ALU.mult), reads=[o_, g_], writes=[og_])
                P.dma('sync', k.yg_d[blk, :, t0:t0 + w], og_.ap[:, 0:w], reads=[og_], writes=[k.ogb[blk]])
    P.release(mk)
    P.barrier()

    mk = P.mark()
    Wo = P.alloc(8 * 1024, BF16, 'Wo')
    Wo_k = [sub(Wo, Wo.ap[:, kc * 1024:(kc + 1) * 1024]) for kc in range(8)]
    W_out = I['hgrn_w_out'][0] if hg else I['gla_w_out'][0]
    for kc in range(8):
        P.dma('gpsimd', Wo_k[kc].ap, W_out[kc * 128:(kc + 1) * 128, :], writes=[Wo_k[kc]])
    og = P.alloc(8 * LT, BF16, 'og_all')
    og_k = [sub(og, og.ap[:, kc * LT:(kc + 1) * LT]) for kc in range(8)]
    for kc in range(8):
        P.dma('sync', og_k[kc].ap, k.yg_d[kc], reads=[k.ogb[kc]], writes=[og_k[kc]])
    xc = [P.alloc(LT, F32, f'xc{j}') for j in range(2)]
    for c in range(8):
        X = xc[c % 2]
        P.dma('sync', X.ap, k.xT[c], reads=k.xTb, writes=[X])
        for tix, (t0, w) in enumerate(TILES512):
            s = 1 if tix == 0 else 0
            ps = P.next_psum()
            for kc in range(8):
                P.op('tensor', lambda e, ps=ps, kc=kc, c=c, t0=t0, w=w: e.matmul(
                    ps.ap[:, 0:w], Wo_k[kc].ap[:, c * 128:(c + 1) * 128], og_k[kc].ap[:, t0:t0 + w],
                    start=(kc == 0), stop=(kc == 7)), reads=[Wo_k[kc], og_k[kc]], writes=[ps])
            if hg or tix == 0:
                xv = X.ap[:, t0:t0 + w]
                pv_ = ps.ap[:, 0:w]
            else:
                cc0 = (t0 - LC) // 64
                xv = X.ap[:, LC:LT].rearrange("p (r cc) -> p cc r", cc=64)[:, cc0:cc0 + 8, :]
                pv_ = ps.ap[:, 0:w].rearrange("p (cc r) -> p cc r", r=64)
            P.op(V, lambda e, pv_=pv_, xv=xv, c=c, s=s: e.scalar_tensor_tensor(
                xv, pv_, HG_ap(k, i, 1, s, c), xv, ALU.mult, ALU.add), reads=[ps, X, k.HG], writes=[X])
        P.dma('sync', k.xT[c], X.ap, reads=[X], writes=k.xTb)
    P.release(mk)


def make_consts():
    ident = np.eye(128, dtype=np.float32)
    iota = np.broadcast_to(np.arange(LT, dtype=np.float32)[None, :], (128, LT)).copy()
    jj = np.arange(64)[:, None]
    ii = np.arange(64)[None, :]
    mask = np.stack([(jj <= ii), (jj >= 63 - ii)]).astype(np.float32)
    return {'k_ident': ident, 'k_iota': iota, 'k_mask': mask}


def make_in_maps(inputs, cores):
    consts = make_consts()
    shared = {}
    for n, s in INPUT_SHAPES.items():
        if n in ('x', 'c', 'ctx') or n.startswith('k_'):
            continue
        shared[n] = np.ascontiguousarray(np.asarray(inputs[n], dtype=np.float32).reshape(s))
    maps = []
    for b in cores:
        m = dict(shared)
        m.update(consts)
        m['x'] = np.ascontiguousarray(np.asarray(inputs['x'][b], dtype=np.float32))
        m['ctx'] = np.ascontiguousarray(np.asarray(inputs['ctx'][b], dtype=np.float32))
        m['c'] = np.ascontiguousarray(np.asarray(inputs['c'][b], dtype=np.float32).reshape(1, D))
        maps.append(m)
    return maps


_NC_CACHE = {}


def kernel(**inputs):
    if 'full' not in _NC_CACHE:
        _NC_CACHE['full'] = build_nc()
    nc = _NC_CACHE['full']
    maps = make_in_maps(inputs, list(range(8)))
    res = run_bass_kernel_spmd(nc, maps, core_ids=list(range(8)))
    return np.stack([np.asarray(r["out"], dtype=np.float32) for r in res.results], axis=0)
```
